# Optimizing a Trainium2 kernel written in Bass

```python
import math
import numpy as np
import jax
import jax.numpy as jnp
from jax import lax

D_MODEL = 4096
BATCH = 2
SEQ = 8192
DEPTH = 2

GRID_W = 64
CTX_LEN = 256
HEAD_DIM = 128
ROPE_THETA = 10000.0
NORM_EPS = 1e-6
MIX_HALF = D_MODEL // 2

GLA_DK = 128
GLA_DV = 256
GLA_HEADS = MIX_HALF // GLA_DV
GLA_QK = GLA_HEADS * GLA_DK
GLA_RANK = 16
GLA_NORMALIZER = 16.0
GLA_CHUNK = 64

SWA_HEADS = MIX_HALF // HEAD_DIM
SWA_KV_HEADS = SWA_HEADS // 4
SWA_WINDOW = 128
SWA_BLOCK = 128

NA_HEADS = MIX_HALF // HEAD_DIM
NA_ROWS = 8
NA_COLS = 16
NA_COL_BLOCK = 16
NA_KEY_COLS = NA_COL_BLOCK + NA_COLS

DIFF_HEADS = MIX_HALF // (2 * HEAD_DIM)
DIFF_DV = 2 * HEAD_DIM
DIFF_BLOCK = 128

FFN_HIDDEN = -(-8 * D_MODEL // (3 * 256)) * 256

EVEN_SPLITS = (GLA_QK, GLA_QK, GLA_HEADS * GLA_DV, GLA_HEADS * GLA_DV, GLA_RANK, GLA_RANK,
               SWA_HEADS * HEAD_DIM, SWA_KV_HEADS * HEAD_DIM, SWA_KV_HEADS * HEAD_DIM)
ODD_SPLITS = (NA_HEADS * HEAD_DIM, NA_HEADS * HEAD_DIM, NA_HEADS * HEAD_DIM,
              DIFF_HEADS * 2 * HEAD_DIM, DIFF_HEADS * 2 * HEAD_DIM, DIFF_HEADS * DIFF_DV)
EVEN_IN = sum(EVEN_SPLITS)
ODD_IN = sum(ODD_SPLITS)

kernel_name = 'hybrid_flow_backbone_gla_swa_natten_diff'


def rms_norm(t, g):
    tf = t.astype(jnp.float32)
    y = tf * lax.rsqrt(jnp.mean(tf * tf, axis=-1, keepdims=True) + NORM_EPS)
    return (y * g.astype(jnp.float32)).astype(t.dtype)


def modulate(t, shift, scale):
    return t * (1 + scale) + shift


def split_cols(t, widths):
    return jnp.split(t, np.cumsum(widths)[:-1].tolist(), axis=-1)


def heads(t, *shape):
    return t.reshape(t.shape[:2] + shape)


def swiglu(t, w_gate, w_up, w_down):
    return (jax.nn.silu(t @ w_gate) * (t @ w_up)) @ w_down


def axial_rope_tables(n_tok):
    quarter = HEAD_DIM // 4
    inv = 1.0 / (ROPE_THETA ** (jnp.arange(quarter, dtype=jnp.float32) / quarter))
    pos = jnp.arange(n_tok, dtype=jnp.int32)
    row = (pos // GRID_W).astype(jnp.float32)
    col = (pos % GRID_W).astype(jnp.float32)
    ang = jnp.concatenate([row[:, None] * inv, col[:, None] * inv], axis=-1)
    return jnp.cos(ang), jnp.sin(ang)


def apply_axial_rope(t, cos, sin):
    quarter = HEAD_DIM // 4
    bshape = (1, t.shape[1]) + (1,) * (t.ndim - 3) + (2, quarter)
    cs = cos.reshape(bshape)
    sn = sin.reshape(bshape)
    tr = t.astype(jnp.float32).reshape(t.shape[:-1] + (2, 2, quarter))
    t1 = tr[..., 0, :]
    t2 = tr[..., 1, :]
    out = jnp.stack([t1 * cs - t2 * sn, t2 * cs + t1 * sn], axis=-2)
    return out.reshape(t.shape).astype(t.dtype)


def gla_chunked(q, k, v, g, s0):
    nb, n_tok, nh, _ = q.shape
    dv = v.shape[-1]
    n_chunk = n_tok // GLA_CHUNK

    def to_chunks(t):
        t = t.astype(jnp.float32).reshape(nb, n_chunk, GLA_CHUNK, nh, t.shape[-1])
        return t.transpose(1, 0, 3, 2, 4)

    qc, kc, vc, gc = to_chunks(q), to_chunks(k), to_chunks(v), to_chunks(g)
    bc = jnp.cumsum(gc, axis=-2)
    incl = jnp.tril(jnp.ones((GLA_CHUNK, GLA_CHUNK), dtype=bool))

    def step(s, inp):
        qi, ki, vi, bi = inp
        q_dec = qi * jnp.exp(bi)
        k_inv = ki * jnp.exp(-bi)
        a = jnp.where(incl, jnp.einsum('bhik,bhjk->bhij', q_dec, k_inv), 0.0)
        o = jnp.einsum('bhij,bhjv->bhiv', a, vi) + jnp.einsum('bhik,bhkv->bhiv', q_dec, s)
        b_last = bi[:, :, -1:, :]
        k_upd = ki * jnp.exp(b_last - bi)
        s = s * jnp.exp(b_last[:, :, 0, :, None]) + jnp.einsum('bhjk,bhjv->bhkv', k_upd, vi)
        return s, o

    s_fin, o = lax.scan(step, s0, (qc, kc, vc, bc))
    return o.transpose(1, 0, 3, 2, 4).reshape(nb, n_tok, nh, dv), s_fin


def bidir_gla(q, k, v, g_f, g_b, s_f0, s_b0):
    o_f, s_f = gla_chunked(q, k, v, g_f, s_f0)
    rev = lambda t: jnp.flip(t, axis=1)
    o_b, s_b = gla_chunked(rev(q), rev(k), rev(v), rev(g_b), s_b0)
    return o_f + rev(o_b), s_f, s_b


def gla_inputs(parts, up_f, bias_f, up_b, bias_b):
    q, k, v, r, dn_f, dn_b = parts
    q = heads(q, GLA_HEADS, GLA_DK) * GLA_DK ** -0.5
    k = heads(k, GLA_HEADS, GLA_DK)
    v = heads(v, GLA_HEADS, GLA_DV)
    r = heads(r, GLA_HEADS, GLA_DV)

    def gate(dn, up, bias):
        z = (dn @ up + bias).astype(jnp.float32)
        return heads(jax.nn.log_sigmoid(z) / GLA_NORMALIZER, GLA_HEADS, GLA_DK)

    return q, k, v, r, gate(dn_f, up_f, bias_f), gate(dn_b, up_b, bias_b)


def gla_output(o, r, norm_g, dtype):
    y = rms_norm(o, norm_g) * jax.nn.silu(r.astype(jnp.float32))
    return y.reshape(o.shape[:2] + (-1,)).astype(dtype)


def swa_sink_attention(q, k, v, kc, vc, sink):
    nb, n_tok, hq, d = q.shape
    grp = hq // SWA_KV_HEADS
    lc = kc.shape[1]
    n_blk = n_tok // SWA_BLOCK
    span = SWA_BLOCK + 2 * SWA_WINDOW
    pad = ((0, 0), (SWA_WINDOW, SWA_WINDOW), (0, 0), (0, 0))
    kp = jnp.pad(k, pad)
    vp = jnp.pad(v, pad)
    qg = q.reshape(nb, n_tok, SWA_KV_HEADS, grp, d) * d ** -0.5
    qi = jnp.arange(SWA_BLOCK)[:, None]
    kj = jnp.arange(span)[None, :]
    rel_ok = jnp.abs(qi + SWA_WINDOW - kj) <= SWA_WINDOW
    sink_g = sink.astype(jnp.float32).reshape(SWA_KV_HEADS, grp, 1, 1)

    def block(bi):
        start = bi * SWA_BLOCK
        qb = lax.dynamic_slice_in_dim(qg, start, SWA_BLOCK, axis=1)
        kb = lax.dynamic_slice_in_dim(kp, start, span, axis=1)
        vb = lax.dynamic_slice_in_dim(vp, start, span, axis=1)
        key_pos = start - SWA_WINDOW + kj
        ok = rel_ok & (key_pos >= 0) & (key_pos < n_tok)
        s_win = jnp.where(ok, jnp.einsum('bqhgd,bkhd->bhgqk', qb, kb).astype(jnp.float32), -jnp.inf)
        s_ctx = jnp.einsum('bqhgd,bchd->bhgqc', qb, kc).astype(jnp.float32)
        s_sink = jnp.broadcast_to(sink_g, s_ctx.shape[:-1] + (1,))
        p = jax.nn.softmax(jnp.concatenate([s_sink, s_ctx, s_win], axis=-1), axis=-1)
        p_ctx = p[..., 1:1 + lc].astype(v.dtype)
        p_win = p[..., 1 + lc:].astype(v.dtype)
        o = jnp.einsum('bhgqc,bchd->bqhgd', p_ctx, vc) + jnp.einsum('bhgqk,bkhd->bqhgd', p_win, vb)
        return o.reshape(nb, SWA_BLOCK, hq * d)

    out = lax.map(block, jnp.arange(n_blk))
    return out.transpose(1, 0, 2, 3).reshape(nb, n_tok, hq * d)


def ctx_sink_attention(qc, kc, vc, sink):
    nb, lc, hq, d = qc.shape
    grp = hq // SWA_KV_HEADS
    qg = qc.reshape(nb, lc, SWA_KV_HEADS, grp, d) * d ** -0.5
    s = jnp.einsum('bqhgd,bchd->bhgqc', qg, kc).astype(jnp.float32)
    s_sink = jnp.broadcast_to(sink.astype(jnp.float32).reshape(SWA_KV_HEADS, grp, 1, 1), s.shape[:-1] + (1,))
    p = jax.nn.softmax(jnp.concatenate([s_sink, s], axis=-1), axis=-1)[..., 1:]
    o = jnp.einsum('bhgqc,bchd->bqhgd', p.astype(vc.dtype), vc)
    return o.reshape(nb, lc, hq * d)


def neighborhood_attention(q, k, v, kc, vc, rpb):
    nb, n_tok, nh, d = q.shape
    lc = kc.shape[1]
    rows = n_tok // GRID_W
    wr = min(NA_ROWS, rows)
    n_cb = GRID_W // NA_COL_BLOCK
    kc0 = [min(max(m * NA_COL_BLOCK - NA_COLS // 2, 0), GRID_W - NA_KEY_COLS) for m in range(n_cb)]
    n_keys = wr * NA_KEY_COLS
    qcol = np.arange(GRID_W, dtype=np.int32).reshape(n_cb, NA_COL_BLOCK)
    col_start = np.clip(qcol - NA_COLS // 2, 0, GRID_W - NA_COLS)
    kcol = np.array(kc0, dtype=np.int32)[:, None] + np.arange(NA_KEY_COLS, dtype=np.int32)[None, :]
    kcol_flat = np.tile(kcol, (1, wr))
    krow_flat = np.repeat(np.arange(wr, dtype=np.int32), NA_KEY_COLS)
    col_ok = (kcol_flat[:, None, :] >= col_start[:, :, None]) & (kcol_flat[:, None, :] < col_start[:, :, None] + NA_COLS)
    dc_idx = np.clip(kcol_flat[:, None, :] - qcol[:, :, None] + NA_COLS - 1, 0, 2 * NA_COLS - 2)
    qg = (q * d ** -0.5).reshape(nb, rows, GRID_W, nh, d)
    kg = k.reshape(nb, rows, GRID_W, nh, d)
    vg = v.reshape(nb, rows, GRID_W, nh, d)

    def row_block(r):
        rs = jnp.clip(r - wr // 2, 0, rows - wr)
        k_rows = lax.dynamic_slice_in_dim(kg, rs, wr, axis=1)
        v_rows = lax.dynamic_slice_in_dim(vg, rs, wr, axis=1)
        kb = jnp.stack([k_rows[:, :, s:s + NA_KEY_COLS] for s in kc0], axis=1).reshape(nb, n_cb, n_keys, nh, d)
        vb = jnp.stack([v_rows[:, :, s:s + NA_KEY_COLS] for s in kc0], axis=1).reshape(nb, n_cb, n_keys, nh, d)
        qb = lax.dynamic_index_in_dim(qg, r, axis=1, keepdims=False).reshape(nb, n_cb, NA_COL_BLOCK, nh, d)
        dr = rs + krow_flat - r + (NA_ROWS - 1)
        bias = rpb[:, dr[None, None, :], dc_idx].astype(jnp.float32)
        s_nb = jnp.einsum('bmqhd,bmkhd->bhmqk', qb, kb).astype(jnp.float32)
        s_nb = jnp.where(col_ok, s_nb + bias, -jnp.inf)
        s_ctx = jnp.einsum('bmqhd,bchd->bhmqc', qb, kc).astype(jnp.float32)
        p = jax.nn.softmax(jnp.concatenate([s_ctx, s_nb], axis=-1), axis=-1)
        p_ctx = p[..., :lc].astype(v.dtype)
        p_nb = p[..., lc:].astype(v.dtype)
        o = jnp.einsum('bhmqc,bchd->bmqhd', p_ctx, vc) + jnp.einsum('bhmqk,bmkhd->bmqhd', p_nb, vb)
        return o.reshape(nb, GRID_W, nh * d)

    out = lax.map(row_block, jnp.arange(rows))
    return out.transpose(1, 0, 2, 3).reshape(nb, n_tok, nh * d)


def dense_ctx_attention(qc, kc, vc):
    nb, lc, nh, d = qc.shape
    s = jnp.einsum('bqhd,bkhd->bhqk', qc * d ** -0.5, kc).astype(jnp.float32)
    p = jax.nn.softmax(s, axis=-1)
    return jnp.einsum('bhqk,bkhd->bqhd', p.astype(vc.dtype), vc).reshape(nb, lc, nh * d)


def diff_attention(q, k_all, v_all, lam):
    nb, n_q, nh, _, d = q.shape
    n_blk = n_q // DIFF_BLOCK
    qs = q * d ** -0.5

    def block(bi):
        qb = lax.dynamic_slice_in_dim(qs, bi * DIFF_BLOCK, DIFF_BLOCK, axis=1)
        s = jnp.einsum('bqhsd,bkhsd->bhsqk', qb, k_all).astype(jnp.float32)
        p = jax.nn.softmax(s, axis=-1)
        w = p[:, :, 0] - lam * p[:, :, 1]
        return jnp.einsum('bhqk,bkhv->bqhv', w.astype(v_all.dtype), v_all)

    out = lax.map(block, jnp.arange(n_blk))
    return out.transpose(1, 0, 2, 3, 4).reshape(nb, n_q, nh, v_all.shape[-1])


def even_mixer(h, hc, cos, sin, w_in, up_f, bias_f, up_b, bias_b, gla_norm_g, sink, with_ctx):
    lat = split_cols(h @ w_in, EVEN_SPLITS)
    cx = split_cols(hc @ w_in, EVEN_SPLITS)
    nb = h.shape[0]
    qa, ka, va, ra, gfa, gba = gla_inputs(lat[:6], up_f, bias_f, up_b, bias_b)
    qac, kac, vac, rac, gfac, gbac = gla_inputs(cx[:6], up_f, bias_f, up_b, bias_b)
    zeros = jnp.zeros((nb, GLA_HEADS, GLA_DK, GLA_DV), jnp.float32)
    oac, s_f, s_b = bidir_gla(qac, kac, vac, gfac, gbac, zeros, zeros)
    oa, _, _ = bidir_gla(qa, ka, va, gfa, gba, s_f, s_b)
    ya = gla_output(oa, ra, gla_norm_g, h.dtype)
    qb = apply_axial_rope(heads(lat[6], SWA_HEADS, HEAD_DIM), cos, sin)
    kb = apply_axial_rope(heads(lat[7], SWA_KV_HEADS, HEAD_DIM), cos, sin)
    vb = heads(lat[8], SWA_KV_HEADS, HEAD_DIM)
    qbc = heads(cx[6], SWA_HEADS, HEAD_DIM)
    kbc = heads(cx[7], SWA_KV_HEADS, HEAD_DIM)
    vbc = heads(cx[8], SWA_KV_HEADS, HEAD_DIM)
    yb = swa_sink_attention(qb, kb, vb, kbc, vbc, sink)
    y = jnp.concatenate([ya, yb.astype(h.dtype)], axis=-1)
    yc = None
    if with_ctx:
        yac = gla_output(oac, rac, gla_norm_g, hc.dtype)
        ybc = ctx_sink_attention(qbc, kbc, vbc, sink)
        yc = jnp.concatenate([yac, ybc.astype(hc.dtype)], axis=-1)
    return y, yc


def odd_mixer(h, hc, cos, sin, w_in, rpb, lq1, lk1, lq2, lk2, diff_norm_g, layer_idx, with_ctx):
    lat = split_cols(h @ w_in, ODD_SPLITS)
    cx = split_cols(hc @ w_in, ODD_SPLITS)
    nb, n_tok = h.shape[:2]
    lc = hc.shape[1]
    qn, kn, vn = (heads(t, NA_HEADS, HEAD_DIM) for t in lat[:3])
    qnc, knc, vnc = (heads(t, NA_HEADS, HEAD_DIM) for t in cx[:3])
    yn = neighborhood_attention(qn, kn, vn, knc, vnc, rpb)
    lam_init = 0.8 - 0.6 * math.exp(-0.3 * layer_idx)
    lam = (jnp.exp(jnp.sum(lq1 * lk1).astype(jnp.float32)) - jnp.exp(jnp.sum(lq2 * lk2).astype(jnp.float32)) + lam_init)
    qd = apply_axial_rope(heads(lat[3], DIFF_HEADS, 2, HEAD_DIM), cos, sin)
    kd = apply_axial_rope(heads(lat[4], DIFF_HEADS, 2, HEAD_DIM), cos, sin)
    vd = heads(lat[5], DIFF_HEADS, DIFF_DV)
    qdc = heads(cx[3], DIFF_HEADS, 2, HEAD_DIM)
    kdc = heads(cx[4], DIFF_HEADS, 2, HEAD_DIM)
    vdc = heads(cx[5], DIFF_HEADS, DIFF_DV)
    k_all = jnp.concatenate([kdc, kd], axis=1)
    v_all = jnp.concatenate([vdc, vd], axis=1)
    od = diff_attention(qd, k_all, v_all, lam)
    yd = (rms_norm(od, diff_norm_g) * (1 - lam_init)).reshape(nb, n_tok, -1)
    y = jnp.concatenate([yn.astype(h.dtype), yd.astype(h.dtype)], axis=-1)
    yc = None
    if with_ctx:
        ync = dense_ctx_attention(qnc, knc, vnc)
        odc = diff_attention(qdc, kdc, vdc, lam)
        ydc = (rms_norm(odc, diff_norm_g) * (1 - lam_init)).reshape(nb, lc, -1)
        yc = jnp.concatenate([ync.astype(hc.dtype), ydc.astype(hc.dtype)], axis=-1)
    return y, yc


def setup_inputs(seed: int = 0) -> dict:
    key = jax.random.key(seed)
    ks = iter(jax.random.split(key, 32))
    f32 = jnp.float32
    D = D_MODEL
    F = FFN_HIDDEN
    n_even = (DEPTH + 1) // 2
    n_odd = DEPTH // 2

    def nrm(shape, scale):
        return jax.random.normal(next(ks), shape, f32) * scale

    return {
        'x': nrm((BATCH, SEQ, D), 1.0),
        'c': nrm((BATCH, D), 1.0),
        'ctx': nrm((BATCH, CTX_LEN, D), 1.0),
        'c_ctx': nrm((D,), 1.0),
        'ada_w': nrm((DEPTH, D, 6 * D), 0.5 * D ** -0.5),
        'ada_b': nrm((DEPTH, 6 * D), 0.02),
        'norm_mix_g': 1.0 + nrm((DEPTH, D), 0.05),
        'norm_ffn_g': 1.0 + nrm((DEPTH, D), 0.05),
        'w_out': nrm((DEPTH, D, D), D ** -0.5),
        'ffn_w_gate': nrm((DEPTH, D, F), D ** -0.5),
        'ffn_w_up': nrm((DEPTH, D, F), D ** -0.5),
        'ffn_w_down': nrm((DEPTH, F, D), F ** -0.5),
        'ev_w_in': nrm((n_even, D, EVEN_IN), D ** -0.5),
        'gla_gate_up_f': nrm((n_even, GLA_RANK, GLA_QK), GLA_RANK ** -0.5),
        'gla_gate_bias_f': nrm((n_even, GLA_QK), 0.1),
        'gla_gate_up_b': nrm((n_even, GLA_RANK, GLA_QK), GLA_RANK ** -0.5),
        'gla_gate_bias_b': nrm((n_even, GLA_QK), 0.1),
        'gla_norm_g': 1.0 + nrm((n_even, GLA_DV), 0.05),
        'swa_sink': nrm((n_even, SWA_HEADS), 0.5),
        'od_w_in': nrm((n_odd, D, ODD_IN), D ** -0.5),
        'na_rpb': nrm((n_odd, NA_HEADS, 2 * NA_ROWS - 1, 2 * NA_COLS - 1), 0.1),
        'diff_lq1': nrm((n_odd, HEAD_DIM), 0.1),
        'diff_lk1': nrm((n_odd, HEAD_DIM), 0.1),
        'diff_lq2': nrm((n_odd, HEAD_DIM), 0.1),
        'diff_lk2': nrm((n_odd, HEAD_DIM), 0.1),
        'diff_norm_g': 1.0 + nrm((n_odd, DIFF_DV), 0.05),
        'final_norm_g': 1.0 + nrm((D,), 0.05),
    }


def reference(x, c, ctx, c_ctx, ada_w, ada_b, norm_mix_g, norm_ffn_g, w_out, ffn_w_gate, ffn_w_up, ffn_w_down,
              ev_w_in, gla_gate_up_f, gla_gate_bias_f, gla_gate_up_b, gla_gate_bias_b, gla_norm_g, swa_sink,
              od_w_in, na_rpb, diff_lq1, diff_lk1, diff_lq2, diff_lk2, diff_norm_g, final_norm_g):
    cos, sin = axial_rope_tables(x.shape[1])
    silu_c = jax.nn.silu(c)
    silu_cc = jax.nn.silu(c_ctx)[None, :]
    for i in range(DEPTH):
        last = i == DEPTH - 1
        mod = (silu_c @ ada_w[i] + ada_b[i])[:, None, :]
        mod_c = (silu_cc @ ada_w[i] + ada_b[i])[:, None, :]
        sh1, sc1, g1, sh2, sc2, g2 = jnp.split(mod, 6, axis=-1)
        csh1, csc1, cg1, csh2, csc2, cg2 = jnp.split(mod_c, 6, axis=-1)
        h = modulate(rms_norm(x, norm_mix_g[i]), sh1, sc1)
        hc = modulate(rms_norm(ctx, norm_mix_g[i]), csh1, csc1)
        j = i // 2
        if i % 2 == 0:
            y, yc = even_mixer(h, hc, cos, sin, ev_w_in[j], gla_gate_up_f[j], gla_gate_bias_f[j],
                               gla_gate_up_b[j], gla_gate_bias_b[j], gla_norm_g[j], swa_sink[j], not last)
        else:
            y, yc = odd_mixer(h, hc, cos, sin, od_w_in[j], na_rpb[j], diff_lq1[j], diff_lk1[j],
                              diff_lq2[j], diff_lk2[j], diff_norm_g[j], i, not last)
        x = x + g1 * (y @ w_out[i])
        x = x + g2 * swiglu(modulate(rms_norm(x, norm_ffn_g[i]), sh2, sc2), ffn_w_gate[i], ffn_w_up[i], ffn_w_down[i])
        if not last:
            ctx = ctx + cg1 * (yc @ w_out[i])
            ctx = ctx + cg2 * swiglu(modulate(rms_norm(ctx, norm_ffn_g[i]), csh2, csc2), ffn_w_gate[i], ffn_w_up[i], ffn_w_down[i])
    return rms_norm(x, final_norm_g)
```

```python
import math
import numpy as np
import concourse.bass as bass
import concourse.mybir as mybir
from concourse.bass_utils import run_bass_kernel_spmd

F32 = mybir.dt.float32
BF16 = mybir.dt.bfloat16
AF = mybir.ActivationFunctionType
ALU = mybir.AluOpType
AX = mybir.AxisListType

NEG = -30000.0


def make_cfg(D=4096, T=8192, C=256, depth=2):
    cfg = dict(D=D, T=T, C=C, depth=depth, GRID_W=64, HD=128)
    half = D // 2
    cfg['GH'] = half // 256
    cfg['GQK'] = cfg['GH'] * 128
    cfg['SH'] = half // 128
    cfg['SKV'] = cfg['SH'] // 4
    cfg['NH'] = half // 128
    cfg['DH'] = half // 256
    cfg['F'] = -(-8 * D // (3 * 256)) * 256
    cfg['TA'] = T + C
    return cfg


class Buf:
    __slots__ = ('w', 'r', 'name')

    def __init__(self, name=''):
        self.w = None
        self.r = []
        self.name = name


class Sched:
    ENG = ('pe', 'act', 'dve', 'pool', 'sp')
    NS = 8

    def __init__(self, nc):
        self.nc = nc
        self.e = {'pe': nc.tensor, 'act': nc.scalar, 'dve': nc.vector, 'pool': nc.gpsimd, 'sp': nc.sync}
        self.sem = {k: nc.alloc_semaphore('sem_' + k) for k in self.ENG}
        self.cnt = {k: 0 for k in self.ENG}
        self.dsem = {}
        self.dtot = {}
        self.dn = {}
        for q in ('sp', 'pool', 'act'):
            self.dn[q] = 0
            for s in range(self.NS):
                self.dsem[(q, s)] = nc.alloc_semaphore('dsem_%s%d' % (q, s))
                self.dtot[(q, s)] = 0
        self.seen = {k: {} for k in self.ENG}
        self.ninstr = 0

    def _semh(self, key):
        return self.sem[key] if isinstance(key, str) else self.dsem[key]

    def _wait(self, eng, toks):
        need = {}
        for t in toks:
            if t is None:
                continue
            k, v = t
            if k == 'pe' and eng == 'pe':
                continue
            if need.get(k, 0) < v:
                need[k] = v
        seen = self.seen[eng]
        for k, v in need.items():
            if seen.get(k, 0) < v:
                self.e[eng].wait_ge(self._semh(k), v)
                seen[k] = v
                self.ninstr += 1

    def _deps(self, r, w):
        toks = []
        for b in r:
            toks.append(b.w)
        for b in w:
            toks.append(b.w)
            toks.extend(b.r)
        return toks

    def _commit(self, tok, r, w):
        for b in w:
            b.w = tok
            b.r = []
        for b in r:
            b.r.append(tok)
            if len(b.r) > 64:
                best = {}
                for k, v in b.r:
                    if best.get(k, 0) < v:
                        best[k] = v
                b.r = list(best.items())

    def op(self, eng, fn, r=(), w=()):
        self._wait(eng, self._deps(r, w))
        ins = fn(self.e[eng])
        ins.then_inc(self.sem[eng], 1)
        self.cnt[eng] += 1
        self.ninstr += 1
        self._commit((eng, self.cnt[eng]), r, w)

    def dma(self, q, out, in_, r=(), w=(), **kw):
        slot = self.dn[q] % self.NS
        self.dn[q] += 1
        key = (q, slot)
        toks = self._deps(r, w)
        if self.dtot[key] > 0:
            toks.append((key, self.dtot[key]))
        self._wait(q, toks)
        self.e[q].dma_start(out=out, in_=in_, **kw).then_inc(self.dsem[key], 16)
        self.dtot[key] += 16
        self.ninstr += 1
        self._commit((key, self.dtot[key]), r, w)

    def finish(self, bufs):
        toks = []
        for b in bufs:
            toks.append(b.w)
        self._wait('sp', toks)


class Tl:
    __slots__ = ('t', 'b')

    def __init__(self, t, name=''):
        self.t = t
        self.b = Buf(name)

    def __getitem__(self, k):
        return self.t[k]


class Builder:
    def __init__(self, cfg, debug=()):
        self.cfg = cfg
        self.debug = set(debug)
        self.nc = bass.Bass("TRN2", target_bir_lowering=False)
        self.s = Sched(self.nc)
        self.dbufs = {}
        self.ins = {}
        self.scr = {}
        self._stack = []

    def inp(self, name, shape, dt=F32):
        self.ins[name] = self.nc.dram_tensor(name, list(shape), dt, kind="ExternalInput").ap()
        return self.ins[name]

    def scratch(self, name, shape, dt):
        kind = "ExternalOutput" if name in self.debug else "Internal"
        self.scr[name] = self.nc.dram_tensor(name, list(shape), dt, kind=kind).ap()
        return self.scr[name]

    def db(self, name, idx=0):
        k = (name, idx)
        if k not in self.dbufs:
            self.dbufs[k] = Buf(str(k))
        return self.dbufs[k]

    def dbs(self, name, lo, hi):
        return [self.db(name, i) for i in range(lo, hi)]


def even_layout(cfg):
    GQK, GH, SH, SKV = cfg['GQK'], cfg['GH'], cfg['SH'], cfg['SKV']
    o = 0
    L = {}
    for nm, w in (('gq', GQK), ('gk', GQK), ('gv', GH * 256), ('gr', GH * 256), ('dnf', 16), ('dnb', 16),
                  ('sq', SH * 128), ('sk', SKV * 128), ('sv', SKV * 128)):
        L[nm] = (o, w)
        o += w
    L['_n'] = o
    return L


def odd_layout(cfg):
    NH, DH = cfg['NH'], cfg['DH']
    o = 0
    L = {}
    for nm, w in (('nq', NH * 128), ('nk', NH * 128), ('nv', NH * 128), ('dq', DH * 256), ('dk', DH * 256),
                  ('dv', DH * 256)):
        L[nm] = (o, w)
        o += w
    L['_n'] = o
    return L


FM_EVEN = ('gq', 'gk', 'dnf', 'dnb', 'sq', 'sk')
TM_EVEN = ('gv', 'gr', 'sv')
ROPE_EVEN = ('sq', 'sk')
FM_ODD = ('nq', 'nk', 'dq', 'dk')
TM_ODD = ('nv', 'dv')
ROPE_ODD = ('dq', 'dk')


class Prog(Builder):
    def declare(self):
        c = self.cfg
        D, T, C, F, dep = c['D'], c['T'], c['C'], c['F'], c['depth']
        EL, OL = even_layout(c), odd_layout(c)
        self.EL, self.OL = EL, OL
        i = self.inp
        i('x', [T, D]); i('ctx', [C, D]); i('cvec', [2, D])
        i('ada_w', [dep, D, 6 * D]); i('ada_b', [dep, 6 * D])
        i('norm_mix_g', [dep, D]); i('norm_ffn_g', [dep, D])
        i('w_out', [dep, D, D]); i('ffn_w_gate', [dep, D, F]); i('ffn_w_up', [dep, D, F]); i('ffn_w_down', [dep, F, D])
        i('ev_w_in', [1, D, EL['_n']]); i('gla_gate_up_f', [1, 16, c['GQK']]); i('gla_gate_bias_f', [1, c['GQK']])
        i('gla_gate_up_b', [1, 16, c['GQK']]); i('gla_gate_bias_b', [1, c['GQK']])
        i('gla_norm_g', [1, 256]); i('swa_sink', [1, c['SH']])
        i('od_w_in', [1, D, OL['_n']]); i('na_bias', [5, c['NH'], 128, 640])
        i('diff_lq1', [1, 128]); i('diff_lk1', [1, 128]); i('diff_lq2', [1, 128]); i('diff_lk2', [1, 128])
        i('diff_norm_g', [1, 256]); i('final_norm_g', [D])
        i('k_ident', [128, 128]); i('k_perm', [128, 128]); i('k_cos', [128, T]); i('k_sin', [128, T])
        i('k_swamask', [128, 384]); i('k_tri', [2, 64, 64])
        self.out = self.nc.dram_tensor('out', [T, D], F32, kind="ExternalOutput").ap()
        s = self.scratch
        TA = c['TA']
        s('xres', [TA, D], F32)
        s('hT', [-(-TA // 512), 128, D // 128, 512], BF16)
        s('y_tm', [TA, D], BF16)
        s('mod', [dep, 2, 6 * D], F32)
        s('modv', [dep, 6, 2, D], F32)
        self.pieces = {}
        for l_, (L_, fmn_, tmn_) in enumerate(((EL, FM_EVEN, TM_EVEN), (OL, FM_ODD, TM_ODD))):
            pcs = []
            for kind_, names_ in (('fm', fmn_), ('tm', tmn_)):
                for nm in names_:
                    for p0 in range(0, L_[nm][1], 512):
                        pcs.append((kind_, nm, p0, min(512, L_[nm][1] - p0)))
            self.pieces[l_] = pcs
            s('wb_in%d' % l_, [len(pcs), 128, D // 128, 512], BF16)
        for l in range(dep):
            s('wb_out%d' % l, [D // 512, 128, D // 128, 512], BF16)
            s('wb_g%d' % l, [len(self.ffn_gugroups()), 128, D // 128, 256], BF16); s('wb_u%d' % l, [len(self.ffn_gugroups()), 128, D // 128, 256], BF16)
            s('wb_d%d' % l, [len(self.ffn_fgroups()), D // 512, 128, 8, 512], BF16)
        nfm = max(sum(-(-EL[n][1] // 128) for n in FM_EVEN), sum(-(-OL[n][1] // 128) for n in FM_ODD))
        ntm = max(sum(EL[n][1] for n in TM_EVEN), sum(OL[n][1] for n in TM_ODD))
        s('pT', [nfm, 128, TA], BF16)
        s('pV', [TA, ntm], BF16)
        s('gla_o', [TA, c['GH'] * 256], F32)
        self.pf = [Tl(self.nc.alloc_psum_tensor('pf%d' % k, [128, 512], F32), 'pf%d' % k) for k in range(6)]
        self.pb = [Tl(self.nc.alloc_psum_tensor('pb%d' % k, [128, 1024], BF16), 'pb%d' % k) for k in range(2)]
        self.pfi = 0
        self.pbi = 0
        self.ident = self.sb('ident', [128, 128], F32, persist=True)
        self.identb = self.sb('identb', [128, 128], BF16, persist=True)
        self.s.dma('sp', self.ident[:], self.ins['k_ident'][:, :], w=[self.ident.b])
        self.s.op('dve', lambda e: e.tensor_copy(out=self.identb[:], in_=self.ident[:]), r=[self.ident.b], w=[self.identb.b])

    def sb(self, name, shape, dt, persist=False):
        self._uid = getattr(self, '_uid', 0) + 1
        nm = '%s_%d' % (name, self._uid)
        if persist:
            return Tl(self.nc.alloc_sbuf_tensor(nm, list(shape), dt), nm)
        return Tl(self._es.enter_context(self.nc.sbuf_tensor(nm, list(shape), dt)), nm)

    def stage_begin(self):
        import contextlib
        self._es = contextlib.ExitStack()

    def stage_end(self):
        self.barrier()
        self._es.close()

    def barrier(self):
        S = self.s
        toks = [(k, S.cnt[k]) for k in S.ENG if S.cnt[k] > 0]
        toks += [(k, v) for k, v in S.dtot.items() if v > 0]
        for e in S.ENG:
            S._wait(e, toks)

    def nextpf(self):
        t = self.pf[self.pfi % len(self.pf)]
        self.pfi += 1
        return t

    def nextpb(self):
        t = self.pb[self.pbi % len(self.pb)]
        self.pbi += 1
        return t

    def ffn_split(self):
        FC = self.cfg['F'] // 128
        nsplit = 2 if FC > 48 else 1
        FS = -(-FC // nsplit)
        return [(sp * FS, min(FC, (sp + 1) * FS)) for sp in range(nsplit)]

    def ffn_fgroups(self):
        out = []
        for sp, (lo, hi) in enumerate(self.ffn_split()):
            for fg in range(lo, hi, 8):
                out.append((sp, fg, min(8, hi - fg)))
        return out

    def ffn_gugroups(self):
        out = []
        for sp, (lo, hi) in enumerate(self.ffn_split()):
            for fg in range(lo, hi, 2):
                out.append((sp, fg, min(2, hi - fg)))
        return out

    def cast_blk(self, dst, src, dname):
        self.s.dma('pool', dst, src.rearrange("(c p) n -> p c n", p=128), w=[self.db(dname, 0)])

    def stage_cast(self):
        I = self.ins
        c = self.cfg
        D, F = c['D'], c['F']
        for l_, (wn, L_) in enumerate((('ev_w_in', self.EL), ('od_w_in', self.OL))):
            for i, (kind, nm, p0, pw) in enumerate(self.pieces[l_]):
                c0 = L_[nm][0] + p0
                self.cast_blk(self.scr['wb_in%d' % l_][i, :, :, 0:pw], I[wn][0][:, c0:c0 + pw], 'wb_in%d' % l_)
        for l in range(c['depth']):
            for g in range(D // 512):
                self.cast_blk(self.scr['wb_out%d' % l][g], I['w_out'][l][:, g * 512:(g + 1) * 512], 'wb_out%d' % l)
            for g, (sp, fg, nf) in enumerate(self.ffn_gugroups()):
                self.cast_blk(self.scr['wb_g%d' % l][g, :, :, 0:nf * 128], I['ffn_w_gate'][l][:, fg * 128:(fg + nf) * 128], 'wb_g%d' % l)
                self.cast_blk(self.scr['wb_u%d' % l][g, :, :, 0:nf * 128], I['ffn_w_up'][l][:, fg * 128:(fg + nf) * 128], 'wb_u%d' % l)
            for gi, (sp, fg, nf) in enumerate(self.ffn_fgroups()):
                for ng in range(D // 512):
                    self.cast_blk(self.scr['wb_d%d' % l][gi, ng, :, 0:nf, :], I['ffn_w_down'][l][fg * 128:(fg + nf) * 128, ng * 512:(ng + 1) * 512],
                                  'wb_d%d' % l)

    def stage_init_x(self):
        T, C = self.cfg['T'], self.cfg['C']
        for t0 in range(0, T, 512):
            self.s.dma('sp', self.scr['xres'][t0:t0 + 512, :], self.ins['x'][t0:t0 + 512, :],
                       w=self.dbs('xres', t0 // 128, t0 // 128 + 4))
        self.s.dma('sp', self.scr['xres'][T:T + C, :], self.ins['ctx'][:, :], w=self.dbs('xres', T // 128, (T + C) // 128))

    def stage_mod(self):
        c = self.cfg
        D, dep = c['D'], c['depth']
        KC = D // 128
        S = self.s
        nc = self.nc
        cv = self.sb('mod_cv', [2, D], F32)
        scT = self.sb('mod_scT', [128, KC, 2], F32)
        S.dma('sp', cv[:], self.ins['cvec'][:, :], w=[cv.b])
        S.op('act', lambda e: e.activation(out=cv[:], in_=cv[:], func=AF.Silu), r=[cv.b], w=[cv.b])
        for g in range(0, KC, 64):
            n = min(64, KC - g)
            ps = self.nextpf()
            for k in range(n):
                S.op('pe', lambda e, k=k: e.transpose(out=ps[:, 2 * k:2 * k + 2], in_=cv[0:2, (g + k) * 128:(g + k + 1) * 128],
                                                      identity=self.ident[0:2, 0:2]), r=[cv.b, self.ident.b], w=[ps.b])
            S.op('dve', lambda e: e.tensor_copy(out=scT[:, g:g + n, :].rearrange("p a b -> p (a b)"), in_=ps[:, 0:2 * n]),
                 r=[ps.b], w=[scT.b])
        wt = [self.sb('mod_w%d' % k, [128, 2048], F32) for k in range(3)]
        res = self.sb('mod_res', [2, 2048], F32)
        bia = self.sb('mod_bias', [2, 2048], F32)
        wi = 0
        for l in range(dep):
            for n0 in range(0, 6 * D, 2048):
                pss = [self.nextpf() for _ in range(4)]
                for kc in range(KC):
                    w = wt[wi % 3]
                    wi += 1
                    S.dma('sp', w[:], self.ins['ada_w'][l, kc * 128:(kc + 1) * 128, n0:n0 + 2048], w=[w.b])
                    for j in range(4):
                        S.op('pe', lambda e, j=j, w=w, kc=kc: e.matmul(pss[j][0:2, :], lhsT=scT[:, kc, :], rhs=w[:, j * 512:(j + 1) * 512],
                                                                      start=(kc == 0), stop=(kc == KC - 1)),
                             r=[scT.b, w.b], w=[pss[j].b])
                for r_ in range(2):
                    S.dma('sp', bia[r_:r_ + 1, :], self.ins['ada_b'][l:l + 1, n0:n0 + 2048], w=[bia.b])
                for j in range(4):
                    S.op('dve', lambda e, j=j: e.tensor_tensor(out=res[:, j * 512:(j + 1) * 512], in0=pss[j][0:2, :],
                                                               in1=bia[:, j * 512:(j + 1) * 512], op=ALU.add),
                         r=[pss[j].b, bia.b], w=[res.b])
                S.dma('sp', self.scr['mod'][l, :, n0:n0 + 2048], res[:], r=[res.b], w=[self.db('mod', l)])
        self.barrier()
        self._es.close()
        self.stage_begin()
        sc_ = self.sb('mod_sc', [2, D], F32)
        g = self.sb('mod_g', [2, D], F32)
        for l in range(dep):
            for k, (gn, sci, shi, gti) in enumerate((('norm_mix_g', 1, 0, 2), ('norm_ffn_g', 4, 3, 5))):
                S.dma('sp', sc_[:], self.scr['mod'][l, :, sci * D:(sci + 1) * D], r=[self.db('mod', l)], w=[sc_.b])
                for r_ in range(2):
                    S.dma('sp', g[r_:r_ + 1, :], self.ins[gn][l:l + 1, :], w=[g.b])
                S.op('dve', lambda e: e.scalar_tensor_tensor(out=sc_[:], in0=sc_[:], scalar=1.0, in1=g[:], op0=ALU.add, op1=ALU.mult),
                     r=[sc_.b, g.b], w=[sc_.b])
                S.dma('sp', self.scr['modv'][l, 3 * k], sc_[:], r=[sc_.b], w=[self.db('modv', l)])
                S.dma('sp', self.scr['modv'][l, 3 * k + 1], self.scr['mod'][l, :, shi * D:(shi + 1) * D], r=[self.db('mod', l)], w=[self.db('modv', l)])
                S.dma('sp', self.scr['modv'][l, 3 * k + 2], self.scr['mod'][l, :, gti * D:(gti + 1) * D], r=[self.db('mod', l)], w=[self.db('modv', l)])

    def load_bcast(self, tile, l, which, row):
        src = self.scr['modv'][l, which, row:row + 1, :].partition_broadcast(128)
        self.s.dma('sp', tile[:], src, r=[self.db('modv', l)], w=[tile.b])

    def stage_norm(self, l, which, tiles, final=False):
        c = self.cfg
        D, T = c['D'], c['T']
        KC = D // 128
        S = self.s
        tag = 'n%d%d%d' % (l, which, int(final))
        A1 = self.sb(tag + 'A', [128, D], F32)
        S1 = self.sb(tag + 'S', [128, D], F32)
        A = [A1, A1]
        Sh = [S1, S1]
        currow = -1
        if final:
            S.dma('sp', A[0][:], self.ins['final_norm_g'][None, :].partition_broadcast(128), w=[A[0].b])
        xt = [self.sb(tag + 'x%d' % k, [128, D], F32) for k in range(2)]
        junk = self.sb(tag + 'j', [128, D], BF16)
        hb = [self.sb(tag + 'h%d' % k, [128, D], BF16) for k in range(2)]
        hf = [self.sb(tag + 'f%d' % k, [128, D], F32) for k in range(2)] if final else None
        ht = [self.sb(tag + 't%d' % k, [128, KC, 512], BF16) for k in range(2)]
        ss = [self.sb(tag + 's%d' % k, [128, 1], F32) for k in range(2)]
        rs = [self.sb(tag + 'r%d' % k, [128, 1], F32) for k in range(2)]
        for n, tt in enumerate(tiles):
            row = 0 if tt * 128 < T else 1
            if not final and row != currow:
                currow = row
                self.load_bcast(A1, l, 3 * which, row)
                self.load_bcast(S1, l, 3 * which + 1, row)
            x, h, t_, s_, r_ = xt[n % 2], hb[n % 2], ht[(tt // 4) % 2], ss[n % 2], rs[n % 2]
            tj = tt % 4
            S.dma('sp', x[:], self.scr['xres'][tt * 128:(tt + 1) * 128, :], r=[self.db('xres', tt)], w=[x.b])
            S.op('act', lambda e, x=x, s_=s_: e.activation(out=junk[:], in_=x[:], func=AF.Square, accum_out=s_[:]),
                 r=[x.b], w=[junk.b, s_.b])
            S.op('act', lambda e, s_=s_, r_=r_: e.activation(out=r_[:], in_=s_[:], func=AF.Sqrt, scale=1.0 / D, bias=1e-6),
                 r=[s_.b], w=[r_.b])
            S.op('dve', lambda e, r_=r_: e.reciprocal(out=r_[:], in_=r_[:]), r=[r_.b], w=[r_.b])
            if final:
                f = hf[n % 2]
                S.op('dve', lambda e, x=x, r_=r_, f=f: e.scalar_tensor_tensor(out=f[:], in0=x[:], scalar=r_[:], in1=A[0][:],
                                                                              op0=ALU.mult, op1=ALU.mult), r=[x.b, r_.b, A[0].b], w=[f.b])
                S.dma('pool', self.out[tt * 128:(tt + 1) * 128, :], f[:], r=[f.b], w=[self.db('out', tt)])
                continue
            S.op('dve', lambda e, x=x, r_=r_, row=row: e.scalar_tensor_tensor(out=x[:], in0=x[:], scalar=r_[:], in1=A[row][:],
                                                                              op0=ALU.mult, op1=ALU.mult), r=[x.b, r_.b, A[row].b], w=[x.b])
            S.op('pool', lambda e, x=x, h=h, row=row: e.tensor_tensor(out=h[:], in0=x[:], in1=Sh[row][:], op=ALU.add),
                 r=[x.b, Sh[row].b], w=[h.b])
            for g in range(0, KC, 8):
                ps = self.nextpb()
                for k in range(8):
                    S.op('pe', lambda e, k=k, g=g, h=h, ps=ps: e.transpose(out=ps[:, k * 128:(k + 1) * 128], in_=h[:, (g + k) * 128:(g + k + 1) * 128],
                                                                          identity=self.identb[:]), r=[h.b, self.identb.b], w=[ps.b])
                for k in range(8):
                    S.op('act' if k % 2 else 'dve',
                         lambda e, g=g, k=k, ps=ps, t_=t_, tj=tj: (e.copy if e is self.nc.scalar else e.tensor_copy)(
                             out=t_[:, g + k, tj * 128:(tj + 1) * 128], in_=ps[:, k * 128:(k + 1) * 128]), r=[ps.b], w=[t_.b])
            if tj == 3 or n == len(tiles) - 1 or tiles[n + 1] // 4 != tt // 4:
                nn = (tj + 1) * 128
                S.dma('pool', self.scr['hT'][tt // 4, :, :, 0:nn], t_[:, :, 0:nn], r=[t_.b], w=[self.db('hT', tt // 4)])

    def tok_blocks(self, upto=None):
        c = self.cfg
        TA = c['TA'] if upto is None else upto
        return [(t0, min(512, TA - t0)) for t0 in range(0, TA, 512)]

    def stage_proj(self, l):
        c = self.cfg
        D, T = c['D'], c['T']
        KC = D // 128
        S = self.s
        even = (l % 2 == 0)
        L = self.EL if even else self.OL
        wbn = 'wb_in0' if even else 'wb_in1'
        wb = self.scr[wbn]
        fmn, tmn, ropen = (FM_EVEN, TM_EVEN, ROPE_EVEN) if even else (FM_ODD, TM_ODD, ROPE_ODD)
        self.fmidx, self.tmcol = {}, {}
        ci = 0
        for nm in fmn:
            for j in range(-(-L[nm][1] // 128)):
                self.fmidx[(nm, j)] = ci
                ci += 1
        co = 0
        for nm in tmn:
            self.tmcol[nm] = co
            co += L[nm][1]
        pieces = [(i,) + pc for i, pc in enumerate(self.pieces[l % 2])]
        hblk = [self.sb('pj_h%d' % k, [128, KC, 512], BF16) for k in range(2)]
        wt = [self.sb('pj_w%d' % k, [128, KC, 512], BF16) for k in range(2)]
        perm = self.sb('pj_perm', [128, 128], F32)
        cs = self.sb('pj_cos', [128, 512], F32)
        sn = self.sb('pj_sin', [128, 512], F32)
        q32 = [self.sb('pj_q%d' % k, [128, 512], F32) for k in range(2)]
        ta = [self.sb('pj_ta%d' % k, [128, 512], F32) for k in range(2)]
        tb_ = [self.sb('pj_tb%d' % k, [128, 512], F32) for k in range(2)]
        ob = [self.sb('pj_o%d' % k, [128, 512], BF16) for k in range(3)]
        S.dma('sp', perm[:], self.ins['k_perm'][:, :], w=[perm.b])
        wi = 0
        oi = 0
        qi = 0
        for bi, (t0, n) in enumerate(self.tok_blocks()):
            hb = hblk[bi % 2]
            latent = t0 < T
            S.dma('sp', hb[:, :, 0:n], self.scr['hT'][t0 // 512, :, :, 0:n], r=[self.db('hT', t0 // 512)], w=[hb.b])
            if latent:
                S.dma('sp', cs[:, 0:n], self.ins['k_cos'][:, t0:t0 + n], w=[cs.b])
                S.dma('sp', sn[:, 0:n], self.ins['k_sin'][:, t0:t0 + n], w=[sn.b])
            for pi, kind, nm, p0, pw in pieces:
                w = wt[wi % 2]
                wi += 1
                c0 = L[nm][0] + p0
                S.dma('sp', w[:, :, 0:pw], wb[pi, :, :, 0:pw], r=[self.db(wbn, 0)], w=[w.b])
                if kind == 'fm':
                    for j in range(-(-pw // 128)):
                        cw = min(128, pw - j * 128)
                        ps = self.nextpf()
                        for kc in range(KC):
                            S.op('pe', lambda e, kc=kc, j=j, cw=cw, w=w, ps=ps: e.matmul(
                                ps[0:cw, 0:n], lhsT=w[:, kc, j * 128:j * 128 + cw], rhs=hb[:, kc, 0:n], start=(kc == 0), stop=(kc == KC - 1)),
                                r=[w.b, hb.b], w=[ps.b])
                        o = ob[oi % 3]
                        oi += 1
                        if nm in ropen and latent:
                            q = q32[qi % 2]; a_ = ta[qi % 2]; b_ = tb_[qi % 2]
                            qi += 1
                            S.op('act', lambda e, q=q, ps=ps: e.copy(out=q[:, 0:n], in_=ps[:, 0:n]), r=[ps.b], w=[q.b])
                            ps2 = self.nextpf()
                            S.op('pe', lambda e, q=q, ps2=ps2: e.matmul(ps2[:, 0:n], lhsT=perm[:, :], rhs=q[:, 0:n], start=True, stop=True),
                                 r=[perm.b, q.b], w=[ps2.b])
                            S.op('pool', lambda e, q=q, a_=a_: e.tensor_tensor(out=a_[:, 0:n], in0=q[:, 0:n], in1=cs[:, 0:n], op=ALU.mult),
                                 r=[q.b, cs.b], w=[a_.b])
                            S.op('dve', lambda e, ps2=ps2, b_=b_: e.tensor_tensor(out=b_[:, 0:n], in0=ps2[:, 0:n], in1=sn[:, 0:n], op=ALU.mult),
                                 r=[ps2.b, sn.b], w=[b_.b])
                            S.op('dve', lambda e, a_=a_, b_=b_, o=o: e.tensor_tensor(out=o[:, 0:n], in0=a_[:, 0:n], in1=b_[:, 0:n], op=ALU.add),
                                 r=[a_.b, b_.b], w=[o.b])
                        else:
                            if oi % 2:
                                S.op('act', lambda e, o=o, ps=ps, cw=cw: e.copy(out=o[0:cw, 0:n], in_=ps[0:cw, 0:n]), r=[ps.b], w=[o.b])
                            else:
                                S.op('dve', lambda e, o=o, ps=ps, cw=cw: e.tensor_copy(out=o[0:cw, 0:n], in_=ps[0:cw, 0:n]), r=[ps.b], w=[o.b])
                        ch = self.fmidx[(nm, (p0 // 128) + j)]
                        S.dma('pool', self.scr['pT'][ch, 0:cw, t0:t0 + n], o[0:cw, 0:n], r=[o.b], w=[self.db('pT', ch)])
                else:
                    for j in range(n // 128):
                        ps = self.nextpf()
                        for kc in range(KC):
                            S.op('pe', lambda e, kc=kc, j=j, w=w, ps=ps: e.matmul(
                                ps[:, 0:pw], lhsT=hb[:, kc, j * 128:(j + 1) * 128], rhs=w[:, kc, 0:pw], start=(kc == 0), stop=(kc == KC - 1)),
                                r=[w.b, hb.b], w=[ps.b])
                        o = ob[oi % 3]
                        oi += 1
                        if oi % 2:
                            S.op('act', lambda e, o=o, ps=ps: e.copy(out=o[:, 0:pw], in_=ps[:, 0:pw]), r=[ps.b], w=[o.b])
                        else:
                            S.op('dve', lambda e, o=o, ps=ps: e.tensor_copy(out=o[:, 0:pw], in_=ps[:, 0:pw]), r=[ps.b], w=[o.b])
                        cc = self.tmcol[nm] + p0
                        S.dma('pool', self.scr['pV'][t0 + j * 128:t0 + (j + 1) * 128, cc:cc + pw], o[:, 0:pw], r=[o.b],
                              w=[self.db('pV', (t0 // 128) + j)])

    def attn_work(self, tag, nkmax, dv, nbuf=1):
        nblk = -(-nkmax // 128)
        W = []
        for k in range(nbuf):
            W.append(dict(
                S=self.sb(tag + 'S%d' % k, [128, nkmax + 1], F32), P=self.sb(tag + 'P%d' % k, [128, nkmax + 1], BF16),
                PT=self.sb(tag + 'PT%d' % k, [128, nblk, 128], BF16), m=self.sb(tag + 'm%d' % k, [128, 1], F32),
                ss=self.sb(tag + 'ss%d' % k, [128, 1], F32), O=self.sb(tag + 'O%d' % k, [128, dv], F32)))
        return W

    def attn_unit(self, W, qT, qb, parts, scale, dv, sink=None):
        S = self.s
        Ssb, P, PT, m, ss, O = W['S'], W['P'], W['PT'], W['m'], W['ss'], W['O']
        off = 0
        ei = 0
        for (kT, kbuf, bias, bbuf, vlist) in parts:
            n_all = kT.shape[1]
            for c0 in range(0, n_all, 512):
                n = min(512, n_all - c0)
                ps = self.nextpf()
                S.op('pe', lambda e, ps=ps, c0=c0, n=n, kT=kT: e.matmul(ps[:, 0:n], lhsT=qT, rhs=kT[:, c0:c0 + n], start=True, stop=True),
                     r=[qb, kbuf], w=[ps.b])
                if bias is not None:
                    S.op('dve', lambda e, ps=ps, c0=c0, n=n, off=off, bias=bias: e.scalar_tensor_tensor(
                        out=Ssb[:, off:off + n], in0=ps[:, 0:n], scalar=scale, in1=bias[:, c0:c0 + n], op0=ALU.mult, op1=ALU.add),
                        r=[ps.b, bbuf], w=[Ssb.b])
                elif ei % 2 == 0:
                    S.op('act', lambda e, ps=ps, n=n, off=off: e.mul(out=Ssb[:, off:off + n], in_=ps[:, 0:n], mul=scale), r=[ps.b], w=[Ssb.b])
                else:
                    S.op('dve', lambda e, ps=ps, n=n, off=off: e.tensor_scalar(out=Ssb[:, off:off + n], in0=ps[:, 0:n], scalar1=scale,
                                                                             scalar2=None, op0=ALU.mult), r=[ps.b], w=[Ssb.b])
                ei += 1
                off += n
        nk = off
        ncol = nk
        if sink is not None:
            S.op('dve', lambda e: e.tensor_copy(out=Ssb[:, nk:nk + 1], in_=sink[0][:, 0:1]), r=[sink[1]], w=[Ssb.b])
            ncol = nk + 1
        S.op('dve', lambda e: e.reduce_max(out=m[:], in_=Ssb[:, 0:ncol], axis=AX.X), r=[Ssb.b], w=[m.b])
        S.op('dve', lambda e: e.tensor_scalar(out=m[:], in0=m[:], scalar1=-1.0, scalar2=None, op0=ALU.mult), r=[m.b], w=[m.b])
        S.op('act', lambda e: e.activation(out=P[:, 0:ncol], in_=Ssb[:, 0:ncol], func=AF.Exp, bias=m[:], scale=1.0, accum_out=ss[:]),
             r=[Ssb.b, m.b], w=[P.b, ss.b])
        S.op('dve', lambda e: e.reciprocal(out=ss[:], in_=ss[:]), r=[ss.b], w=[ss.b])
        vall = []
        off = 0
        for (kT, kbuf, bias, bbuf, vlist) in parts:
            for (vap, kn, vbuf) in vlist:
                vall.append((vap, kn, vbuf, off))
                off += kn
        for g in range(0, len(vall), 8):
            grp = vall[g:g + 8]
            pb = self.nextpb()
            for k, (vap, kn, vbuf, o_) in enumerate(grp):
                S.op('pe', lambda e, k=k, kn=kn, o_=o_, pb=pb: e.transpose(out=pb[0:kn, k * 128:(k + 1) * 128], in_=P[:, o_:o_ + kn],
                                                                          identity=self.identb[:]), r=[P.b, self.identb.b], w=[pb.b])
            eng = 'act' if (g // 8) % 2 else 'dve'
            S.op(eng, lambda e, g=g, pb=pb, ng=len(grp): (e.copy if e is self.nc.scalar else e.tensor_copy)(
                out=PT[:, g:g + ng, :].rearrange("p a b -> p (a b)"), in_=pb[:, 0:ng * 128]), r=[pb.b], w=[PT.b])
        po = self.nextpf()
        for k, (vap, kn, vbuf, o_) in enumerate(vall):
            S.op('pe', lambda e, k=k, kn=kn, vap=vap: e.matmul(po[:, 0:dv], lhsT=PT[0:kn, k, :], rhs=vap, start=(k == 0), stop=(k == len(vall) - 1)),
                 r=[PT.b, vbuf], w=[po.b])
        S.op('act', lambda e: e.activation(out=O[:, 0:dv], in_=po[:, 0:dv], func=AF.Identity, scale=ss[:]), r=[po.b, ss.b], w=[O.b])

    def attn_work2(self, tag, nkmax, dv, nbuf=2):
        nblk = -(-nkmax // 128)
        nch = -(-nkmax // 512)
        W = []
        for k in range(nbuf):
            W.append(dict(
                P=self.sb(tag + 'P%d' % k, [128, nkmax], BF16), PT=self.sb(tag + 'PT%d' % k, [128, nblk, 128], BF16),
                mc=self.sb(tag + 'mc%d' % k, [128, nch], F32), sc=self.sb(tag + 'sc%d' % k, [128, nch], F32),
                m=self.sb(tag + 'm%d' % k, [128, 1], F32), ss=self.sb(tag + 'ss%d' % k, [128, 1], F32),
                O=self.sb(tag + 'O%d' % k, [128, dv], F32)))
        return W

    def attn_unit2(self, W, qT, qb, kT, kbuf, vlist, scale, dv):
        S = self.s
        P, PT, mc, sc, m, ss, O = W['P'], W['PT'], W['mc'], W['sc'], W['m'], W['ss'], W['O']
        nk = kT.shape[1]
        chunks = [(c0, min(512, nk - c0)) for c0 in range(0, nk, 512)]
        nch = len(chunks)
        for ci, (c0, n) in enumerate(chunks):
            ps = self.nextpf()
            S.op('pe', lambda e, ps=ps, c0=c0, n=n: e.matmul(ps[:, 0:n], lhsT=qT, rhs=kT[:, c0:c0 + n], start=True, stop=True),
                 r=[qb, kbuf], w=[ps.b])
            S.op('dve', lambda e, ps=ps, n=n, ci=ci: e.reduce_max(out=mc[:, ci:ci + 1], in_=ps[:, 0:n], axis=AX.X), r=[ps.b], w=[mc.b])
        S.op('dve', lambda e: e.reduce_max(out=m[:], in_=mc[:, 0:nch], axis=AX.X), r=[mc.b], w=[m.b])
        S.op('dve', lambda e: e.tensor_scalar(out=m[:], in0=m[:], scalar1=-scale, scalar2=None, op0=ALU.mult), r=[m.b], w=[m.b])
        for ci, (c0, n) in enumerate(chunks):
            ps = self.nextpf()
            S.op('pe', lambda e, ps=ps, c0=c0, n=n: e.matmul(ps[:, 0:n], lhsT=qT, rhs=kT[:, c0:c0 + n], start=True, stop=True),
                 r=[qb, kbuf], w=[ps.b])
            S.op('act', lambda e, ps=ps, c0=c0, n=n, ci=ci: e.activation(out=P[:, c0:c0 + n], in_=ps[:, 0:n], func=AF.Exp, bias=m[:], scale=scale,
                                                                        accum_out=sc[:, ci:ci + 1]), r=[ps.b, m.b], w=[P.b, sc.b])
        S.op('dve', lambda e: e.reduce_sum(out=ss[:], in_=sc[:, 0:nch], axis=AX.X), r=[sc.b], w=[ss.b])
        S.op('dve', lambda e: e.reciprocal(out=ss[:], in_=ss[:]), r=[ss.b], w=[ss.b])
        off = 0
        vall = []
        for (vap, kn, vbuf) in vlist:
            vall.append((vap, kn, vbuf, off))
            off += kn
        for g in range(0, len(vall), 8):
            grp = vall[g:g + 8]
            pb = self.nextpb()
            for k, (vap, kn, vbuf, o_) in enumerate(grp):
                S.op('pe', lambda e, k=k, kn=kn, o_=o_, pb=pb: e.transpose(out=pb[0:kn, k * 128:(k + 1) * 128], in_=P[:, o_:o_ + kn],
                                                                          identity=self.identb[:]), r=[P.b, self.identb.b], w=[pb.b])
            eng = 'act' if (g // 8) % 3 == 2 else 'dve'
            S.op(eng, lambda e, g=g, pb=pb, ng=len(grp): (e.copy if e is self.nc.scalar else e.tensor_copy)(
                out=PT[:, g:g + ng, :].rearrange("p a b -> p (a b)"), in_=pb[:, 0:ng * 128]), r=[pb.b], w=[PT.b])
        po = self.nextpf()
        for k, (vap, kn, vbuf, o_) in enumerate(vall):
            S.op('pe', lambda e, k=k, kn=kn, vap=vap: e.matmul(po[:, 0:dv], lhsT=PT[0:kn, k, :], rhs=vap, start=(k == 0), stop=(k == len(vall) - 1)),
                 r=[PT.b, vbuf], w=[po.b])
        S.op('act', lambda e: e.activation(out=O[:, 0:dv], in_=po[:, 0:dv], func=AF.Identity, scale=ss[:]), r=[po.b, ss.b], w=[O.b])

    def load_fm(self, tile, nm, j, q='sp'):
        ch = self.fmidx[(nm, j)]
        self.s.dma(q, tile[:], self.scr['pT'][ch, :, :], r=[self.db('pT', ch)], w=[tile.b])

    def load_tm(self, tile, nm, c0, w, q='sp'):
        cc = self.tmcol[nm] + c0
        ntl = self.cfg['TA'] // 128
        self.s.dma(q, tile[:], self.scr['pV'][:, cc:cc + w].rearrange("(n p) c -> p n c", p=128),
                   r=self.dbs('pV', 0, ntl), w=[tile.b])

    def store_y(self, O, ob, tt, col, w):
        S = self.s
        S.op('dve', lambda e: e.tensor_copy(out=ob[:, 0:w], in_=O[:, 0:w]), r=[O.b], w=[ob.b])
        S.dma('pool', self.scr['y_tm'][tt * 128:(tt + 1) * 128, col:col + w], ob[:, 0:w], r=[ob.b], w=[self.db('y_tm', tt)])

    def stage_swa(self, l):
        c = self.cfg
        T, C, TA, D = c['T'], c['C'], c['TA'], c['D']
        S = self.s
        ntl, nlat = TA // 128, T // 128
        scale = 128.0 ** -0.5
        kT = [self.sb('sw_k%d' % k, [128, TA], BF16) for k in range(2)]
        V = [self.sb('sw_v%d' % k, [128, ntl, 128], BF16) for k in range(2)]
        qT = [self.sb('sw_q%d' % k, [128, TA], BF16) for k in range(2)]
        sk = [self.sb('sw_s%d' % k, [128, 1], F32) for k in range(2)]
        mask = self.sb('sw_mask', [128, 384], F32)
        ob = [self.sb('sw_ob%d' % k, [128, 128], BF16) for k in range(2)]
        S.dma('sp', mask[:], self.ins['k_swamask'][:, :], w=[mask.b])
        W = self.attn_work('sw', C + 384, 128, nbuf=2)
        u = 0
        for g in range(c['SKV']):
            k_, v_ = kT[g % 2], V[g % 2]
            self.load_fm(k_, 'sk', g)
            self.load_tm(v_, 'sv', g * 128, 128)
            for hh in range(4):
                h = g * 4 + hh
                q_, s_ = qT[h % 2], sk[h % 2]
                self.load_fm(q_, 'sq', h)
                S.dma('sp', s_[:], self.ins['swa_sink'][0:1, h:h + 1].partition_broadcast(128), w=[s_.b])
                for tt in range(ntl):
                    ctxpart = (k_[:, T:TA], k_.b, None, None, [(v_[:, nlat + j, :], 128, v_.b) for j in range(C // 128)])
                    if tt < nlat:
                        lo, hi = max(tt - 1, 0), min(tt + 2, nlat)
                        m0 = (lo - (tt - 1)) * 128
                        parts = [ctxpart, (k_[:, lo * 128:hi * 128], k_.b, mask[:, m0:m0 + (hi - lo) * 128], mask.b,
                                           [(v_[:, j, :], 128, v_.b) for j in range(lo, hi)])]
                    else:
                        parts = [ctxpart]
                    w_ = W[u % 2]
                    self.attn_unit(w_, q_[:, tt * 128:(tt + 1) * 128], q_.b, parts, scale, 128, sink=(s_, s_.b))
                    self.store_y(w_['O'], ob[u % 2], tt, D // 2 + h * 128, 128)
                    u += 1

    def stage_gla(self, l):
        c = self.cfg
        T, C, TA, D, GH = c['T'], c['C'], c['TA'], c['D'], c['GH']
        S = self.s
        I = self.ins
        qT = self.sb('gl_q', [128, TA], BF16)
        kT = self.sb('gl_k', [128, TA], BF16)
        lg = [self.sb('gl_l%d' % d, [128, TA], F32) for d in range(2)]
        dn1 = self.sb('gl_dn', [16, TA], BF16)
        dn = [dn1, dn1]
        upf = self.sb('gl_upf', [16, 128], F32)
        upb = self.sb('gl_upb', [16, 128], BF16)
        nb = self.sb('gl_nb', [128, 1], F32)
        et = self.sb('gl_et', [128, 512], F32)
        tri = self.sb('gl_tri', [64, 2, 64], F32)
        ones = self.sb('gl_ones', [128, 64], F32)
        ng = self.sb('gl_ng', [64, 256], F32)
        St = self.sb('gl_S', [128, 256], F32)
        Sb = self.sb('gl_Sb', [128, 256], BF16)
        vg = [self.sb('gl_v%d' % k, [64, 8, 256], BF16) for k in range(2)]
        rg = [self.sb('gl_r%d' % k, [64, 8, 256], BF16) for k in range(2)]
        og = [self.sb('gl_o%d' % k, [64, 8, 256], F32) for k in range(2)]
        yg = [self.sb('gl_y%d' % k, [64, 8, 256], BF16) for k in range(2)]
        cc = [self.sb('gl_c%d' % k, [128, 64], F32) for k in range(2)]
        c2 = [self.sb('gl_c2%d' % k, [128, 64], F32) for k in range(2)]
        ncl = [self.sb('gl_ncl%d' % k, [128, 1], F32) for k in range(2)]
        ebl = [self.sb('gl_ebl%d' % k, [128, 1], F32) for k in range(2)]
        eb = [self.sb('gl_eb%d' % k, [128, 64], F32) for k in range(2)]
        ei = [self.sb('gl_ei%d' % k, [128, 64], F32) for k in range(2)]
        eu = [self.sb('gl_eu%d' % k, [128, 64], F32) for k in range(2)]
        qd = [self.sb('gl_qd%d' % k, [128, 64], BF16) for k in range(2)]
        ki = [self.sb('gl_ki%d' % k, [128, 64], BF16) for k in range(2)]
        ku = [self.sb('gl_ku%d' % k, [128, 64], BF16) for k in range(2)]
        kut = [self.sb('gl_kut%d' % k, [64, 128], BF16) for k in range(2)]
        at = [self.sb('gl_at%d' % k, [64, 64], BF16) for k in range(2)]
        ot = [self.sb('gl_ot%d' % k, [64, 256], F32) for k in range(2)]
        sr = [None, None]
        srg = [self.sb('gl_srg%d' % k, [64, 8, 256], F32) for k in range(2)]
        epst = self.sb('gl_eps', [128, 1], F32)
        S.op('pool', lambda e: e.memset(epst[:], 1e-6), w=[epst.b])
        jk = self.sb('gl_jk', [64, 256], F32)
        ss = [self.sb('gl_ss%d' % k, [64, 1], F32) for k in range(2)]
        S.dma('sp', tri[:], I['k_tri'].rearrange("d j i -> j d i"), w=[tri.b])
        S.op('pool', lambda e: e.memset(ones[:], 1.0), w=[ones.b])
        S.dma('sp', ng[:], I['gla_norm_g'][0:1, :].partition_broadcast(64), w=[ng.b])
        gcv = self.tmcol['gv']
        gcr = self.tmcol['gr']
        groups = [(T + g0, min(8, (C - g0) // 64)) for g0 in range(0, C, 512)] + [(g0, 8) for g0 in range(0, T, 512)]
        n = 0
        for h in range(GH):
            self.load_fm(qT, 'gq', h)
            self.load_fm(kT, 'gk', h)
            for d, (un, bn) in enumerate((('gla_gate_up_f', 'gla_gate_bias_f'), ('gla_gate_up_b', 'gla_gate_bias_b'))):
                ch = self.fmidx[(('dnf', 'dnb')[d], 0)]
                S.dma('sp', dn[d][:], self.scr['pT'][ch, 0:16, :], r=[self.db('pT', ch)], w=[dn[d].b])
                S.dma('sp', upf[:], I[un][0, :, h * 128:(h + 1) * 128], w=[upf.b])
                S.op('dve', lambda e: e.tensor_copy(out=upb[:], in_=upf[:]), r=[upf.b], w=[upb.b])
                S.dma('sp', nb[:], I[bn][0, h * 128:(h + 1) * 128].rearrange("(p o) -> p o", o=1), w=[nb.b])
                S.op('dve', lambda e: e.tensor_scalar(out=nb[:], in0=nb[:], scalar1=-1.0, scalar2=None, op0=ALU.mult), r=[nb.b], w=[nb.b])
                for (t0, nt) in self.tok_blocks():
                    ps = self.nextpf()
                    S.op('pe', lambda e, ps=ps, d=d, t0=t0, nt=nt: e.matmul(ps[:, 0:nt], lhsT=upb[:, :], rhs=dn[d][:, t0:t0 + nt], start=True, stop=True),
                         r=[upb.b, dn[d].b], w=[ps.b])
                    S.op('act', lambda e, ps=ps, nt=nt: e.activation(out=et[:, 0:nt], in_=ps[:, 0:nt], func=AF.Exp, bias=nb[:], scale=-1.0),
                         r=[ps.b, nb.b], w=[et.b])
                    S.op('act', lambda e, d=d, t0=t0, nt=nt: e.activation(out=lg[d][:, t0:t0 + nt], in_=et[:, 0:nt], func=AF.Ln, bias=1.0, scale=1.0),
                         r=[et.b], w=[lg[d].b])
            for d in range(2):
                S.op('pool', lambda e: e.memset(St[:], 0.0), w=[St.b])
                S.op('pool', lambda e: e.memset(Sb[:], 0.0), w=[Sb.b])
                glist = groups if d == 0 else [groups[i] for i in list(range(len(groups) - 1, -1, -1))]
                if d == 1:
                    nctx = -(-C // 512)
                    glist = groups[:nctx][::-1] + groups[nctx:][::-1]
                for gi, (g0, gn) in enumerate(glist):
                    v_, r_, o_, y_ = vg[gi % 2], rg[gi % 2], og[gi % 2], yg[gi % 2]
                    rows = slice(g0, g0 + gn * 64)
                    tl = list(range(g0 // 128, -(-(g0 + gn * 64) // 128)))
                    S.dma('sp', v_[:, 0:gn, :], self.scr['pV'][rows, gcv + h * 256:gcv + (h + 1) * 256].rearrange("(n p) c -> p n c", p=64),
                          r=[self.db('pV', t) for t in tl], w=[v_.b])
                    if d == 1:
                        S.dma('sp', r_[:, 0:gn, :], self.scr['pV'][rows, gcr + h * 256:gcr + (h + 1) * 256].rearrange("(n p) c -> p n c", p=64),
                              r=[self.db('pV', t) for t in tl], w=[r_.b])
                        S.dma('sp', o_[:, 0:gn, :], self.scr['gla_o'][rows, h * 256:(h + 1) * 256].rearrange("(n p) c -> p n c", p=64),
                              r=[self.db('gla_o', t) for t in tl], w=[o_.b])
                        srg_ = srg[gi % 2]
                        S.op('act', lambda e, srg_=srg_, r_=r_, gn=gn: e.activation(out=srg_[:, 0:gn, :], in_=r_[:, 0:gn, :], func=AF.Silu), r=[r_.b], w=[srg_.b])
                    korder = range(gn) if d == 0 else range(gn - 1, -1, -1)
                    for k in korder:
                        t0 = g0 + k * 64
                        i2 = n % 2
                        n += 1
                        c_, c2_, ncl_, ebl_, eb_, ei_, eu_ = cc[i2], c2[i2], ncl[i2], ebl[i2], eb[i2], ei[i2], eu[i2]
                        qd_, ki_, ku_, kut_, at_, ot_, sr_, ss_ = qd[i2], ki[i2], ku[i2], kut[i2], at[i2], ot[i2], sr[i2], ss[i2]
                        lch = lg[d][:, t0:t0 + 64]
                        S.op('dve', lambda e, c_=c_, lch=lch: e.tensor_tensor_scan(out=c_[:], data0=ones[:], data1=lch, initial=0.0,
                                                                                  op0=ALU.mult, op1=ALU.add), r=[ones.b, lg[d].b], w=[c_.b])
                        S.op('dve', lambda e, c_=c_, ncl_=ncl_: e.tensor_scalar(out=ncl_[:], in0=c_[:, 63:64], scalar1=-1.0 / 16, scalar2=None, op0=ALU.mult),
                             r=[c_.b], w=[ncl_.b])
                        if d == 0:
                            cu = c_
                        else:
                            S.op('dve', lambda e, c_=c_, c2_=c2_, lch=lch: e.scalar_tensor_tensor(out=c2_[:], in0=c_[:], scalar=-1.0, in1=lch,
                                                                                                   op0=ALU.mult, op1=ALU.add), r=[c_.b, lg[d].b], w=[c2_.b])
                            S.op('dve', lambda e, c_=c_, c2_=c2_: e.tensor_scalar(out=c2_[:], in0=c2_[:], scalar1=c_[:, 63:64], scalar2=None, op0=ALU.add),
                                 r=[c_.b, c2_.b], w=[c2_.b])
                            cu = c2_
                        S.op('act', lambda e, cu=cu, eb_=eb_: e.activation(out=eb_[:], in_=cu[:], func=AF.Exp, scale=-1.0 / 16), r=[cu.b], w=[eb_.b])
                        S.op('act', lambda e, cu=cu, ei_=ei_: e.activation(out=ei_[:], in_=cu[:], func=AF.Exp, scale=1.0 / 16), r=[cu.b], w=[ei_.b])
                        S.op('act', lambda e, cu=cu, eu_=eu_, ncl_=ncl_: e.activation(out=eu_[:], in_=cu[:], func=AF.Exp, scale=1.0 / 16, bias=ncl_[:]),
                             r=[cu.b, ncl_.b], w=[eu_.b])
                        S.op('act', lambda e, ebl_=ebl_, ncl_=ncl_: e.activation(out=ebl_[:], in_=ncl_[:], func=AF.Exp), r=[ncl_.b], w=[ebl_.b])
                        S.op('dve', lambda e, qd_=qd_, eb_=eb_, t0=t0: e.scalar_tensor_tensor(out=qd_[:], in0=qT[:, t0:t0 + 64], scalar=128.0 ** -0.5, in1=eb_[:],
                                                                                              op0=ALU.mult, op1=ALU.mult), r=[qT.b, eb_.b], w=[qd_.b])
                        S.op('pool', lambda e, ki_=ki_, ei_=ei_, t0=t0: e.tensor_tensor(out=ki_[:], in0=kT[:, t0:t0 + 64], in1=ei_[:], op=ALU.mult),
                             r=[kT.b, ei_.b], w=[ki_.b])
                        S.op('pool', lambda e, ku_=ku_, eu_=eu_, t0=t0: e.tensor_tensor(out=ku_[:], in0=kT[:, t0:t0 + 64], in1=eu_[:], op=ALU.mult),
                             r=[kT.b, eu_.b], w=[ku_.b])
                        pa = self.nextpf()
                        S.op('pe', lambda e, pa=pa, ki_=ki_, qd_=qd_: e.matmul(pa[0:64, 0:64], lhsT=ki_[:, :], rhs=qd_[:, :], start=True, stop=True),
                             r=[ki_.b, qd_.b], w=[pa.b])
                        S.op('dve', lambda e, pa=pa, at_=at_, d=d: e.tensor_tensor(out=at_[:], in0=pa[0:64, 0:64], in1=tri[:, d, :], op=ALU.mult),
                             r=[pa.b, tri.b], w=[at_.b])
                        po = self.nextpf()
                        S.op('pe', lambda e, po=po, at_=at_, v_=v_, k=k: e.matmul(po[0:64, 0:256], lhsT=at_[:, :], rhs=v_[:, k, :], start=True, stop=False),
                             r=[at_.b, v_.b], w=[po.b])
                        S.op('pe', lambda e, po=po, qd_=qd_: e.matmul(po[0:64, 0:256], lhsT=qd_[:, :], rhs=Sb[:, :], start=False, stop=True),
                             r=[qd_.b, Sb.b], w=[po.b])
                        if d == 0:
                            S.op('act', lambda e, po=po, o_=o_, k=k: e.copy(out=o_[:, k, :], in_=po[0:64, 0:256]), r=[po.b], w=[o_.b])
                        else:
                            S.op('dve', lambda e, po=po, o_=o_, ot_=ot_, k=k: e.tensor_tensor(out=ot_[:], in0=po[0:64, 0:256], in1=o_[:, k, :], op=ALU.add),
                                 r=[po.b, o_.b], w=[ot_.b])
                            S.op('pool', lambda e, ot_=ot_: e.tensor_tensor(out=jk[:], in0=ot_[:], in1=ot_[:], op=ALU.mult), r=[ot_.b], w=[jk.b])
                            S.op('dve', lambda e, ss_=ss_: e.reduce_sum(out=ss_[:], in_=jk[:], axis=AX.X), r=[jk.b], w=[ss_.b])
                            S.op('act', lambda e, ss_=ss_: e.activation(out=ss_[:], in_=ss_[:], func=AF.Ln, scale=1.0 / 256, bias=epst[0:64, :]), r=[ss_.b, epst.b], w=[ss_.b])
                            S.op('act', lambda e, ss_=ss_: e.activation(out=ss_[:], in_=ss_[:], func=AF.Exp, scale=-0.5), r=[ss_.b], w=[ss_.b])
                            S.op('dve', lambda e, ot_=ot_, ss_=ss_: e.scalar_tensor_tensor(out=ot_[:], in0=ot_[:], scalar=ss_[:], in1=ng[:], op0=ALU.mult, op1=ALU.mult),
                                 r=[ot_.b, ss_.b, ng.b], w=[ot_.b])
                            S.op('pool', lambda e, ot_=ot_, srg_=srg_, y_=y_, k=k: e.tensor_tensor(out=y_[:, k, :], in0=ot_[:], in1=srg_[:, k, :], op=ALU.mult),
                                 r=[ot_.b, srg_.b], w=[y_.b])
                        pt = self.nextpb()
                        S.op('pe', lambda e, pt=pt, ku_=ku_: e.transpose(out=pt[0:64, 0:128], in_=ku_[:, :], identity=self.identb[:]),
                             r=[ku_.b, self.identb.b], w=[pt.b])
                        S.op('act', lambda e, pt=pt, kut_=kut_: e.copy(out=kut_[:], in_=pt[0:64, 0:128]), r=[pt.b], w=[kut_.b])
                        pd = self.nextpf()
                        S.op('pe', lambda e, pd=pd, kut_=kut_, v_=v_, k=k: e.matmul(pd[:, 0:256], lhsT=kut_[:, :], rhs=v_[:, k, :], start=True, stop=True),
                             r=[kut_.b, v_.b], w=[pd.b])
                        S.op('dve', lambda e, pd=pd, ebl_=ebl_: e.scalar_tensor_tensor(out=St[:], in0=St[:], scalar=ebl_[:], in1=pd[:, 0:256], op0=ALU.mult, op1=ALU.add),
                             r=[St.b, ebl_.b, pd.b], w=[St.b])
                        S.op('act', lambda e: e.copy(out=Sb[:], in_=St[:]), r=[St.b], w=[Sb.b])
                    if d == 0:
                        S.dma('pool', self.scr['gla_o'][rows, h * 256:(h + 1) * 256].rearrange("(n p) c -> p n c", p=64), o_[:, 0:gn, :],
                              r=[o_.b], w=[self.db('gla_o', t) for t in tl])
                    else:
                        S.dma('pool', self.scr['y_tm'][rows, h * 256:(h + 1) * 256].rearrange("(n p) c -> p n c", p=64), y_[:, 0:gn, :],
                              r=[y_.b], w=[self.db('y_tm', t) for t in tl])

    def stage_mix_even(self, l):
        self.stage_gla(l)
        self.barrier()
        self._es.close()
        self.stage_begin()
        self.stage_swa(l)

    def stage_na(self, l):
        c = self.cfg
        T, C, TA, D, NH = c['T'], c['C'], c['TA'], c['D'], c['NH']
        S = self.s
        ntl, nlat = TA // 128, T // 128
        scale = 128.0 ** -0.5
        kT = [self.sb('na_k%d' % k, [128, TA], BF16) for k in range(2)]
        V = [self.sb('na_v%d' % k, [128, ntl, 128], BF16) for k in range(2)]
        qT = [self.sb('na_q%d' % k, [128, TA], BF16) for k in range(2)]
        bt = [self.sb('na_b%d' % k, [128, 5, 640], F32) for k in range(2)]
        ob = [self.sb('na_ob%d' % k, [128, 128], BF16) for k in range(2)]
        W = self.attn_work('na', C + 640, 128, nbuf=2)
        u = 0
        for h in range(NH):
            k_, v_, q_, b_ = kT[h % 2], V[h % 2], qT[h % 2], bt[h % 2]
            self.load_fm(k_, 'nk', h)
            self.load_tm(v_, 'nv', h * 128, 128)
            self.load_fm(q_, 'nq', h)
            S.dma('sp', b_[:], self.ins['na_bias'][:, h, :, :].rearrange("v q k -> q v k"), w=[b_.b])
            for tt in range(nlat):
                vi, lo = na_variant(c, 2 * tt)
                k0 = lo * 64
                parts = [(k_[:, T:TA], k_.b, None, None, [(v_[:, nlat + j, :], 128, v_.b) for j in range(C // 128)]),
                         (k_[:, k0:k0 + 640], k_.b, b_[:, vi, :], b_.b, [(v_[:, k0 // 128 + j, :], 128, v_.b) for j in range(5)])]
                w_ = W[u % 2]
                self.attn_unit(w_, q_[:, tt * 128:(tt + 1) * 128], q_.b, parts, scale, 128)
                self.store_y(w_['O'], ob[u % 2], tt, h * 128, 128)
                u += 1

    def stage_diff(self, l):
        c = self.cfg
        T, C, TA, D, DH = c['T'], c['C'], c['TA'], c['D'], c['DH']
        S = self.s
        I = self.ins
        ntl, nlat = TA // 128, T // 128
        scale = 128.0 ** -0.5
        lam_init = 0.8 - 0.6 * math.exp(-0.3 * l)
        kT = [self.sb('df_k%d' % k, [128, TA], BF16) for k in range(2)]
        qT = [self.sb('df_q%d' % k, [128, TA], BF16) for k in range(2)]
        V = self.sb('df_v', [128, ntl, 256], BF16)
        ng = self.sb('df_ng', [128, 256], F32)
        lv = [self.sb('df_l%d' % k, [128, 128], F32) for k in range(4)]
        lj = self.sb('df_lj', [128, 128], F32)
        la = [self.sb('df_la%d' % k, [128, 1], F32) for k in range(2)]
        nlam = self.sb('df_nlam', [128, 1], F32)
        od = self.sb('df_od', [128, 256], F32)
        jk = self.sb('df_jk', [128, 256], F32)
        ss = self.sb('df_ss', [128, 1], F32)
        ob = [self.sb('df_ob%d' % k, [128, 256], BF16) for k in range(2)]
        epst = self.sb('df_eps', [128, 1], F32)
        S.op('pool', lambda e: e.memset(epst[:], 1e-6), w=[epst.b])
        W = self.attn_work2('df', TA, 256, nbuf=2)
        S.dma('sp', ng[:], I['diff_norm_g'][0:1, :].partition_broadcast(128), w=[ng.b])
        for k, nm in enumerate(('diff_lq1', 'diff_lk1', 'diff_lq2', 'diff_lk2')):
            S.dma('sp', lv[k][:], I[nm][0:1, :].partition_broadcast(128), w=[lv[k].b])
        for k in range(2):
            S.op('dve', lambda e, k=k: e.tensor_tensor(out=lj[:], in0=lv[2 * k][:], in1=lv[2 * k + 1][:], op=ALU.mult),
                 r=[lv[2 * k].b, lv[2 * k + 1].b], w=[lj.b])
            S.op('dve', lambda e, k=k: e.reduce_sum(out=la[k][:], in_=lj[:], axis=AX.X), r=[lj.b], w=[la[k].b])
            S.op('act', lambda e, k=k: e.activation(out=la[k][:], in_=la[k][:], func=AF.Exp), r=[la[k].b], w=[la[k].b])
        S.op('dve', lambda e: e.tensor_tensor(out=nlam[:], in0=la[1][:], in1=la[0][:], op=ALU.subtract), r=[la[0].b, la[1].b], w=[nlam.b])
        S.op('dve', lambda e: e.tensor_scalar(out=nlam[:], in0=nlam[:], scalar1=-lam_init, scalar2=None, op0=ALU.add), r=[nlam.b], w=[nlam.b])
        u = 0
        for h in range(DH):
            self.load_tm(V, 'dv', h * 256, 256)
            for s_ in range(2):
                self.load_fm(kT[s_], 'dk', 2 * h + s_)
                self.load_fm(qT[s_], 'dq', 2 * h + s_)
            vlist = [(V[:, j, :], 128, V.b) for j in range(ntl)]
            for tt in range(nlat):
                for s_ in range(2):
                    self.attn_unit2(W[s_], qT[s_][:, tt * 128:(tt + 1) * 128], qT[s_].b, kT[s_][:, 0:TA], kT[s_].b, vlist, scale, 256)
                S.op('dve', lambda e: e.scalar_tensor_tensor(out=od[:], in0=W[1]['O'][:], scalar=nlam[:], in1=W[0]['O'][:], op0=ALU.mult, op1=ALU.add),
                     r=[W[0]['O'].b, W[1]['O'].b, nlam.b], w=[od.b])
                S.op('pool', lambda e: e.tensor_tensor(out=jk[:], in0=od[:], in1=od[:], op=ALU.mult), r=[od.b], w=[jk.b])
                S.op('dve', lambda e: e.reduce_sum(out=ss[:], in_=jk[:], axis=AX.X), r=[jk.b], w=[ss.b])
                S.op('act', lambda e: e.activation(out=ss[:], in_=ss[:], func=AF.Ln, scale=1.0 / 256, bias=epst[:]), r=[ss.b, epst.b], w=[ss.b])
                S.op('act', lambda e: e.activation(out=ss[:], in_=ss[:], func=AF.Exp, scale=-0.5), r=[ss.b], w=[ss.b])
                S.op('dve', lambda e: e.scalar_tensor_tensor(out=od[:], in0=od[:], scalar=ss[:], in1=ng[:], op0=ALU.mult, op1=ALU.mult),
                     r=[od.b, ss.b, ng.b], w=[od.b])
                o_ = ob[u % 2]
                u += 1
                S.op('act', lambda e, o_=o_: e.mul(out=o_[:], in_=od[:], mul=1.0 - lam_init), r=[od.b], w=[o_.b])
                S.dma('pool', self.scr['y_tm'][tt * 128:(tt + 1) * 128, D // 2 + h * 256:D // 2 + (h + 1) * 256], o_[:], r=[o_.b], w=[self.db('y_tm', tt)])

    def stage_mix_odd(self, l):
        self.stage_na(l)
        self.barrier()
        self._es.close()
        self.stage_begin()
        self.stage_diff(l)

    def resid_update(self, ps, n0, nw, tt, Gt, xp, tmp):
        S = self.s
        xr = self.scr['xres'][tt * 128:(tt + 1) * 128, n0:n0 + nw]
        S.dma('sp', xp[:, 0:nw], xr, r=[self.db('xres', tt)], w=[xp.b])
        S.op('dve', lambda e: e.tensor_tensor(out=tmp[:, 0:nw], in0=ps[:, 0:nw], in1=Gt[:, 0:nw], op=ALU.mult), r=[ps.b, Gt.b], w=[tmp.b])
        S.op('pool', lambda e: e.tensor_tensor(out=xp[:, 0:nw], in0=xp[:, 0:nw], in1=tmp[:, 0:nw], op=ALU.add), r=[xp.b, tmp.b], w=[xp.b])
        S.dma('pool', xr, xp[:, 0:nw], r=[xp.b], w=[self.db('xres', tt)])

    def blocks_of(self, tiles):
        nlat = self.cfg['T'] // 128
        out, cur = [], []
        for tt in tiles:
            if cur and (len(cur) == 4 or tt != cur[-1] + 1 or (tt == nlat)):
                out.append(cur)
                cur = []
            cur.append(tt)
        if cur:
            out.append(cur)
        return out

    def stage_wout(self, l, tiles):
        c = self.cfg
        D, T = c['D'], c['T']
        KC = D // 128
        S = self.s
        wbn = 'wb_out%d' % l
        wb = self.scr[wbn]
        yt = [self.sb('wo_y%d' % k, [128, D], BF16) for k in range(2)]
        yT = self.sb('wo_yT', [128, KC, 512], BF16)
        wt = [self.sb('wo_w%d' % k, [128, KC, 512], BF16) for k in range(2)]
        Gt = [self.sb('wo_G%d' % k, [128, 512], F32) for k in range(2)]
        xp = [self.sb('wo_x%d' % k, [128, 512], F32) for k in range(4)]
        tmp = [self.sb('wo_t%d' % k, [128, 512], F32) for k in range(4)]
        yi = wi = xi = 0
        for blk in self.blocks_of(tiles):
            row = 0 if blk[0] * 128 < T else 1
            for j, tt in enumerate(blk):
                y_ = yt[yi % 2]
                yi += 1
                S.dma('sp', y_[:], self.scr['y_tm'][tt * 128:(tt + 1) * 128, :], r=[self.db('y_tm', tt)], w=[y_.b])
                for g in range(0, KC, 8):
                    pb = self.nextpb()
                    for k in range(8):
                        S.op('pe', lambda e, k=k, g=g, y_=y_, pb=pb: e.transpose(out=pb[:, k * 128:(k + 1) * 128], in_=y_[:, (g + k) * 128:(g + k + 1) * 128],
                                                                              identity=self.identb[:]), r=[y_.b, self.identb.b], w=[pb.b])
                    for k in range(8):
                        eng = 'act' if k % 2 else 'dve'
                        S.op(eng, lambda e, k=k, g=g, j=j, pb=pb: (e.copy if e is self.nc.scalar else e.tensor_copy)(
                            out=yT[:, g + k, j * 128:(j + 1) * 128], in_=pb[:, k * 128:(k + 1) * 128]), r=[pb.b], w=[yT.b])
            for n0 in range(0, D, 512):
                w = wt[wi % 2]
                G_ = Gt[wi % 2]
                wi += 1
                S.dma('sp', w[:], wb[n0 // 512], r=[self.db(wbn, 0)], w=[w.b])
                S.dma('sp', G_[:], self.scr['modv'][l, 2, row:row + 1, n0:n0 + 512].partition_broadcast(128), r=[self.db('modv', l)], w=[G_.b])
                for j, tt in enumerate(blk):
                    ps = self.nextpf()
                    for kc in range(KC):
                        S.op('pe', lambda e, kc=kc, j=j, w=w, ps=ps: e.matmul(ps[:, :], lhsT=yT[:, kc, j * 128:(j + 1) * 128], rhs=w[:, kc, :],
                                                                             start=(kc == 0), stop=(kc == KC - 1)), r=[yT.b, w.b], w=[ps.b])
                    self.resid_update(ps, n0, 512, tt, G_, xp[xi % 4], tmp[xi % 4])
                    xi += 1

    def stage_ffn(self, l, tiles):
        c = self.cfg
        D, T, F = c['D'], c['T'], c['F']
        KC, FC = D // 128, F // 128
        S = self.s
        splits = self.ffn_split()
        nsplit = len(splits)
        FS = max(hi - lo for lo, hi in splits)
        gug = self.ffn_gugroups()
        dg = self.ffn_fgroups()
        hb = self.sb('ff_h', [128, KC, 512], BF16)
        aT = self.sb('ff_a', [128, FS, 512], BF16)
        wg = [self.sb('ff_g%d' % k, [128, KC, 256], BF16) for k in range(2)]
        wu = [self.sb('ff_u%d' % k, [128, KC, 256], BF16) for k in range(2)]
        wd = [self.sb('ff_d%d' % k, [128, 8, 512], BF16) for k in range(2)]
        sg = [self.sb('ff_s%d' % k, [128, 512], F32) for k in range(2)]
        Gt = [self.sb('ff_G%d' % k, [128, 512], F32) for k in range(2)]
        xp = [self.sb('ff_x%d' % k, [128, 512], F32) for k in range(4)]
        tmp = [self.sb('ff_t%d' % k, [128, 512], F32) for k in range(4)]
        wgs, wus, wds = self.scr['wb_g%d' % l], self.scr['wb_u%d' % l], self.scr['wb_d%d' % l]
        gi = di = si = xi = Gi = 0
        for blk in self.blocks_of(tiles):
            row = 0 if blk[0] * 128 < T else 1
            t0, n = blk[0] * 128, len(blk) * 128
            S.dma('sp', hb[:, :, 0:n], self.scr['hT'][blk[0] // 4, :, :, 0:n], r=[self.db('hT', blk[0] // 4)], w=[hb.b])
            for sp_ in range(nsplit):
                f_lo, f_hi = splits[sp_]
                for gidx, (gsp, fg, nf) in enumerate(gug):
                    if gsp != sp_:
                        continue
                    g_, u_ = wg[gi % 2], wu[gi % 2]
                    gi += 1
                    S.dma('sp', g_[:, :, 0:nf * 128], wgs[gidx, :, :, 0:nf * 128], r=[self.db('wb_g%d' % l, 0)], w=[g_.b])
                    S.dma('sp', u_[:, :, 0:nf * 128], wus[gidx, :, :, 0:nf * 128], r=[self.db('wb_u%d' % l, 0)], w=[u_.b])
                    for j in range(nf):
                        pg, pu = self.nextpf(), self.nextpf()
                        for kc in range(KC):
                            S.op('pe', lambda e, kc=kc, j=j, g_=g_, pg=pg: e.matmul(pg[:, 0:n], lhsT=g_[:, kc, j * 128:(j + 1) * 128], rhs=hb[:, kc, 0:n],
                                                                                   start=(kc == 0), stop=(kc == KC - 1)), r=[g_.b, hb.b], w=[pg.b])
                        for kc in range(KC):
                            S.op('pe', lambda e, kc=kc, j=j, u_=u_, pu=pu: e.matmul(pu[:, 0:n], lhsT=u_[:, kc, j * 128:(j + 1) * 128], rhs=hb[:, kc, 0:n],
                                                                                   start=(kc == 0), stop=(kc == KC - 1)), r=[u_.b, hb.b], w=[pu.b])
                        s_ = sg[si % 2]
                        si += 1
                        S.op('act', lambda e, s_=s_, pg=pg: e.activation(out=s_[:, 0:n], in_=pg[:, 0:n], func=AF.Silu), r=[pg.b], w=[s_.b])
                        S.op('dve', lambda e, s_=s_, pu=pu, fg=fg, j=j, f_lo=f_lo: e.tensor_tensor(out=aT[:, fg + j - f_lo, 0:n], in0=s_[:, 0:n], in1=pu[:, 0:n], op=ALU.mult),
                             r=[s_.b, pu.b], w=[aT.b])
                for n0 in range(0, D, 512):
                    G_ = Gt[Gi % 2]
                    Gi += 1
                    S.dma('sp', G_[:], self.scr['modv'][l, 5, row:row + 1, n0:n0 + 512].partition_broadcast(128), r=[self.db('modv', l)], w=[G_.b])
                    pss = [self.nextpf() for _ in blk]
                    for didx, (dsp, fg, nf) in enumerate(dg):
                        if dsp != sp_:
                            continue
                        d_ = wd[di % 2]
                        di += 1
                        S.dma('sp', d_[:, 0:nf, :], wds[didx, n0 // 512, :, 0:nf, :], r=[self.db('wb_d%d' % l, 0)], w=[d_.b])
                        for j in range(len(blk)):
                            for k in range(nf):
                                fc = fg + k
                                S.op('pe', lambda e, j=j, k=k, fc=fc, d_=d_: e.matmul(pss[j][:, :], lhsT=aT[:, fc - f_lo, j * 128:(j + 1) * 128], rhs=d_[:, k, :],
                                                                                     start=(fc == f_lo), stop=(fc == f_hi - 1)), r=[aT.b, d_.b], w=[pss[j].b])
                    for j, tt in enumerate(blk):
                        self.resid_update(pss[j], n0, 512, tt, G_, xp[xi % 4], tmp[xi % 4])
                        xi += 1

    def run_stage(self, fn, *a, **k):
        self.stage_begin()
        fn(*a, **k)
        self.stage_end()

    def build(self, upto='all'):
        c = self.cfg
        T, TA = c['T'], c['TA']
        self.declare()
        self.run_stage(self.stage_cast)
        self.run_stage(self.stage_init_x)
        self.run_stage(self.stage_mod)
        alltiles = list(range(TA // 128))
        lat = list(range(T // 128))
        if upto == 'mod':
            return self.finish()
        self.run_stage(self.stage_norm, 0, 0, alltiles)
        if upto == 'norm':
            return self.finish()
        for l in range(c['depth']):
            last = (l == c['depth'] - 1)
            self.run_stage(self.stage_proj, l)
            if upto == 'proj%d' % l:
                return self.finish()
            self.run_stage(self.stage_mix_even if l % 2 == 0 else self.stage_mix_odd, l)
            if upto == 'mix%d' % l:
                return self.finish()
            tiles = lat if last else alltiles
            self.run_stage(self.stage_wout, l, tiles)
            self.run_stage(self.stage_norm, l, 1, tiles)
            self.run_stage(self.stage_ffn, l, tiles)
            if upto == 'ffn%d' % l:
                return self.finish()
            if not last:
                self.run_stage(self.stage_norm, l + 1, 0, alltiles)
        self.run_stage(self.stage_norm, 0, 0, lat, final=True)
        return self.finish()

    def finish(self):
        bufs = [b for (nm, i), b in self.dbufs.items() if nm == 'out' or nm in self.debug]
        self.barrier()
        return self.nc


def host_consts(cfg):
    T = cfg['T']
    k = {}
    k['k_ident'] = np.eye(128, dtype=np.float32)
    pm = np.zeros((128, 128), np.float32)
    for m in range(128):
        h, i = divmod(m, 64)
        src = h * 64 + (i + 32) % 64
        pm[src, m] = 1.0
    k['k_perm'] = pm
    inv = (1.0 / (np.float32(10000.0) ** (np.arange(32, dtype=np.float32) / np.float32(32)))).astype(np.float32)
    pos = np.arange(T)
    row = (pos // 64).astype(np.float32)
    col = (pos % 64).astype(np.float32)
    cosT = np.zeros((128, T), np.float32)
    sinT = np.zeros((128, T), np.float32)
    for f in range(128):
        h, i = divmod(f, 64)
        j = i % 32
        ang = ((row if h == 0 else col) * inv[j]).astype(np.float32)
        cosT[f] = np.cos(ang).astype(np.float32)
        sg = -1.0 if i < 32 else 1.0
        sinT[f] = sg * np.sin(ang).astype(np.float32)
    k['k_cos'] = cosT
    k['k_sin'] = sinT
    qi = np.arange(128)[:, None]
    kj = np.arange(384)[None, :]
    k['k_swamask'] = np.where(np.abs(qi + 128 - kj) <= 128, 0.0, NEG).astype(np.float32)
    tri = np.zeros((2, 64, 64), np.float32)
    jj = np.arange(64)[:, None]
    ii = np.arange(64)[None, :]
    tri[0] = (ii >= jj)
    tri[1] = (ii <= jj)
    k['k_tri'] = tri
    return k


def na_bias_tables(cfg, rpb):
    T = cfg['T']
    rows = T // 64
    NH = rpb.shape[0]
    out = np.full((5, NH, 128, 640), NEG, np.float32)
    variants = [4, 0, 2, rows - 4, rows - 2]
    for vi, r0 in enumerate(variants):
        lo = min(max(r0 - 4, 0), rows - 10)
        for dq in range(2):
            r = r0 + dq
            rs = min(max(r - 4, 0), rows - 8)
            for cq in range(64):
                cst = min(max(cq - 8, 0), 64 - 16)
                q = dq * 64 + cq
                for kr in range(10):
                    ar = lo + kr
                    if not (rs <= ar < rs + 8):
                        continue
                    dr = ar - r + 7
                    kc = np.arange(cst, cst + 16)
                    dc = kc - cq + 15
                    out[vi, :, q, kr * 64 + kc] = rpb[:, dr, dc].T
    return out


def na_variant(cfg, r0):
    rows = cfg['T'] // 64
    if 4 <= r0 <= rows - 6:
        return 0, r0 - 4
    if r0 == 0:
        return 1, 0
    if r0 == 2:
        return 2, 0
    if r0 == rows - 4:
        return 3, rows - 10
    return 4, rows - 10


def make_in_maps(cfg, inputs, ncores):
    ks = host_consts(cfg)
    maps = []
    nab = na_bias_tables(cfg, np.asarray(inputs['na_rpb'])[0])
    for b in range(ncores):
        m = {}
        m['x'] = np.ascontiguousarray(inputs['x'][b])
        m['ctx'] = np.ascontiguousarray(inputs['ctx'][b])
        m['cvec'] = np.ascontiguousarray(np.stack([inputs['c'][b], inputs['c_ctx']]))
        for nm in ('ada_w', 'ada_b', 'norm_mix_g', 'norm_ffn_g', 'w_out', 'ffn_w_gate', 'ffn_w_up', 'ffn_w_down', 'ev_w_in',
                   'gla_gate_up_f', 'gla_gate_bias_f', 'gla_gate_up_b', 'gla_gate_bias_b', 'gla_norm_g', 'swa_sink', 'od_w_in',
                   'diff_lq1', 'diff_lk1', 'diff_lq2', 'diff_lk2', 'diff_norm_g', 'final_norm_g'):
            m[nm] = np.asarray(inputs[nm])
        m['na_bias'] = nab
        m.update(ks)
        maps.append(m)
    return maps


def kernel(**inputs):
    inputs = {k: np.asarray(v) for k, v in inputs.items()}
    B, T, D = inputs['x'].shape
    cfg = make_cfg(D=D, T=T, C=inputs['ctx'].shape[1], depth=inputs['ada_w'].shape[0])
    p = Prog(cfg)
    nc = p.build()
    maps = make_in_maps(cfg, inputs, B)
    res = run_bass_kernel_spmd(nc, maps, core_ids=list(range(B)))
    return np.stack([res.results[b]['out'] for b in range(B)]).astype(np.float32)
```

```python
import math
import numpy as np
import concourse.bass as bass
import concourse.mybir as mybir
from concourse.bass_utils import run_bass_kernel_spmd

F32 = mybir.dt.float32
BF16 = mybir.dt.bfloat16
AF = mybir.ActivationFunctionType
ALU = mybir.AluOpType
AX = mybir.AxisListType

NEG = -30000.0


def make_cfg(D=4096, T=8192, C=256, depth=2):
    cfg = dict(D=D, T=T, C=C, depth=depth, GRID_W=64, HD=128)
    half = D // 2
    cfg['GH'] = half // 256
    cfg['GQK'] = cfg['GH'] * 128
    cfg['SH'] = half // 128
    cfg['SKV'] = cfg['SH'] // 4
    cfg['NH'] = half // 128
    cfg['DH'] = half // 256
    cfg['F'] = -(-8 * D // (3 * 256)) * 256
    cfg['TA'] = T + C
    return cfg


class Buf:
    __slots__ = ('w', 'r', 'name')

    def __init__(self, name=''):
        self.w = None
        self.r = []
        self.name = name


class Sched:
    ENG = ('pe', 'act', 'dve', 'pool', 'sp')
    NS = 8

    def __init__(self, nc):
        self.nc = nc
        self.e = {'pe': nc.tensor, 'act': nc.scalar, 'dve': nc.vector, 'pool': nc.gpsimd, 'sp': nc.sync}
        self.sem = {k: nc.alloc_semaphore('sem_' + k) for k in self.ENG}
        self.cnt = {k: 0 for k in self.ENG}
        self.dsem = {}
        self.dtot = {}
        self.dn = {}
        for q in ('sp', 'pool', 'act'):
            self.dn[q] = 0
            for s in range(self.NS):
                self.dsem[(q, s)] = nc.alloc_semaphore('dsem_%s%d' % (q, s))
                self.dtot[(q, s)] = 0
        self.seen = {k: {} for k in self.ENG}
        self.ninstr = 0

    def _semh(self, key):
        return self.sem[key] if isinstance(key, str) else self.dsem[key]

    def _wait(self, eng, toks):
        need = {}
        for t in toks:
            if t is None:
                continue
            k, v = t
            if k == 'pe' and eng == 'pe':
                continue
            if need.get(k, 0) < v:
                need[k] = v
        seen = self.seen[eng]
        for k, v in need.items():
            if seen.get(k, 0) < v:
                self.e[eng].wait_ge(self._semh(k), v)
                seen[k] = v
                self.ninstr += 1

    def _deps(self, r, w):
        toks = []
        for b in r:
            toks.append(b.w)
        for b in w:
            toks.append(b.w)
            toks.extend(b.r)
        return toks

    def _commit(self, tok, r, w):
        for b in w:
            b.w = tok
            b.r = []
        for b in r:
            b.r.append(tok)
            if len(b.r) > 64:
                best = {}
                for k, v in b.r:
                    if best.get(k, 0) < v:
                        best[k] = v
                b.r = list(best.items())

    def op(self, eng, fn, r=(), w=()):
        self._wait(eng, self._deps(r, w))
        ins = fn(self.e[eng])
        ins.then_inc(self.sem[eng], 1)
        self.cnt[eng] += 1
        self.ninstr += 1
        self._commit((eng, self.cnt[eng]), r, w)

    def dma(self, q, out, in_, r=(), w=(), **kw):
        slot = self.dn[q] % self.NS
        self.dn[q] += 1
        key = (q, slot)
        toks = self._deps(r, w)
        if self.dtot[key] > 0:
            toks.append((key, self.dtot[key]))
        self._wait(q, toks)
        self.e[q].dma_start(out=out, in_=in_, **kw).then_inc(self.dsem[key], 16)
        self.dtot[key] += 16
        self.ninstr += 1
        self._commit((key, self.dtot[key]), r, w)

    def finish(self, bufs):
        toks = []
        for b in bufs:
            toks.append(b.w)
        self._wait('sp', toks)


class Tl:
    __slots__ = ('t', 'b')

    def __init__(self, t, name=''):
        self.t = t
        self.b = Buf(name)

    def __getitem__(self, k):
        return self.t[k]


class Builder:
    def __init__(self, cfg, debug=()):
        self.cfg = cfg
        self.debug = set(debug)
        self.nc = bass.Bass("TRN2", target_bir_lowering=False)
        self.s = Sched(self.nc)
        self.dbufs = {}
        self.ins = {}
        self.scr = {}
        self._stack = []

    def inp(self, name, shape, dt=F32):
        self.ins[name] = self.nc.dram_tensor(name, list(shape), dt, kind="ExternalInput").ap()
        return self.ins[name]

    def scratch(self, name, shape, dt):
        kind = "ExternalOutput" if name in self.debug else "Internal"
        self.scr[name] = self.nc.dram_tensor(name, list(shape), dt, kind=kind).ap()
        return self.scr[name]

    def db(self, name, idx=0):
        k = (name, idx)
        if k not in self.dbufs:
            self.dbufs[k] = Buf(str(k))
        return self.dbufs[k]

    def dbs(self, name, lo, hi):
        return [self.db(name, i) for i in range(lo, hi)]


def even_layout(cfg):
    GQK, GH, SH, SKV = cfg['GQK'], cfg['GH'], cfg['SH'], cfg['SKV']
    o = 0
    L = {}
    for nm, w in (('gq', GQK), ('gk', GQK), ('gv', GH * 256), ('gr', GH * 256), ('dnf', 16), ('dnb', 16),
                  ('sq', SH * 128), ('sk', SKV * 128), ('sv', SKV * 128)):
        L[nm] = (o, w)
        o += w
    L['_n'] = o
    return L


def odd_layout(cfg):
    NH, DH = cfg['NH'], cfg['DH']
    o = 0
    L = {}
    for nm, w in (('nq', NH * 128), ('nk', NH * 128), ('nv', NH * 128), ('dq', DH * 256), ('dk', DH * 256),
                  ('dv', DH * 256)):
        L[nm] = (o, w)
        o += w
    L['_n'] = o
    return L


FM_EVEN = ('gq', 'gk', 'dnf', 'dnb', 'sq', 'sk')
TM_EVEN = ('gv', 'gr', 'sv')
ROPE_EVEN = ('sq', 'sk')
FM_ODD = ('nq', 'nk', 'dq', 'dk')
TM_ODD = ('nv', 'dv')
ROPE_ODD = ('dq', 'dk')


class Prog(Builder):
    def declare(self):
        c = self.cfg
        D, T, C, F, dep = c['D'], c['T'], c['C'], c['F'], c['depth']
        EL, OL = even_layout(c), odd_layout(c)
        self.EL, self.OL = EL, OL
        i = self.inp
        i('x', [T, D]); i('ctx', [C, D]); i('cvec', [2, D])
        i('ada_w', [dep, D, 6 * D]); i('ada_b', [dep, 6 * D])
        i('norm_mix_g', [dep, D]); i('norm_ffn_g', [dep, D])
        i('w_out', [dep, D, D]); i('ffn_w_gate', [dep, D, F]); i('ffn_w_up', [dep, D, F]); i('ffn_w_down', [dep, F, D])
        i('ev_w_in', [1, D, EL['_n']]); i('gla_gate_up_f', [1, 16, c['GQK']]); i('gla_gate_bias_f', [1, c['GQK']])
        i('gla_gate_up_b', [1, 16, c['GQK']]); i('gla_gate_bias_b', [1, c['GQK']])
        i('gla_norm_g', [1, 256]); i('swa_sink', [1, c['SH']])
        i('od_w_in', [1, D, OL['_n']]); i('na_bias', [5, c['NH'], 128, 640])
        i('diff_lq1', [1, 128]); i('diff_lk1', [1, 128]); i('diff_lq2', [1, 128]); i('diff_lk2', [1, 128])
        i('diff_norm_g', [1, 256]); i('final_norm_g', [D])
        i('k_ident', [128, 128]); i('k_perm', [128, 128]); i('k_cos', [128, T]); i('k_sin', [128, T])
        i('k_swamask', [128, 384]); i('k_tri', [2, 64, 64])
        self.out = self.nc.dram_tensor('out', [T, D], F32, kind="ExternalOutput").ap()
        s = self.scratch
        TA = c['TA']
        s('xres', [TA, D], F32)
        s('hT', [-(-TA // 512), 128, D // 128, 512], BF16)
        s('y_tm', [TA, D], BF16)
        s('mod', [dep, 2, 6 * D], F32)
        s('modv', [dep, 6, 2, D], F32)
        self.pieces = {}
        for l_, (L_, fmn_, tmn_) in enumerate(((EL, FM_EVEN, TM_EVEN), (OL, FM_ODD, TM_ODD))):
            pcs = []
            for kind_, names_ in (('fm', fmn_), ('tm', tmn_)):
                for nm in names_:
                    for p0 in range(0, L_[nm][1], 512):
                        pcs.append((kind_, nm, p0, min(512, L_[nm][1] - p0)))
            self.pieces[l_] = pcs
            s('wb_in%d' % l_, [len(pcs), 128, D // 128, 512], BF16)
        for l in range(dep):
            s('wb_out%d' % l, [D // 512, 128, D // 128, 512], BF16)
            s('wb_g%d' % l, [len(self.ffn_gugroups()), 128, D // 128, 256], BF16); s('wb_u%d' % l, [len(self.ffn_gugroups()), 128, D // 128, 256], BF16)
            s('wb_d%d' % l, [len(self.ffn_fgroups()), D // 512, 128, 8, 512], BF16)
        nfm = max(sum(-(-EL[n][1] // 128) for n in FM_EVEN), sum(-(-OL[n][1] // 128) for n in FM_ODD))
        ntm = max(sum(EL[n][1] for n in TM_EVEN), sum(OL[n][1] for n in TM_ODD))
        s('pT', [nfm, 128, TA], BF16)
        s('pV', [TA, ntm], BF16)
        s('gla_o', [TA, c['GH'] * 256], F32)
        self.pf = [Tl(self.nc.alloc_psum_tensor('pf%d' % k, [128, 512], F32), 'pf%d' % k) for k in range(6)]
        self.pb = [Tl(self.nc.alloc_psum_tensor('pb%d' % k, [128, 1024], BF16), 'pb%d' % k) for k in range(2)]
        self.pfi = 0
        self.pbi = 0
        self.ident = self.sb('ident', [128, 128], F32, persist=True)
        self.identb = self.sb('identb', [128, 128], BF16, persist=True)
        self.s.dma('sp', self.ident[:], self.ins['k_ident'][:, :], w=[self.ident.b])
        self.s.op('dve', lambda e: e.tensor_copy(out=self.identb[:], in_=self.ident[:]), r=[self.ident.b], w=[self.identb.b])

    def sb(self, name, shape, dt, persist=False):
        self._uid = getattr(self, '_uid', 0) + 1
        nm = '%s_%d' % (name, self._uid)
        if persist:
            return Tl(self.nc.alloc_sbuf_tensor(nm, list(shape), dt), nm)
        return Tl(self._es.enter_context(self.nc.sbuf_tensor(nm, list(shape), dt)), nm)

    def stage_begin(self):
        import contextlib
        self._es = contextlib.ExitStack()

    def stage_end(self):
        self.barrier()
        self._es.close()

    def barrier(self):
        S = self.s
        toks = [(k, S.cnt[k]) for k in S.ENG if S.cnt[k] > 0]
        toks += [(k, v) for k, v in S.dtot.items() if v > 0]
        for e in S.ENG:
            S._wait(e, toks)

    def nextpf(self):
        t = self.pf[self.pfi % len(self.pf)]
        self.pfi += 1
        return t

    def nextpb(self):
        t = self.pb[self.pbi % len(self.pb)]
        self.pbi += 1
        return t

    def ffn_split(self):
        FC = self.cfg['F'] // 128
        nsplit = 2 if FC > 48 else 1
        FS = -(-FC // nsplit)
        return [(sp * FS, min(FC, (sp + 1) * FS)) for sp in range(nsplit)]

    def ffn_fgroups(self):
        out = []
        for sp, (lo, hi) in enumerate(self.ffn_split()):
            for fg in range(lo, hi, 8):
                out.append((sp, fg, min(8, hi - fg)))
        return out

    def ffn_gugroups(self):
        out = []
        for sp, (lo, hi) in enumerate(self.ffn_split()):
            for fg in range(lo, hi, 2):
                out.append((sp, fg, min(2, hi - fg)))
        return out

    def cast_blk(self, dst, src, dname):
        self.s.dma('pool', dst, src.rearrange("(c p) n -> p c n", p=128), w=[self.db(dname, 0)])

    def stage_cast(self):
        I = self.ins
        c = self.cfg
        D, F = c['D'], c['F']
        for l_, (wn, L_) in enumerate((('ev_w_in', self.EL), ('od_w_in', self.OL))):
            for i, (kind, nm, p0, pw) in enumerate(self.pieces[l_]):
                c0 = L_[nm][0] + p0
                self.cast_blk(self.scr['wb_in%d' % l_][i, :, :, 0:pw], I[wn][0][:, c0:c0 + pw], 'wb_in%d' % l_)
        for l in range(c['depth']):
            for g in range(D // 512):
                self.cast_blk(self.scr['wb_out%d' % l][g], I['w_out'][l][:, g * 512:(g + 1) * 512], 'wb_out%d' % l)
            for g, (sp, fg, nf) in enumerate(self.ffn_gugroups()):
                self.cast_blk(self.scr['wb_g%d' % l][g, :, :, 0:nf * 128], I['ffn_w_gate'][l][:, fg * 128:(fg + nf) * 128], 'wb_g%d' % l)
                self.cast_blk(self.scr['wb_u%d' % l][g, :, :, 0:nf * 128], I['ffn_w_up'][l][:, fg * 128:(fg + nf) * 128], 'wb_u%d' % l)
            for gi, (sp, fg, nf) in enumerate(self.ffn_fgroups()):
                for ng in range(D // 512):
                    self.cast_blk(self.scr['wb_d%d' % l][gi, ng, :, 0:nf, :], I['ffn_w_down'][l][fg * 128:(fg + nf) * 128, ng * 512:(ng + 1) * 512],
                                  'wb_d%d' % l)

    def stage_init_x(self):
        T, C = self.cfg['T'], self.cfg['C']
        for t0 in range(0, T, 512):
            self.s.dma('sp', self.scr['xres'][t0:t0 + 512, :], self.ins['x'][t0:t0 + 512, :],
                       w=self.dbs('xres', t0 // 128, t0 // 128 + 4))
        self.s.dma('sp', self.scr['xres'][T:T + C, :], self.ins['ctx'][:, :], w=self.dbs('xres', T // 128, (T + C) // 128))

    def stage_mod(self):
        c = self.cfg
        D, dep = c['D'], c['depth']
        KC = D // 128
        S = self.s
        nc = self.nc
        cv = self.sb('mod_cv', [2, D], F32)
        scT = self.sb('mod_scT', [128, KC, 2], F32)
        S.dma('sp', cv[:], self.ins['cvec'][:, :], w=[cv.b])
        S.op('act', lambda e: e.activation(out=cv[:], in_=cv[:], func=AF.Silu), r=[cv.b], w=[cv.b])
        for g in range(0, KC, 64):
            n = min(64, KC - g)
            ps = self.nextpf()
            for k in range(n):
                S.op('pe', lambda e, k=k: e.transpose(out=ps[:, 2 * k:2 * k + 2], in_=cv[0:2, (g + k) * 128:(g + k + 1) * 128],
                                                      identity=self.ident[0:2, 0:2]), r=[cv.b, self.ident.b], w=[ps.b])
            S.op('dve', lambda e: e.tensor_copy(out=scT[:, g:g + n, :].rearrange("p a b -> p (a b)"), in_=ps[:, 0:2 * n]),
                 r=[ps.b], w=[scT.b])
        wt = [self.sb('mod_w%d' % k, [128, 2048], F32) for k in range(3)]
        res = self.sb('mod_res', [2, 2048], F32)
        bia = self.sb('mod_bias', [2, 2048], F32)
        wi = 0
        for l in range(dep):
            for n0 in range(0, 6 * D, 2048):
                pss = [self.nextpf() for _ in range(4)]
                for kc in range(KC):
                    w = wt[wi % 3]
                    wi += 1
                    S.dma('sp', w[:], self.ins['ada_w'][l, kc * 128:(kc + 1) * 128, n0:n0 + 2048], w=[w.b])
                    for j in range(4):
                        S.op('pe', lambda e, j=j, w=w, kc=kc: e.matmul(pss[j][0:2, :], lhsT=scT[:, kc, :], rhs=w[:, j * 512:(j + 1) * 512],
                                                                      start=(kc == 0), stop=(kc == KC - 1)),
                             r=[scT.b, w.b], w=[pss[j].b])
                for r_ in range(2):
                    S.dma('sp', bia[r_:r_ + 1, :], self.ins['ada_b'][l:l + 1, n0:n0 + 2048], w=[bia.b])
                for j in range(4):
                    S.op('dve', lambda e, j=j: e.tensor_tensor(out=res[:, j * 512:(j + 1) * 512], in0=pss[j][0:2, :],
                                                               in1=bia[:, j * 512:(j + 1) * 512], op=ALU.add),
                         r=[pss[j].b, bia.b], w=[res.b])
                S.dma('sp', self.scr['mod'][l, :, n0:n0 + 2048], res[:], r=[res.b], w=[self.db('mod', l)])
        self.barrier()
        self._es.close()
        self.stage_begin()
        sc_ = self.sb('mod_sc', [2, D], F32)
        g = self.sb('mod_g', [2, D], F32)
        for l in range(dep):
            for k, (gn, sci, shi, gti) in enumerate((('norm_mix_g', 1, 0, 2), ('norm_ffn_g', 4, 3, 5))):
                S.dma('sp', sc_[:], self.scr['mod'][l, :, sci * D:(sci + 1) * D], r=[self.db('mod', l)], w=[sc_.b])
                for r_ in range(2):
                    S.dma('sp', g[r_:r_ + 1, :], self.ins[gn][l:l + 1, :], w=[g.b])
                S.op('dve', lambda e: e.scalar_tensor_tensor(out=sc_[:], in0=sc_[:], scalar=1.0, in1=g[:], op0=ALU.add, op1=ALU.mult),
                     r=[sc_.b, g.b], w=[sc_.b])
                S.dma('sp', self.scr['modv'][l, 3 * k], sc_[:], r=[sc_.b], w=[self.db('modv', l)])
                S.dma('sp', self.scr['modv'][l, 3 * k + 1], self.scr['mod'][l, :, shi * D:(shi + 1) * D], r=[self.db('mod', l)], w=[self.db('modv', l)])
                S.dma('sp', self.scr['modv'][l, 3 * k + 2], self.scr['mod'][l, :, gti * D:(gti + 1) * D], r=[self.db('mod', l)], w=[self.db('modv', l)])

    def load_bcast(self, tile, l, which, row):
        src = self.scr['modv'][l, which, row:row + 1, :].partition_broadcast(128)
        self.s.dma('sp', tile[:], src, r=[self.db('modv', l)], w=[tile.b])

    def stage_norm(self, l, which, tiles, final=False):
        c = self.cfg
        D, T = c['D'], c['T']
        KC = D // 128
        S = self.s
        tag = 'n%d%d%d' % (l, which, int(final))
        A1 = self.sb(tag + 'A', [128, D], F32)
        S1 = self.sb(tag + 'S', [128, D], F32)
        A = [A1, A1]
        Sh = [S1, S1]
        currow = -1
        if final:
            S.dma('sp', A[0][:], self.ins['final_norm_g'][None, :].partition_broadcast(128), w=[A[0].b])
        xt = [self.sb(tag + 'x%d' % k, [128, D], F32) for k in range(2)]
        junk = self.sb(tag + 'j', [128, D], BF16)
        hb = [self.sb(tag + 'h%d' % k, [128, D], BF16) for k in range(2)]
        hf = [self.sb(tag + 'f%d' % k, [128, D], F32) for k in range(2)] if final else None
        ht = [self.sb(tag + 't%d' % k, [128, KC, 512], BF16) for k in range(2)]
        ss = [self.sb(tag + 's%d' % k, [128, 1], F32) for k in range(2)]
        rs = [self.sb(tag + 'r%d' % k, [128, 1], F32) for k in range(2)]
        for n, tt in enumerate(tiles):
            row = 0 if tt * 128 < T else 1
            if not final and row != currow:
                currow = row
                self.load_bcast(A1, l, 3 * which, row)
                self.load_bcast(S1, l, 3 * which + 1, row)
            x, h, t_, s_, r_ = xt[n % 2], hb[n % 2], ht[(tt // 4) % 2], ss[n % 2], rs[n % 2]
            tj = tt % 4
            S.dma('sp', x[:], self.scr['xres'][tt * 128:(tt + 1) * 128, :], r=[self.db('xres', tt)], w=[x.b])
            S.op('act', lambda e, x=x, s_=s_: e.activation(out=junk[:], in_=x[:], func=AF.Square, accum_out=s_[:]),
                 r=[x.b], w=[junk.b, s_.b])
            S.op('act', lambda e, s_=s_, r_=r_: e.activation(out=r_[:], in_=s_[:], func=AF.Sqrt, scale=1.0 / D, bias=1e-6),
                 r=[s_.b], w=[r_.b])
            S.op('dve', lambda e, r_=r_: e.reciprocal(out=r_[:], in_=r_[:]), r=[r_.b], w=[r_.b])
            if final:
                f = hf[n % 2]
                S.op('dve', lambda e, x=x, r_=r_, f=f: e.scalar_tensor_tensor(out=f[:], in0=x[:], scalar=r_[:], in1=A[0][:],
                                                                              op0=ALU.mult, op1=ALU.mult), r=[x.b, r_.b, A[0].b], w=[f.b])
                S.dma('pool', self.out[tt * 128:(tt + 1) * 128, :], f[:], r=[f.b], w=[self.db('out', tt)])
                continue
            S.op('dve', lambda e, x=x, r_=r_, row=row: e.scalar_tensor_tensor(out=x[:], in0=x[:], scalar=r_[:], in1=A[row][:],
                                                                              op0=ALU.mult, op1=ALU.mult), r=[x.b, r_.b, A[row].b], w=[x.b])
            S.op('pool', lambda e, x=x, h=h, row=row: e.tensor_tensor(out=h[:], in0=x[:], in1=Sh[row][:], op=ALU.add),
                 r=[x.b, Sh[row].b], w=[h.b])
            for g in range(0, KC, 8):
                ps = self.nextpb()
                for k in range(8):
                    S.op('pe', lambda e, k=k, g=g, h=h, ps=ps: e.transpose(out=ps[:, k * 128:(k + 1) * 128], in_=h[:, (g + k) * 128:(g + k + 1) * 128],
                                                                          identity=self.identb[:]), r=[h.b, self.identb.b], w=[ps.b])
                for k in range(8):
                    S.op('act' if k % 2 else 'dve',
                         lambda e, g=g, k=k, ps=ps, t_=t_, tj=tj: (e.copy if e is self.nc.scalar else e.tensor_copy)(
                             out=t_[:, g + k, tj * 128:(tj + 1) * 128], in_=ps[:, k * 128:(k + 1) * 128]), r=[ps.b], w=[t_.b])
            if tj == 3 or n == len(tiles) - 1 or tiles[n + 1] // 4 != tt // 4:
                nn = (tj + 1) * 128
                S.dma('pool', self.scr['hT'][tt // 4, :, :, 0:nn], t_[:, :, 0:nn], r=[t_.b], w=[self.db('hT', tt // 4)])

    def tok_blocks(self, upto=None):
        c = self.cfg
        TA = c['TA'] if upto is None else upto
        return [(t0, min(512, TA - t0)) for t0 in range(0, TA, 512)]

    def stage_proj(self, l):
        c = self.cfg
        D, T = c['D'], c['T']
        KC = D // 128
        S = self.s
        even = (l % 2 == 0)
        L = self.EL if even else self.OL
        wbn = 'wb_in0' if even else 'wb_in1'
        wb = self.scr[wbn]
        fmn, tmn, ropen = (FM_EVEN, TM_EVEN, ROPE_EVEN) if even else (FM_ODD, TM_ODD, ROPE_ODD)
        self.fmidx, self.tmcol = {}, {}
        ci = 0
        for nm in fmn:
            for j in range(-(-L[nm][1] // 128)):
                self.fmidx[(nm, j)] = ci
                ci += 1
        co = 0
        for nm in tmn:
            self.tmcol[nm] = co
            co += L[nm][1]
        pieces = [(i,) + pc for i, pc in enumerate(self.pieces[l % 2])]
        hblk = [self.sb('pj_h%d' % k, [128, KC, 512], BF16) for k in range(2)]
        wt = [self.sb('pj_w%d' % k, [128, KC, 512], BF16) for k in range(2)]
        perm = self.sb('pj_perm', [128, 128], F32)
        cs = self.sb('pj_cos', [128, 512], F32)
        sn = self.sb('pj_sin', [128, 512], F32)
        q32 = [self.sb('pj_q%d' % k, [128, 512], F32) for k in range(2)]
        ta = [self.sb('pj_ta%d' % k, [128, 512], F32) for k in range(2)]
        tb_ = [self.sb('pj_tb%d' % k, [128, 512], F32) for k in range(2)]
        ob = [self.sb('pj_o%d' % k, [128, 512], BF16) for k in range(3)]
        S.dma('sp', perm[:], self.ins['k_perm'][:, :], w=[perm.b])
        wi = 0
        oi = 0
        qi = 0
        for bi, (t0, n) in enumerate(self.tok_blocks()):
            hb = hblk[bi % 2]
            latent = t0 < T
            S.dma('sp', hb[:, :, 0:n], self.scr['hT'][t0 // 512, :, :, 0:n], r=[self.db('hT', t0 // 512)], w=[hb.b])
            if latent:
                S.dma('sp', cs[:, 0:n], self.ins['k_cos'][:, t0:t0 + n], w=[cs.b])
                S.dma('sp', sn[:, 0:n], self.ins['k_sin'][:, t0:t0 + n], w=[sn.b])
            for pi, kind, nm, p0, pw in pieces:
                w = wt[wi % 2]
                wi += 1
                c0 = L[nm][0] + p0
                S.dma('sp', w[:, :, 0:pw], wb[pi, :, :, 0:pw], r=[self.db(wbn, 0)], w=[w.b])
                if kind == 'fm':
                    for j in range(-(-pw // 128)):
                        cw = min(128, pw - j * 128)
                        ps = self.nextpf()
                        for kc in range(KC):
                            S.op('pe', lambda e, kc=kc, j=j, cw=cw, w=w, ps=ps: e.matmul(
                                ps[0:cw, 0:n], lhsT=w[:, kc, j * 128:j * 128 + cw], rhs=hb[:, kc, 0:n], start=(kc == 0), stop=(kc == KC - 1)),
                                r=[w.b, hb.b], w=[ps.b])
                        o = ob[oi % 3]
                        oi += 1
                        if nm in ropen and latent:
                            q = q32[qi % 2]; a_ = ta[qi % 2]; b_ = tb_[qi % 2]
                            qi += 1
                            S.op('act', lambda e, q=q, ps=ps: e.copy(out=q[:, 0:n], in_=ps[:, 0:n]), r=[ps.b], w=[q.b])
                            ps2 = self.nextpf()
                            S.op('pe', lambda e, q=q, ps2=ps2: e.matmul(ps2[:, 0:n], lhsT=perm[:, :], rhs=q[:, 0:n], start=True, stop=True),
                                 r=[perm.b, q.b], w=[ps2.b])
                            S.op('pool', lambda e, q=q, a_=a_: e.tensor_tensor(out=a_[:, 0:n], in0=q[:, 0:n], in1=cs[:, 0:n], op=ALU.mult),
                                 r=[q.b, cs.b], w=[a_.b])
                            S.op('dve', lambda e, ps2=ps2, b_=b_: e.tensor_tensor(out=b_[:, 0:n], in0=ps2[:, 0:n], in1=sn[:, 0:n], op=ALU.mult),
                                 r=[ps2.b, sn.b], w=[b_.b])
                            S.op('dve', lambda e, a_=a_, b_=b_, o=o: e.tensor_tensor(out=o[:, 0:n], in0=a_[:, 0:n], in1=b_[:, 0:n], op=ALU.add),
                                 r=[a_.b, b_.b], w=[o.b])
                        else:
                            if oi % 2:
                                S.op('act', lambda e, o=o, ps=ps, cw=cw: e.copy(out=o[0:cw, 0:n], in_=ps[0:cw, 0:n]), r=[ps.b], w=[o.b])
                            else:
                                S.op('dve', lambda e, o=o, ps=ps, cw=cw: e.tensor_copy(out=o[0:cw, 0:n], in_=ps[0:cw, 0:n]), r=[ps.b], w=[o.b])
                        ch = self.fmidx[(nm, (p0 // 128) + j)]
                        S.dma('pool', self.scr['pT'][ch, 0:cw, t0:t0 + n], o[0:cw, 0:n], r=[o.b], w=[self.db('pT', ch)])
                else:
                    for j in range(n // 128):
                        ps = self.nextpf()
                        for kc in range(KC):
                            S.op('pe', lambda e, kc=kc, j=j, w=w, ps=ps: e.matmul(
                                ps[:, 0:pw], lhsT=hb[:, kc, j * 128:(j + 1) * 128], rhs=w[:, kc, 0:pw], start=(kc == 0), stop=(kc == KC - 1)),
                                r=[w.b, hb.b], w=[ps.b])
                        o = ob[oi % 3]
                        oi += 1
                        if oi % 2:
                            S.op('act', lambda e, o=o, ps=ps: e.copy(out=o[:, 0:pw], in_=ps[:, 0:pw]), r=[ps.b], w=[o.b])
                        else:
                            S.op('dve', lambda e, o=o, ps=ps: e.tensor_copy(out=o[:, 0:pw], in_=ps[:, 0:pw]), r=[ps.b], w=[o.b])
                        cc = self.tmcol[nm] + p0
                        S.dma('pool', self.scr['pV'][t0 + j * 128:t0 + (j + 1) * 128, cc:cc + pw], o[:, 0:pw], r=[o.b],
                              w=[self.db('pV', (t0 // 128) + j)])

    def attn_work(self, tag, nkmax, dv, nbuf=1):
        nblk = -(-nkmax // 128)
        W = []
        for k in range(nbuf):
            W.append(dict(
                S=self.sb(tag + 'S%d' % k, [128, nkmax + 1], F32), P=self.sb(tag + 'P%d' % k, [128, nkmax + 1], BF16),
                PT=self.sb(tag + 'PT%d' % k, [128, nblk, 128], BF16), m=self.sb(tag + 'm%d' % k, [128, 1], F32),
                ss=self.sb(tag + 'ss%d' % k, [128, 1], F32), O=self.sb(tag + 'O%d' % k, [128, dv], F32)))
        return W

    def attn_unit(self, W, qT, qb, parts, scale, dv, sink=None):
        S = self.s
        Ssb, P, PT, m, ss, O = W['S'], W['P'], W['PT'], W['m'], W['ss'], W['O']
        off = 0
        ei = 0
        for (kT, kbuf, bias, bbuf, vlist) in parts:
            n_all = kT.shape[1]
            for c0 in range(0, n_all, 512):
                n = min(512, n_all - c0)
                ps = self.nextpf()
                S.op('pe', lambda e, ps=ps, c0=c0, n=n, kT=kT: e.matmul(ps[:, 0:n], lhsT=qT, rhs=kT[:, c0:c0 + n], start=True, stop=True),
                     r=[qb, kbuf], w=[ps.b])
                if bias is not None:
                    S.op('dve', lambda e, ps=ps, c0=c0, n=n, off=off, bias=bias: e.scalar_tensor_tensor(
                        out=Ssb[:, off:off + n], in0=ps[:, 0:n], scalar=scale, in1=bias[:, c0:c0 + n], op0=ALU.mult, op1=ALU.add),
                        r=[ps.b, bbuf], w=[Ssb.b])
                elif ei % 2 == 0:
                    S.op('act', lambda e, ps=ps, n=n, off=off: e.mul(out=Ssb[:, off:off + n], in_=ps[:, 0:n], mul=scale), r=[ps.b], w=[Ssb.b])
                else:
                    S.op('dve', lambda e, ps=ps, n=n, off=off: e.tensor_scalar(out=Ssb[:, off:off + n], in0=ps[:, 0:n], scalar1=scale,
                                                                             scalar2=None, op0=ALU.mult), r=[ps.b], w=[Ssb.b])
                ei += 1
                off += n
        nk = off
        ncol = nk
        if sink is not None:
            S.op('dve', lambda e: e.tensor_copy(out=Ssb[:, nk:nk + 1], in_=sink[0][:, 0:1]), r=[sink[1]], w=[Ssb.b])
            ncol = nk + 1
        S.op('dve', lambda e: e.reduce_max(out=m[:], in_=Ssb[:, 0:ncol], axis=AX.X), r=[Ssb.b], w=[m.b])
        S.op('dve', lambda e: e.tensor_scalar(out=m[:], in0=m[:], scalar1=-1.0, scalar2=None, op0=ALU.mult), r=[m.b], w=[m.b])
        S.op('act', lambda e: e.activation(out=P[:, 0:ncol], in_=Ssb[:, 0:ncol], func=AF.Exp, bias=m[:], scale=1.0, accum_out=ss[:]),
             r=[Ssb.b, m.b], w=[P.b, ss.b])
        S.op('dve', lambda e: e.reciprocal(out=ss[:], in_=ss[:]), r=[ss.b], w=[ss.b])
        vall = []
        off = 0
        for (kT, kbuf, bias, bbuf, vlist) in parts:
            for (vap, kn, vbuf) in vlist:
                vall.append((vap, kn, vbuf, off))
                off += kn
        for g in range(0, len(vall), 8):
            grp = vall[g:g + 8]
            pb = self.nextpb()
            for k, (vap, kn, vbuf, o_) in enumerate(grp):
                S.op('pe', lambda e, k=k, kn=kn, o_=o_, pb=pb: e.transpose(out=pb[0:kn, k * 128:(k + 1) * 128], in_=P[:, o_:o_ + kn],
                                                                          identity=self.identb[:]), r=[P.b, self.identb.b], w=[pb.b])
            eng = 'act' if (g // 8) % 2 else 'dve'
            S.op(eng, lambda e, g=g, pb=pb, ng=len(grp): (e.copy if e is self.nc.scalar else e.tensor_copy)(
                out=PT[:, g:g + ng, :].rearrange("p a b -> p (a b)"), in_=pb[:, 0:ng * 128]), r=[pb.b], w=[PT.b])
        po = self.nextpf()
        for k, (vap, kn, vbuf, o_) in enumerate(vall):
            S.op('pe', lambda e, k=k, kn=kn, vap=vap: e.matmul(po[:, 0:dv], lhsT=PT[0:kn, k, :], rhs=vap, start=(k == 0), stop=(k == len(vall) - 1)),
                 r=[PT.b, vbuf], w=[po.b])
        S.op('act', lambda e: e.activation(out=O[:, 0:dv], in_=po[:, 0:dv], func=AF.Identity, scale=ss[:]), r=[po.b, ss.b], w=[O.b])

    def attn_work2(self, tag, nkmax, dv, nbuf=2):
        nblk = -(-nkmax // 128)
        nch = -(-nkmax // 512)
        W = []
        for k in range(nbuf):
            W.append(dict(
                P=self.sb(tag + 'P%d' % k, [128, nkmax], BF16), PT=self.sb(tag + 'PT%d' % k, [128, nblk, 128], BF16),
                mc=self.sb(tag + 'mc%d' % k, [128, nch], F32), sc=self.sb(tag + 'sc%d' % k, [128, nch], F32),
                m=self.sb(tag + 'm%d' % k, [128, 1], F32), ss=self.sb(tag + 'ss%d' % k, [128, 1], F32),
                O=self.sb(tag + 'O%d' % k, [128, dv], F32)))
        return W

    def attn_unit2(self, W, qT, qb, kT, kbuf, vlist, scale, dv):
        S = self.s
        P, PT, mc, sc, m, ss, O = W['P'], W['PT'], W['mc'], W['sc'], W['m'], W['ss'], W['O']
        nk = kT.shape[1]
        chunks = [(c0, min(512, nk - c0)) for c0 in range(0, nk, 512)]
        nch = len(chunks)
        for ci, (c0, n) in enumerate(chunks):
            ps = self.nextpf()
            S.op('pe', lambda e, ps=ps, c0=c0, n=n: e.matmul(ps[:, 0:n], lhsT=qT, rhs=kT[:, c0:c0 + n], start=True, stop=True),
                 r=[qb, kbuf], w=[ps.b])
            S.op('dve', lambda e, ps=ps, n=n, ci=ci: e.reduce_max(out=mc[:, ci:ci + 1], in_=ps[:, 0:n], axis=AX.X), r=[ps.b], w=[mc.b])
        S.op('dve', lambda e: e.reduce_max(out=m[:], in_=mc[:, 0:nch], axis=AX.X), r=[mc.b], w=[m.b])
        S.op('dve', lambda e: e.tensor_scalar(out=m[:], in0=m[:], scalar1=-scale, scalar2=None, op0=ALU.mult), r=[m.b], w=[m.b])
        for ci, (c0, n) in enumerate(chunks):
            ps = self.nextpf()
            S.op('pe', lambda e, ps=ps, c0=c0, n=n: e.matmul(ps[:, 0:n], lhsT=qT, rhs=kT[:, c0:c0 + n], start=True, stop=True),
                 r=[qb, kbuf], w=[ps.b])
            S.op('act', lambda e, ps=ps, c0=c0, n=n, ci=ci: e.activation(out=P[:, c0:c0 + n], in_=ps[:, 0:n], func=AF.Exp, bias=m[:], scale=scale,
                                                                        accum_out=sc[:, ci:ci + 1]), r=[ps.b, m.b], w=[P.b, sc.b])
        S.op('dve', lambda e: e.reduce_sum(out=ss[:], in_=sc[:, 0:nch], axis=AX.X), r=[sc.b], w=[ss.b])
        S.op('dve', lambda e: e.reciprocal(out=ss[:], in_=ss[:]), r=[ss.b], w=[ss.b])
        off = 0
        vall = []
        for (vap, kn, vbuf) in vlist:
            vall.append((vap, kn, vbuf, off))
            off += kn
        for g in range(0, len(vall), 8):
            grp = vall[g:g + 8]
            pb = self.nextpb()
            for k, (vap, kn, vbuf, o_) in enumerate(grp):
                S.op('pe', lambda e, k=k, kn=kn, o_=o_, pb=pb: e.transpose(out=pb[0:kn, k * 128:(k + 1) * 128], in_=P[:, o_:o_ + kn],
                                                                          identity=self.identb[:]), r=[P.b, self.identb.b], w=[pb.b])
            eng = 'act' if (g // 8) % 3 == 2 else 'dve'
            S.op(eng, lambda e, g=g, pb=pb, ng=len(grp): (e.copy if e is self.nc.scalar else e.tensor_copy)(
                out=PT[:, g:g + ng, :].rearrange("p a b -> p (a b)"), in_=pb[:, 0:ng * 128]), r=[pb.b], w=[PT.b])
        po = self.nextpf()
        for k, (vap, kn, vbuf, o_) in enumerate(vall):
            S.op('pe', lambda e, k=k, kn=kn, vap=vap: e.matmul(po[:, 0:dv], lhsT=PT[0:kn, k, :], rhs=vap, start=(k == 0), stop=(k == len(vall) - 1)),
                 r=[PT.b, vbuf], w=[po.b])
        S.op('act', lambda e: e.activation(out=O[:, 0:dv], in_=po[:, 0:dv], func=AF.Identity, scale=ss[:]), r=[po.b, ss.b], w=[O.b])

    def attn_pair2(self, Ws, qTs, qbs, kTs, kbufs, vlist, scale, dv):
        S = self.s
        nk = kTs[0].shape[1]
        chunks = [(c0, min(512, nk - c0)) for c0 in range(0, nk, 512)]
        nch = len(chunks)
        U = range(len(Ws))
        for u in U:
            W = Ws[u]
            for ci, (c0, n) in enumerate(chunks):
                ps = self.nextpf()
                S.op('pe', lambda e, ps=ps, c0=c0, n=n, u=u: e.matmul(ps[:, 0:n], lhsT=qTs[u], rhs=kTs[u][:, c0:c0 + n], start=True, stop=True),
                     r=[qbs[u], kbufs[u]], w=[ps.b])
                S.op('dve', lambda e, ps=ps, n=n, ci=ci, W=W: e.reduce_max(out=W['mc'][:, ci:ci + 1], in_=ps[:, 0:n], axis=AX.X), r=[ps.b], w=[W['mc'].b])
        for u in U:
            W = Ws[u]
            S.op('dve', lambda e, W=W: e.reduce_max(out=W['m'][:], in_=W['mc'][:, 0:nch], axis=AX.X), r=[W['mc'].b], w=[W['m'].b])
            S.op('dve', lambda e, W=W: e.tensor_scalar(out=W['m'][:], in0=W['m'][:], scalar1=-scale, scalar2=None, op0=ALU.mult), r=[W['m'].b], w=[W['m'].b])
        for u in U:
            W = Ws[u]
            for ci, (c0, n) in enumerate(chunks):
                ps = self.nextpf()
                S.op('pe', lambda e, ps=ps, c0=c0, n=n, u=u: e.matmul(ps[:, 0:n], lhsT=qTs[u], rhs=kTs[u][:, c0:c0 + n], start=True, stop=True),
                     r=[qbs[u], kbufs[u]], w=[ps.b])
                S.op('act', lambda e, ps=ps, c0=c0, n=n, ci=ci, W=W: e.activation(out=W['P'][:, c0:c0 + n], in_=ps[:, 0:n], func=AF.Exp, bias=W['m'][:], scale=scale,
                                                                                 accum_out=W['sc'][:, ci:ci + 1]), r=[ps.b, W['m'].b], w=[W['P'].b, W['sc'].b])
        for u in U:
            W = Ws[u]
            S.op('dve', lambda e, W=W: e.reduce_sum(out=W['ss'][:], in_=W['sc'][:, 0:nch], axis=AX.X), r=[W['sc'].b], w=[W['ss'].b])
            S.op('dve', lambda e, W=W: e.reciprocal(out=W['ss'][:], in_=W['ss'][:]), r=[W['ss'].b], w=[W['ss'].b])
        vall = []
        off = 0
        for (vap, kn, vbuf) in vlist:
            vall.append((vap, kn, vbuf, off))
            off += kn
        for g in range(0, len(vall), 8):
            grp = vall[g:g + 8]
            for u in U:
                W = Ws[u]
                pb = self.nextpb()
                for k, (vap, kn, vbuf, o_) in enumerate(grp):
                    S.op('pe', lambda e, k=k, kn=kn, o_=o_, pb=pb, W=W: e.transpose(out=pb[0:kn, k * 128:(k + 1) * 128], in_=W['P'][:, o_:o_ + kn],
                                                                                   identity=self.identb[:]), r=[W['P'].b, self.identb.b], w=[pb.b])
                eng = 'act' if u % 2 else 'dve'
                S.op(eng, lambda e, g=g, pb=pb, ng=len(grp), W=W: (e.copy if e is self.nc.scalar else e.tensor_copy)(
                    out=W['PT'][:, g:g + ng, :].rearrange("p a b -> p (a b)"), in_=pb[:, 0:ng * 128]), r=[pb.b], w=[W['PT'].b])
        for u in U:
            W = Ws[u]
            po = self.nextpf()
            for k, (vap, kn, vbuf, o_) in enumerate(vall):
                S.op('pe', lambda e, k=k, kn=kn, vap=vap, W=W, po=po: e.matmul(po[:, 0:dv], lhsT=W['PT'][0:kn, k, :], rhs=vap, start=(k == 0), stop=(k == len(vall) - 1)),
                     r=[W['PT'].b, vbuf], w=[po.b])
            S.op('act', lambda e, W=W, po=po: e.activation(out=W['O'][:, 0:dv], in_=po[:, 0:dv], func=AF.Identity, scale=W['ss'][:]), r=[po.b, W['ss'].b], w=[W['O'].b])

    def load_fm(self, tile, nm, j, q='sp'):
        ch = self.fmidx[(nm, j)]
        self.s.dma(q, tile[:], self.scr['pT'][ch, :, :], r=[self.db('pT', ch)], w=[tile.b])

    def load_tm(self, tile, nm, c0, w, q='sp'):
        cc = self.tmcol[nm] + c0
        ntl = self.cfg['TA'] // 128
        self.s.dma(q, tile[:], self.scr['pV'][:, cc:cc + w].rearrange("(n p) c -> p n c", p=128),
                   r=self.dbs('pV', 0, ntl), w=[tile.b])

    def store_y(self, O, ob, tt, col, w):
        S = self.s
        S.op('dve', lambda e: e.tensor_copy(out=ob[:, 0:w], in_=O[:, 0:w]), r=[O.b], w=[ob.b])
        S.dma('pool', self.scr['y_tm'][tt * 128:(tt + 1) * 128, col:col + w], ob[:, 0:w], r=[ob.b], w=[self.db('y_tm', tt)])

    def stage_swa(self, l):
        c = self.cfg
        T, C, TA, D = c['T'], c['C'], c['TA'], c['D']
        S = self.s
        ntl, nlat = TA // 128, T // 128
        scale = 128.0 ** -0.5
        kT = [self.sb('sw_k%d' % k, [128, TA], BF16) for k in range(2)]
        V = [self.sb('sw_v%d' % k, [128, ntl, 128], BF16) for k in range(2)]
        qT = [self.sb('sw_q%d' % k, [128, TA], BF16) for k in range(2)]
        sk = [self.sb('sw_s%d' % k, [128, 1], F32) for k in range(2)]
        mask = self.sb('sw_mask', [128, 384], F32)
        ob = [self.sb('sw_ob%d' % k, [128, 128], BF16) for k in range(2)]
        S.dma('sp', mask[:], self.ins['k_swamask'][:, :], w=[mask.b])
        W = self.attn_work('sw', C + 384, 128, nbuf=2)
        u = 0
        for g in range(c['SKV']):
            k_, v_ = kT[g % 2], V[g % 2]
            self.load_fm(k_, 'sk', g)
            self.load_tm(v_, 'sv', g * 128, 128)
            for hh in range(4):
                h = g * 4 + hh
                q_, s_ = qT[h % 2], sk[h % 2]
                self.load_fm(q_, 'sq', h)
                S.dma('sp', s_[:], self.ins['swa_sink'][0:1, h:h + 1].partition_broadcast(128), w=[s_.b])
                for tt in range(ntl):
                    ctxpart = (k_[:, T:TA], k_.b, None, None, [(v_[:, nlat + j, :], 128, v_.b) for j in range(C // 128)])
                    if tt < nlat:
                        lo, hi = max(tt - 1, 0), min(tt + 2, nlat)
                        m0 = (lo - (tt - 1)) * 128
                        parts = [ctxpart, (k_[:, lo * 128:hi * 128], k_.b, mask[:, m0:m0 + (hi - lo) * 128], mask.b,
                                           [(v_[:, j, :], 128, v_.b) for j in range(lo, hi)])]
                    else:
                        parts = [ctxpart]
                    w_ = W[u % 2]
                    self.attn_unit(w_, q_[:, tt * 128:(tt + 1) * 128], q_.b, parts, scale, 128, sink=(s_, s_.b))
                    self.store_y(w_['O'], ob[u % 2], tt, D // 2 + h * 128, 128)
                    u += 1

    def stage_gla(self, l):
        c = self.cfg
        T, C, TA, D, GH = c['T'], c['C'], c['TA'], c['D'], c['GH']
        S = self.s
        I = self.ins
        qT = self.sb('gl_q', [128, TA], BF16)
        kT = self.sb('gl_k', [128, TA], BF16)
        lg = [self.sb('gl_l%d' % d, [128, TA], F32) for d in range(2)]
        dn1 = self.sb('gl_dn', [16, TA], BF16)
        dn = [dn1, dn1]
        upf = self.sb('gl_upf', [16, 128], F32)
        upb = self.sb('gl_upb', [16, 128], BF16)
        nb = self.sb('gl_nb', [128, 1], F32)
        et = self.sb('gl_et', [128, 512], F32)
        tri = self.sb('gl_tri', [64, 2, 64], F32)
        ones = self.sb('gl_ones', [128, 64], F32)
        ng = self.sb('gl_ng', [64, 256], F32)
        St = self.sb('gl_S', [128, 256], F32)
        Sb = self.sb('gl_Sb', [128, 256], BF16)
        vg = [self.sb('gl_v%d' % k, [64, 8, 256], BF16) for k in range(2)]
        rg = [self.sb('gl_r%d' % k, [64, 8, 256], BF16) for k in range(2)]
        og = [self.sb('gl_o%d' % k, [64, 8, 256], F32) for k in range(2)]
        yg = [self.sb('gl_y%d' % k, [64, 8, 256], BF16) for k in range(2)]
        cc = [self.sb('gl_c%d' % k, [128, 64], F32) for k in range(2)]
        c2 = [self.sb('gl_c2%d' % k, [128, 64], F32) for k in range(2)]
        ncl = [self.sb('gl_ncl%d' % k, [128, 1], F32) for k in range(2)]
        ebl = [self.sb('gl_ebl%d' % k, [128, 1], F32) for k in range(2)]
        eb = [self.sb('gl_eb%d' % k, [128, 64], F32) for k in range(2)]
        ei = [self.sb('gl_ei%d' % k, [128, 64], F32) for k in range(2)]
        eu = [self.sb('gl_eu%d' % k, [128, 64], F32) for k in range(2)]
        qd = [self.sb('gl_qd%d' % k, [128, 64], BF16) for k in range(2)]
        ki = [self.sb('gl_ki%d' % k, [128, 64], BF16) for k in range(2)]
        ku = [self.sb('gl_ku%d' % k, [128, 64], BF16) for k in range(2)]
        kut = [self.sb('gl_kut%d' % k, [64, 128], BF16) for k in range(2)]
        at = [self.sb('gl_at%d' % k, [64, 64], BF16) for k in range(2)]
        ot = [self.sb('gl_ot%d' % k, [64, 256], F32) for k in range(2)]
        sr = [None, None]
        srg = [self.sb('gl_srg%d' % k, [64, 8, 256], F32) for k in range(2)]
        epst = self.sb('gl_eps', [128, 1], F32)
        S.op('pool', lambda e: e.memset(epst[:], 1e-6), w=[epst.b])
        jk = self.sb('gl_jk', [64, 256], F32)
        ss = [self.sb('gl_ss%d' % k, [64, 1], F32) for k in range(2)]
        S.dma('sp', tri[:], I['k_tri'].rearrange("d j i -> j d i"), w=[tri.b])
        S.op('pool', lambda e: e.memset(ones[:], 1.0), w=[ones.b])
        S.dma('sp', ng[:], I['gla_norm_g'][0:1, :].partition_broadcast(64), w=[ng.b])
        gcv = self.tmcol['gv']
        gcr = self.tmcol['gr']
        groups = [(T + g0, min(8, (C - g0) // 64)) for g0 in range(0, C, 512)] + [(g0, 8) for g0 in range(0, T, 512)]
        n = 0
        for h in range(GH):
            self.load_fm(qT, 'gq', h)
            self.load_fm(kT, 'gk', h)
            for d, (un, bn) in enumerate((('gla_gate_up_f', 'gla_gate_bias_f'), ('gla_gate_up_b', 'gla_gate_bias_b'))):
                ch = self.fmidx[(('dnf', 'dnb')[d], 0)]
                S.dma('sp', dn[d][:], self.scr['pT'][ch, 0:16, :], r=[self.db('pT', ch)], w=[dn[d].b])
                S.dma('sp', upf[:], I[un][0, :, h * 128:(h + 1) * 128], w=[upf.b])
                S.op('dve', lambda e: e.tensor_copy(out=upb[:], in_=upf[:]), r=[upf.b], w=[upb.b])
                S.dma('sp', nb[:], I[bn][0, h * 128:(h + 1) * 128].rearrange("(p o) -> p o", o=1), w=[nb.b])
                S.op('dve', lambda e: e.tensor_scalar(out=nb[:], in0=nb[:], scalar1=-1.0, scalar2=None, op0=ALU.mult), r=[nb.b], w=[nb.b])
                for (t0, nt) in self.tok_blocks():
                    ps = self.nextpf()
                    S.op('pe', lambda e, ps=ps, d=d, t0=t0, nt=nt: e.matmul(ps[:, 0:nt], lhsT=upb[:, :], rhs=dn[d][:, t0:t0 + nt], start=True, stop=True),
                         r=[upb.b, dn[d].b], w=[ps.b])
                    S.op('act', lambda e, ps=ps, nt=nt: e.activation(out=et[:, 0:nt], in_=ps[:, 0:nt], func=AF.Exp, bias=nb[:], scale=-1.0),
                         r=[ps.b, nb.b], w=[et.b])
                    S.op('act', lambda e, d=d, t0=t0, nt=nt: e.activation(out=lg[d][:, t0:t0 + nt], in_=et[:, 0:nt], func=AF.Ln, bias=1.0, scale=1.0),
                         r=[et.b], w=[lg[d].b])
            for d in range(2):
                S.op('pool', lambda e: e.memset(St[:], 0.0), w=[St.b])
                S.op('pool', lambda e: e.memset(Sb[:], 0.0), w=[Sb.b])
                glist = groups if d == 0 else [groups[i] for i in list(range(len(groups) - 1, -1, -1))]
                if d == 1:
                    nctx = -(-C // 512)
                    glist = groups[:nctx][::-1] + groups[nctx:][::-1]
                for gi, (g0, gn) in enumerate(glist):
                    v_, r_, o_, y_ = vg[gi % 2], rg[gi % 2], og[gi % 2], yg[gi % 2]
                    rows = slice(g0, g0 + gn * 64)
                    tl = list(range(g0 // 128, -(-(g0 + gn * 64) // 128)))
                    S.dma('sp', v_[:, 0:gn, :], self.scr['pV'][rows, gcv + h * 256:gcv + (h + 1) * 256].rearrange("(n p) c -> p n c", p=64),
                          r=[self.db('pV', t) for t in tl], w=[v_.b])
                    if d == 1:
                        S.dma('sp', r_[:, 0:gn, :], self.scr['pV'][rows, gcr + h * 256:gcr + (h + 1) * 256].rearrange("(n p) c -> p n c", p=64),
                              r=[self.db('pV', t) for t in tl], w=[r_.b])
                        S.dma('sp', o_[:, 0:gn, :], self.scr['gla_o'][rows, h * 256:(h + 1) * 256].rearrange("(n p) c -> p n c", p=64),
                              r=[self.db('gla_o', t) for t in tl], w=[o_.b])
                        srg_ = srg[gi % 2]
                        S.op('act', lambda e, srg_=srg_, r_=r_, gn=gn: e.activation(out=srg_[:, 0:gn, :], in_=r_[:, 0:gn, :], func=AF.Silu), r=[r_.b], w=[srg_.b])
                    korder = range(gn) if d == 0 else range(gn - 1, -1, -1)
                    for k in korder:
                        t0 = g0 + k * 64
                        i2 = n % 2
                        n += 1
                        c_, c2_, ncl_, ebl_, eb_, ei_, eu_ = cc[i2], c2[i2], ncl[i2], ebl[i2], eb[i2], ei[i2], eu[i2]
                        qd_, ki_, ku_, kut_, at_, ot_, sr_, ss_ = qd[i2], ki[i2], ku[i2], kut[i2], at[i2], ot[i2], sr[i2], ss[i2]
                        lch = lg[d][:, t0:t0 + 64]
                        S.op('dve', lambda e, c_=c_, lch=lch: e.tensor_tensor_scan(out=c_[:], data0=ones[:], data1=lch, initial=0.0,
                                                                                  op0=ALU.mult, op1=ALU.add), r=[ones.b, lg[d].b], w=[c_.b])
                        S.op('dve', lambda e, c_=c_, ncl_=ncl_: e.tensor_scalar(out=ncl_[:], in0=c_[:, 63:64], scalar1=-1.0 / 16, scalar2=None, op0=ALU.mult),
                             r=[c_.b], w=[ncl_.b])
                        if d == 0:
                            cu = c_
                        else:
                            S.op('dve', lambda e, c_=c_, c2_=c2_, lch=lch: e.scalar_tensor_tensor(out=c2_[:], in0=c_[:], scalar=-1.0, in1=lch,
                                                                                                   op0=ALU.mult, op1=ALU.add), r=[c_.b, lg[d].b], w=[c2_.b])
                            S.op('dve', lambda e, c_=c_, c2_=c2_: e.tensor_scalar(out=c2_[:], in0=c2_[:], scalar1=c_[:, 63:64], scalar2=None, op0=ALU.add),
                                 r=[c_.b, c2_.b], w=[c2_.b])
                            cu = c2_
                        S.op('act', lambda e, cu=cu, eb_=eb_: e.activation(out=eb_[:], in_=cu[:], func=AF.Exp, scale=-1.0 / 16), r=[cu.b], w=[eb_.b])
                        S.op('act', lambda e, cu=cu, ei_=ei_: e.activation(out=ei_[:], in_=cu[:], func=AF.Exp, scale=1.0 / 16), r=[cu.b], w=[ei_.b])
                        S.op('act', lambda e, cu=cu, eu_=eu_, ncl_=ncl_: e.activation(out=eu_[:], in_=cu[:], func=AF.Exp, scale=1.0 / 16, bias=ncl_[:]),
                             r=[cu.b, ncl_.b], w=[eu_.b])
                        S.op('act', lambda e, ebl_=ebl_, ncl_=ncl_: e.activation(out=ebl_[:], in_=ncl_[:], func=AF.Exp), r=[ncl_.b], w=[ebl_.b])
                        S.op('dve', lambda e, qd_=qd_, eb_=eb_, t0=t0: e.scalar_tensor_tensor(out=qd_[:], in0=qT[:, t0:t0 + 64], scalar=128.0 ** -0.5, in1=eb_[:],
                                                                                              op0=ALU.mult, op1=ALU.mult), r=[qT.b, eb_.b], w=[qd_.b])
                        S.op('pool', lambda e, ki_=ki_, ei_=ei_, t0=t0: e.tensor_tensor(out=ki_[:], in0=kT[:, t0:t0 + 64], in1=ei_[:], op=ALU.mult),
                             r=[kT.b, ei_.b], w=[ki_.b])
                        S.op('pool', lambda e, ku_=ku_, eu_=eu_, t0=t0: e.tensor_tensor(out=ku_[:], in0=kT[:, t0:t0 + 64], in1=eu_[:], op=ALU.mult),
                             r=[kT.b, eu_.b], w=[ku_.b])
                        pa = self.nextpf()
                        S.op('pe', lambda e, pa=pa, ki_=ki_, qd_=qd_: e.matmul(pa[0:64, 0:64], lhsT=ki_[:, :], rhs=qd_[:, :], start=True, stop=True),
                             r=[ki_.b, qd_.b], w=[pa.b])
                        S.op('dve', lambda e, pa=pa, at_=at_, d=d: e.tensor_tensor(out=at_[:], in0=pa[0:64, 0:64], in1=tri[:, d, :], op=ALU.mult),
                             r=[pa.b, tri.b], w=[at_.b])
                        po = self.nextpf()
                        S.op('pe', lambda e, po=po, at_=at_, v_=v_, k=k: e.matmul(po[0:64, 0:256], lhsT=at_[:, :], rhs=v_[:, k, :], start=True, stop=False),
                             r=[at_.b, v_.b], w=[po.b])
                        S.op('pe', lambda e, po=po, qd_=qd_: e.matmul(po[0:64, 0:256], lhsT=qd_[:, :], rhs=Sb[:, :], start=False, stop=True),
                             r=[qd_.b, Sb.b], w=[po.b])
                        if d == 0:
                            S.op('act', lambda e, po=po, o_=o_, k=k: e.copy(out=o_[:, k, :], in_=po[0:64, 0:256]), r=[po.b], w=[o_.b])
                        else:
                            S.op('dve', lambda e, po=po, o_=o_, ot_=ot_, k=k: e.tensor_tensor(out=ot_[:], in0=po[0:64, 0:256], in1=o_[:, k, :], op=ALU.add),
                                 r=[po.b, o_.b], w=[ot_.b])
                            S.op('pool', lambda e, ot_=ot_: e.tensor_tensor(out=jk[:], in0=ot_[:], in1=ot_[:], op=ALU.mult), r=[ot_.b], w=[jk.b])
                            S.op('dve', lambda e, ss_=ss_: e.reduce_sum(out=ss_[:], in_=jk[:], axis=AX.X), r=[jk.b], w=[ss_.b])
                            S.op('act', lambda e, ss_=ss_: e.activation(out=ss_[:], in_=ss_[:], func=AF.Ln, scale=1.0 / 256, bias=epst[0:64, :]), r=[ss_.b, epst.b], w=[ss_.b])
                            S.op('act', lambda e, ss_=ss_: e.activation(out=ss_[:], in_=ss_[:], func=AF.Exp, scale=-0.5), r=[ss_.b], w=[ss_.b])
                            S.op('dve', lambda e, ot_=ot_, ss_=ss_: e.scalar_tensor_tensor(out=ot_[:], in0=ot_[:], scalar=ss_[:], in1=ng[:], op0=ALU.mult, op1=ALU.mult),
                                 r=[ot_.b, ss_.b, ng.b], w=[ot_.b])
                            S.op('pool', lambda e, ot_=ot_, srg_=srg_, y_=y_, k=k: e.tensor_tensor(out=y_[:, k, :], in0=ot_[:], in1=srg_[:, k, :], op=ALU.mult),
                                 r=[ot_.b, srg_.b], w=[y_.b])
                        pt = self.nextpb()
                        S.op('pe', lambda e, pt=pt, ku_=ku_: e.transpose(out=pt[0:64, 0:128], in_=ku_[:, :], identity=self.identb[:]),
                             r=[ku_.b, self.identb.b], w=[pt.b])
                        S.op('act', lambda e, pt=pt, kut_=kut_: e.copy(out=kut_[:], in_=pt[0:64, 0:128]), r=[pt.b], w=[kut_.b])
                        pd = self.nextpf()
                        S.op('pe', lambda e, pd=pd, kut_=kut_, v_=v_, k=k: e.matmul(pd[:, 0:256], lhsT=kut_[:, :], rhs=v_[:, k, :], start=True, stop=True),
                             r=[kut_.b, v_.b], w=[pd.b])
                        S.op('dve', lambda e, pd=pd, ebl_=ebl_: e.scalar_tensor_tensor(out=St[:], in0=St[:], scalar=ebl_[:], in1=pd[:, 0:256], op0=ALU.mult, op1=ALU.add),
                             r=[St.b, ebl_.b, pd.b], w=[St.b])
                        S.op('act', lambda e: e.copy(out=Sb[:], in_=St[:]), r=[St.b], w=[Sb.b])
                    if d == 0:
                        S.dma('pool', self.scr['gla_o'][rows, h * 256:(h + 1) * 256].rearrange("(n p) c -> p n c", p=64), o_[:, 0:gn, :],
                              r=[o_.b], w=[self.db('gla_o', t) for t in tl])
                    else:
                        S.dma('pool', self.scr['y_tm'][rows, h * 256:(h + 1) * 256].rearrange("(n p) c -> p n c", p=64), y_[:, 0:gn, :],
                              r=[y_.b], w=[self.db('y_tm', t) for t in tl])

    def stage_mix_even(self, l):
        self.stage_gla(l)
        self.barrier()
        self._es.close()
        self.stage_begin()
        self.stage_swa(l)

    def stage_na(self, l):
        c = self.cfg
        T, C, TA, D, NH = c['T'], c['C'], c['TA'], c['D'], c['NH']
        S = self.s
        ntl, nlat = TA // 128, T // 128
        scale = 128.0 ** -0.5
        kT = [self.sb('na_k%d' % k, [128, TA], BF16) for k in range(2)]
        V = [self.sb('na_v%d' % k, [128, ntl, 128], BF16) for k in range(2)]
        qT = [self.sb('na_q%d' % k, [128, TA], BF16) for k in range(2)]
        bt = [self.sb('na_b%d' % k, [128, 5, 640], F32) for k in range(2)]
        ob = [self.sb('na_ob%d' % k, [128, 128], BF16) for k in range(2)]
        W = self.attn_work('na', C + 640, 128, nbuf=2)
        u = 0
        for h in range(NH):
            k_, v_, q_, b_ = kT[h % 2], V[h % 2], qT[h % 2], bt[h % 2]
            self.load_fm(k_, 'nk', h)
            self.load_tm(v_, 'nv', h * 128, 128)
            self.load_fm(q_, 'nq', h)
            S.dma('sp', b_[:], self.ins['na_bias'][:, h, :, :].rearrange("v q k -> q v k"), w=[b_.b])
            for tt in range(nlat):
                vi, lo = na_variant(c, 2 * tt)
                k0 = lo * 64
                parts = [(k_[:, T:TA], k_.b, None, None, [(v_[:, nlat + j, :], 128, v_.b) for j in range(C // 128)]),
                         (k_[:, k0:k0 + 640], k_.b, b_[:, vi, :], b_.b, [(v_[:, k0 // 128 + j, :], 128, v_.b) for j in range(5)])]
                w_ = W[u % 2]
                self.attn_unit(w_, q_[:, tt * 128:(tt + 1) * 128], q_.b, parts, scale, 128)
                self.store_y(w_['O'], ob[u % 2], tt, h * 128, 128)
                u += 1

    def stage_diff(self, l):
        c = self.cfg
        T, C, TA, D, DH = c['T'], c['C'], c['TA'], c['D'], c['DH']
        S = self.s
        I = self.ins
        ntl, nlat = TA // 128, T // 128
        scale = 128.0 ** -0.5
        lam_init = 0.8 - 0.6 * math.exp(-0.3 * l)
        kT = [self.sb('df_k%d' % k, [128, TA], BF16) for k in range(2)]
        qT = [self.sb('df_q%d' % k, [128, TA], BF16) for k in range(2)]
        V = self.sb('df_v', [128, ntl, 256], BF16)
        ng = self.sb('df_ng', [128, 256], F32)
        lv = [self.sb('df_l%d' % k, [128, 128], F32) for k in range(4)]
        lj = self.sb('df_lj', [128, 128], F32)
        la = [self.sb('df_la%d' % k, [128, 1], F32) for k in range(2)]
        nlam = self.sb('df_nlam', [128, 1], F32)
        od = self.sb('df_od', [128, 256], F32)
        jk = self.sb('df_jk', [128, 256], F32)
        ss = self.sb('df_ss', [128, 1], F32)
        ob = [self.sb('df_ob%d' % k, [128, 256], BF16) for k in range(2)]
        epst = self.sb('df_eps', [128, 1], F32)
        S.op('pool', lambda e: e.memset(epst[:], 1e-6), w=[epst.b])
        W = self.attn_work2('df', TA, 256, nbuf=2)
        S.dma('sp', ng[:], I['diff_norm_g'][0:1, :].partition_broadcast(128), w=[ng.b])
        for k, nm in enumerate(('diff_lq1', 'diff_lk1', 'diff_lq2', 'diff_lk2')):
            S.dma('sp', lv[k][:], I[nm][0:1, :].partition_broadcast(128), w=[lv[k].b])
        for k in range(2):
            S.op('dve', lambda e, k=k: e.tensor_tensor(out=lj[:], in0=lv[2 * k][:], in1=lv[2 * k + 1][:], op=ALU.mult),
                 r=[lv[2 * k].b, lv[2 * k + 1].b], w=[lj.b])
            S.op('dve', lambda e, k=k: e.reduce_sum(out=la[k][:], in_=lj[:], axis=AX.X), r=[lj.b], w=[la[k].b])
            S.op('act', lambda e, k=k: e.activation(out=la[k][:], in_=la[k][:], func=AF.Exp), r=[la[k].b], w=[la[k].b])
        S.op('dve', lambda e: e.tensor_tensor(out=nlam[:], in0=la[1][:], in1=la[0][:], op=ALU.subtract), r=[la[0].b, la[1].b], w=[nlam.b])
        S.op('dve', lambda e: e.tensor_scalar(out=nlam[:], in0=nlam[:], scalar1=-lam_init, scalar2=None, op0=ALU.add), r=[nlam.b], w=[nlam.b])
        u = 0
        for h in range(DH):
            self.load_tm(V, 'dv', h * 256, 256)
            for s_ in range(2):
                self.load_fm(kT[s_], 'dk', 2 * h + s_)
                self.load_fm(qT[s_], 'dq', 2 * h + s_)
            vlist = [(V[:, j, :], 128, V.b) for j in range(ntl)]
            for tt in range(nlat):
                self.attn_pair2(W, [qT[s_][:, tt * 128:(tt + 1) * 128] for s_ in range(2)], [qT[s_].b for s_ in range(2)],
                                [kT[s_][:, 0:TA] for s_ in range(2)], [kT[s_].b for s_ in range(2)], vlist, scale, 256)
                S.op('dve', lambda e: e.scalar_tensor_tensor(out=od[:], in0=W[1]['O'][:], scalar=nlam[:], in1=W[0]['O'][:], op0=ALU.mult, op1=ALU.add),
                     r=[W[0]['O'].b, W[1]['O'].b, nlam.b], w=[od.b])
                S.op('pool', lambda e: e.tensor_tensor(out=jk[:], in0=od[:], in1=od[:], op=ALU.mult), r=[od.b], w=[jk.b])
                S.op('dve', lambda e: e.reduce_sum(out=ss[:], in_=jk[:], axis=AX.X), r=[jk.b], w=[ss.b])
                S.op('act', lambda e: e.activation(out=ss[:], in_=ss[:], func=AF.Ln, scale=1.0 / 256, bias=epst[:]), r=[ss.b, epst.b], w=[ss.b])
                S.op('act', lambda e: e.activation(out=ss[:], in_=ss[:], func=AF.Exp, scale=-0.5), r=[ss.b], w=[ss.b])
                S.op('dve', lambda e: e.scalar_tensor_tensor(out=od[:], in0=od[:], scalar=ss[:], in1=ng[:], op0=ALU.mult, op1=ALU.mult),
                     r=[od.b, ss.b, ng.b], w=[od.b])
                o_ = ob[u % 2]
                u += 1
                S.op('act', lambda e, o_=o_: e.mul(out=o_[:], in_=od[:], mul=1.0 - lam_init), r=[od.b], w=[o_.b])
                S.dma('pool', self.scr['y_tm'][tt * 128:(tt + 1) * 128, D // 2 + h * 256:D // 2 + (h + 1) * 256], o_[:], r=[o_.b], w=[self.db('y_tm', tt)])

    def stage_mix_odd(self, l):
        self.stage_na(l)
        self.barrier()
        self._es.close()
        self.stage_begin()
        self.stage_diff(l)

    def resid_update(self, ps, n0, nw, tt, Gt, xp, tmp):
        S = self.s
        xr = self.scr['xres'][tt * 128:(tt + 1) * 128, n0:n0 + nw]
        S.dma('sp', xp[:, 0:nw], xr, r=[self.db('xres', tt)], w=[xp.b])
        S.op('dve', lambda e: e.tensor_tensor(out=tmp[:, 0:nw], in0=ps[:, 0:nw], in1=Gt[:, 0:nw], op=ALU.mult), r=[ps.b, Gt.b], w=[tmp.b])
        S.op('pool', lambda e: e.tensor_tensor(out=xp[:, 0:nw], in0=xp[:, 0:nw], in1=tmp[:, 0:nw], op=ALU.add), r=[xp.b, tmp.b], w=[xp.b])
        S.dma('pool', xr, xp[:, 0:nw], r=[xp.b], w=[self.db('xres', tt)])

    def blocks_of(self, tiles):
        nlat = self.cfg['T'] // 128
        out, cur = [], []
        for tt in tiles:
            if cur and (len(cur) == 4 or tt != cur[-1] + 1 or (tt == nlat)):
                out.append(cur)
                cur = []
            cur.append(tt)
        if cur:
            out.append(cur)
        return out

    def stage_wout(self, l, tiles):
        c = self.cfg
        D, T = c['D'], c['T']
        KC = D // 128
        S = self.s
        wbn = 'wb_out%d' % l
        wb = self.scr[wbn]
        yt = [self.sb('wo_y%d' % k, [128, D], BF16) for k in range(2)]
        yT = self.sb('wo_yT', [128, KC, 512], BF16)
        wt = [self.sb('wo_w%d' % k, [128, KC, 512], BF16) for k in range(2)]
        Gt = [self.sb('wo_G%d' % k, [128, 512], F32) for k in range(2)]
        xp = [self.sb('wo_x%d' % k, [128, 512], F32) for k in range(4)]
        tmp = [self.sb('wo_t%d' % k, [128, 512], F32) for k in range(4)]
        yi = wi = xi = 0
        for blk in self.blocks_of(tiles):
            row = 0 if blk[0] * 128 < T else 1
            for j, tt in enumerate(blk):
                y_ = yt[yi % 2]
                yi += 1
                S.dma('sp', y_[:], self.scr['y_tm'][tt * 128:(tt + 1) * 128, :], r=[self.db('y_tm', tt)], w=[y_.b])
                for g in range(0, KC, 8):
                    pb = self.nextpb()
                    for k in range(8):
                        S.op('pe', lambda e, k=k, g=g, y_=y_, pb=pb: e.transpose(out=pb[:, k * 128:(k + 1) * 128], in_=y_[:, (g + k) * 128:(g + k + 1) * 128],
                                                                              identity=self.identb[:]), r=[y_.b, self.identb.b], w=[pb.b])
                    for k in range(8):
                        eng = 'act' if k % 2 else 'dve'
                        S.op(eng, lambda e, k=k, g=g, j=j, pb=pb: (e.copy if e is self.nc.scalar else e.tensor_copy)(
                            out=yT[:, g + k, j * 128:(j + 1) * 128], in_=pb[:, k * 128:(k + 1) * 128]), r=[pb.b], w=[yT.b])
            for n0 in range(0, D, 512):
                w = wt[wi % 2]
                G_ = Gt[wi % 2]
                wi += 1
                S.dma('sp', w[:], wb[n0 // 512], r=[self.db(wbn, 0)], w=[w.b])
                S.dma('sp', G_[:], self.scr['modv'][l, 2, row:row + 1, n0:n0 + 512].partition_broadcast(128), r=[self.db('modv', l)], w=[G_.b])
                for j, tt in enumerate(blk):
                    ps = self.nextpf()
                    for kc in range(KC):
                        S.op('pe', lambda e, kc=kc, j=j, w=w, ps=ps: e.matmul(ps[:, :], lhsT=yT[:, kc, j * 128:(j + 1) * 128], rhs=w[:, kc, :],
                                                                             start=(kc == 0), stop=(kc == KC - 1)), r=[yT.b, w.b], w=[ps.b])
                    self.resid_update(ps, n0, 512, tt, G_, xp[xi % 4], tmp[xi % 4])
                    xi += 1

    def stage_ffn(self, l, tiles):
        c = self.cfg
        D, T, F = c['D'], c['T'], c['F']
        KC, FC = D // 128, F // 128
        S = self.s
        splits = self.ffn_split()
        nsplit = len(splits)
        FS = max(hi - lo for lo, hi in splits)
        gug = self.ffn_gugroups()
        dg = self.ffn_fgroups()
        hb = self.sb('ff_h', [128, KC, 512], BF16)
        aT = self.sb('ff_a', [128, FS, 512], BF16)
        wg = [self.sb('ff_g%d' % k, [128, KC, 256], BF16) for k in range(2)]
        wu = [self.sb('ff_u%d' % k, [128, KC, 256], BF16) for k in range(2)]
        wd = [self.sb('ff_d%d' % k, [128, 8, 512], BF16) for k in range(2)]
        sg = [self.sb('ff_s%d' % k, [128, 512], F32) for k in range(2)]
        Gt = [self.sb('ff_G%d' % k, [128, 512], F32) for k in range(2)]
        xp = [self.sb('ff_x%d' % k, [128, 512], F32) for k in range(4)]
        tmp = [self.sb('ff_t%d' % k, [128, 512], F32) for k in range(4)]
        wgs, wus, wds = self.scr['wb_g%d' % l], self.scr['wb_u%d' % l], self.scr['wb_d%d' % l]
        gi = di = si = xi = Gi = 0
        for blk in self.blocks_of(tiles):
            row = 0 if blk[0] * 128 < T else 1
            t0, n = blk[0] * 128, len(blk) * 128
            S.dma('sp', hb[:, :, 0:n], self.scr['hT'][blk[0] // 4, :, :, 0:n], r=[self.db('hT', blk[0] // 4)], w=[hb.b])
            for sp_ in range(nsplit):
                f_lo, f_hi = splits[sp_]
                for gidx, (gsp, fg, nf) in enumerate(gug):
                    if gsp != sp_:
                        continue
                    g_, u_ = wg[gi % 2], wu[gi % 2]
                    gi += 1
                    S.dma('sp', g_[:, :, 0:nf * 128], wgs[gidx, :, :, 0:nf * 128], r=[self.db('wb_g%d' % l, 0)], w=[g_.b])
                    S.dma('sp', u_[:, :, 0:nf * 128], wus[gidx, :, :, 0:nf * 128], r=[self.db('wb_u%d' % l, 0)], w=[u_.b])
                    for j in range(nf):
                        pg, pu = self.nextpf(), self.nextpf()
                        for kc in range(KC):
                            S.op('pe', lambda e, kc=kc, j=j, g_=g_, pg=pg: e.matmul(pg[:, 0:n], lhsT=g_[:, kc, j * 128:(j + 1) * 128], rhs=hb[:, kc, 0:n],
                                                                                   start=(kc == 0), stop=(kc == KC - 1)), r=[g_.b, hb.b], w=[pg.b])
                        for kc in range(KC):
                            S.op('pe', lambda e, kc=kc, j=j, u_=u_, pu=pu: e.matmul(pu[:, 0:n], lhsT=u_[:, kc, j * 128:(j + 1) * 128], rhs=hb[:, kc, 0:n],
                                                                                   start=(kc == 0), stop=(kc == KC - 1)), r=[u_.b, hb.b], w=[pu.b])
                        s_ = sg[si % 2]
                        si += 1
                        S.op('act', lambda e, s_=s_, pg=pg: e.activation(out=s_[:, 0:n], in_=pg[:, 0:n], func=AF.Silu), r=[pg.b], w=[s_.b])
                        S.op('dve', lambda e, s_=s_, pu=pu, fg=fg, j=j, f_lo=f_lo: e.tensor_tensor(out=aT[:, fg + j - f_lo, 0:n], in0=s_[:, 0:n], in1=pu[:, 0:n], op=ALU.mult),
                             r=[s_.b, pu.b], w=[aT.b])
                for n0 in range(0, D, 512):
                    G_ = Gt[Gi % 2]
                    Gi += 1
                    S.dma('sp', G_[:], self.scr['modv'][l, 5, row:row + 1, n0:n0 + 512].partition_broadcast(128), r=[self.db('modv', l)], w=[G_.b])
                    pss = [self.nextpf() for _ in blk]
                    for didx, (dsp, fg, nf) in enumerate(dg):
                        if dsp != sp_:
                            continue
                        d_ = wd[di % 2]
                        di += 1
                        S.dma('sp', d_[:, 0:nf, :], wds[didx, n0 // 512, :, 0:nf, :], r=[self.db('wb_d%d' % l, 0)], w=[d_.b])
                        for j in range(len(blk)):
                            for k in range(nf):
                                fc = fg + k
                                S.op('pe', lambda e, j=j, k=k, fc=fc, d_=d_: e.matmul(pss[j][:, :], lhsT=aT[:, fc - f_lo, j * 128:(j + 1) * 128], rhs=d_[:, k, :],
                                                                                     start=(fc == f_lo), stop=(fc == f_hi - 1)), r=[aT.b, d_.b], w=[pss[j].b])
                    for j, tt in enumerate(blk):
                        self.resid_update(pss[j], n0, 512, tt, G_, xp[xi % 4], tmp[xi % 4])
                        xi += 1

    def run_stage(self, fn, *a, **k):
        self.stage_begin()
        fn(*a, **k)
        self.stage_end()

    def build(self, upto='all'):
        c = self.cfg
        T, TA = c['T'], c['TA']
        self.declare()
        self.run_stage(self.stage_cast)
        self.run_stage(self.stage_init_x)
        self.run_stage(self.stage_mod)
        alltiles = list(range(TA // 128))
        lat = list(range(T // 128))
        if upto == 'mod':
            return self.finish()
        self.run_stage(self.stage_norm, 0, 0, alltiles)
        if upto == 'norm':
            return self.finish()
        for l in range(c['depth']):
            last = (l == c['depth'] - 1)
            self.run_stage(self.stage_proj, l)
            if upto == 'proj%d' % l:
                return self.finish()
            self.run_stage(self.stage_mix_even if l % 2 == 0 else self.stage_mix_odd, l)
            if upto == 'mix%d' % l:
                return self.finish()
            tiles = lat if last else alltiles
            self.run_stage(self.stage_wout, l, tiles)
            self.run_stage(self.stage_norm, l, 1, tiles)
            self.run_stage(self.stage_ffn, l, tiles)
            if upto == 'ffn%d' % l:
                return self.finish()
            if not last:
                self.run_stage(self.stage_norm, l + 1, 0, alltiles)
        self.run_stage(self.stage_norm, 0, 0, lat, final=True)
        return self.finish()

    def finish(self):
        bufs = [b for (nm, i), b in self.dbufs.items() if nm == 'out' or nm in self.debug]
        self.barrier()
        return self.nc


def host_consts(cfg):
    T = cfg['T']
    k = {}
    k['k_ident'] = np.eye(128, dtype=np.float32)
    pm = np.zeros((128, 128), np.float32)
    for m in range(128):
        h, i = divmod(m, 64)
        src = h * 64 + (i + 32) % 64
        pm[src, m] = 1.0
    k['k_perm'] = pm
    inv = (1.0 / (np.float32(10000.0) ** (np.arange(32, dtype=np.float32) / np.float32(32)))).astype(np.float32)
    pos = np.arange(T)
    row = (pos // 64).astype(np.float32)
    col = (pos % 64).astype(np.float32)
    cosT = np.zeros((128, T), np.float32)
    sinT = np.zeros((128, T), np.float32)
    for f in range(128):
        h, i = divmod(f, 64)
        j = i % 32
        ang = ((row if h == 0 else col) * inv[j]).astype(np.float32)
        cosT[f] = np.cos(ang).astype(np.float32)
        sg = -1.0 if i < 32 else 1.0
        sinT[f] = sg * np.sin(ang).astype(np.float32)
    k['k_cos'] = cosT
    k['k_sin'] = sinT
    qi = np.arange(128)[:, None]
    kj = np.arange(384)[None, :]
    k['k_swamask'] = np.where(np.abs(qi + 128 - kj) <= 128, 0.0, NEG).astype(np.float32)
    tri = np.zeros((2, 64, 64), np.float32)
    jj = np.arange(64)[:, None]
    ii = np.arange(64)[None, :]
    tri[0] = (ii >= jj)
    tri[1] = (ii <= jj)
    k['k_tri'] = tri
    return k


def na_bias_tables(cfg, rpb):
    T = cfg['T']
    rows = T // 64
    NH = rpb.shape[0]
    out = np.full((5, NH, 128, 640), NEG, np.float32)
    variants = [4, 0, 2, rows - 4, rows - 2]
    for vi, r0 in enumerate(variants):
        lo = min(max(r0 - 4, 0), rows - 10)
        for dq in range(2):
            r = r0 + dq
            rs = min(max(r - 4, 0), rows - 8)
            for cq in range(64):
                cst = min(max(cq - 8, 0), 64 - 16)
                q = dq * 64 + cq
                for kr in range(10):
                    ar = lo + kr
                    if not (rs <= ar < rs + 8):
                        continue
                    dr = ar - r + 7
                    kc = np.arange(cst, cst + 16)
                    dc = kc - cq + 15
                    out[vi, :, q, kr * 64 + kc] = rpb[:, dr, dc].T
    return out


def na_variant(cfg, r0):
    rows = cfg['T'] // 64
    if 4 <= r0 <= rows - 6:
        return 0, r0 - 4
    if r0 == 0:
        return 1, 0
    if r0 == 2:
        return 2, 0
    if r0 == rows - 4:
        return 3, rows - 10
    return 4, rows - 10


def make_in_maps(cfg, inputs, ncores):
    ks = host_consts(cfg)
    maps = []
    nab = na_bias_tables(cfg, np.asarray(inputs['na_rpb'])[0])
    for b in range(ncores):
        m = {}
        m['x'] = np.ascontiguousarray(inputs['x'][b])
        m['ctx'] = np.ascontiguousarray(inputs['ctx'][b])
        m['cvec'] = np.ascontiguousarray(np.stack([inputs['c'][b], inputs['c_ctx']]))
        for nm in ('ada_w', 'ada_b', 'norm_mix_g', 'norm_ffn_g', 'w_out', 'ffn_w_gate', 'ffn_w_up', 'ffn_w_down', 'ev_w_in',
                   'gla_gate_up_f', 'gla_gate_bias_f', 'gla_gate_up_b', 'gla_gate_bias_b', 'gla_norm_g', 'swa_sink', 'od_w_in',
                   'diff_lq1', 'diff_lk1', 'diff_lq2', 'diff_lk2', 'diff_norm_g', 'final_norm_g'):
            m[nm] = np.asarray(inputs[nm])
        m['na_bias'] = nab
        m.update(ks)
        maps.append(m)
    return maps


def kernel(**inputs):
    inputs = {k: np.asarray(v) for k, v in inputs.items()}
    B, T, D = inputs['x'].shape
    cfg = make_cfg(D=D, T=T, C=inputs['ctx'].shape[1], depth=inputs['ada_w'].shape[0])
    p = Prog(cfg)
    nc = p.build()
    maps = make_in_maps(cfg, inputs, B)
    res = run_bass_kernel_spmd(nc, maps, core_ids=list(range(B)))
    return np.stack([res.results[b]['out'] for b in range(B)]).astype(np.float32)
```

```python
import math
import numpy as np
import concourse.bass as bass
import concourse.mybir as mybir
from concourse.bass_utils import run_bass_kernel_spmd

F32 = mybir.dt.float32
BF16 = mybir.dt.bfloat16
AF = mybir.ActivationFunctionType
ALU = mybir.AluOpType
AX = mybir.AxisListType

NEG = -30000.0


def make_cfg(D=4096, T=8192, C=256, depth=2):
    cfg = dict(D=D, T=T, C=C, depth=depth, GRID_W=64, HD=128)
    half = D // 2
    cfg['GH'] = half // 256
    cfg['GQK'] = cfg['GH'] * 128
    cfg['SH'] = half // 128
    cfg['SKV'] = cfg['SH'] // 4
    cfg['NH'] = half // 128
    cfg['DH'] = half // 256
    cfg['F'] = -(-8 * D // (3 * 256)) * 256
    cfg['TA'] = T + C
    return cfg


class Buf:
    __slots__ = ('w', 'r', 'name')

    def __init__(self, name=''):
        self.w = None
        self.r = []
        self.name = name


class Sched:
    ENG = ('pe', 'act', 'dve', 'pool', 'sp')
    NS = 8

    def __init__(self, nc):
        self.nc = nc
        self.e = {'pe': nc.tensor, 'act': nc.scalar, 'dve': nc.vector, 'pool': nc.gpsimd, 'sp': nc.sync}
        self.sem = {k: nc.alloc_semaphore('sem_' + k) for k in self.ENG}
        self.cnt = {k: 0 for k in self.ENG}
        self.dsem = {}
        self.dtot = {}
        self.dn = {}
        for q in ('sp', 'pool', 'act'):
            self.dn[q] = 0
            for s in range(self.NS):
                self.dsem[(q, s)] = nc.alloc_semaphore('dsem_%s%d' % (q, s))
                self.dtot[(q, s)] = 0
        self.seen = {k: {} for k in self.ENG}
        self.ninstr = 0

    def _semh(self, key):
        return self.sem[key] if isinstance(key, str) else self.dsem[key]

    def _wait(self, eng, toks):
        need = {}
        for t in toks:
            if t is None:
                continue
            k, v = t
            if k == 'pe' and eng == 'pe':
                continue
            if need.get(k, 0) < v:
                need[k] = v
        seen = self.seen[eng]
        for k, v in need.items():
            if seen.get(k, 0) < v:
                self.e[eng].wait_ge(self._semh(k), v)
                seen[k] = v
                self.ninstr += 1

    def _deps(self, r, w):
        toks = []
        for b in r:
            toks.append(b.w)
        for b in w:
            toks.append(b.w)
            toks.extend(b.r)
        return toks

    def _commit(self, tok, r, w):
        for b in w:
            b.w = tok
            b.r = []
        for b in r:
            b.r.append(tok)
            if len(b.r) > 64:
                best = {}
                for k, v in b.r:
                    if best.get(k, 0) < v:
                        best[k] = v
                b.r = list(best.items())

    def op(self, eng, fn, r=(), w=()):
        self._wait(eng, self._deps(r, w))
        ins = fn(self.e[eng])
        ins.then_inc(self.sem[eng], 1)
        self.cnt[eng] += 1
        self.ninstr += 1
        self._commit((eng, self.cnt[eng]), r, w)

    def dma(self, q, out, in_, r=(), w=(), **kw):
        slot = self.dn[q] % self.NS
        self.dn[q] += 1
        key = (q, slot)
        toks = self._deps(r, w)
        if self.dtot[key] > 0:
            toks.append((key, self.dtot[key]))
        self._wait(q, toks)
        self.e[q].dma_start(out=out, in_=in_, **kw).then_inc(self.dsem[key], 16)
        self.dtot[key] += 16
        self.ninstr += 1
        self._commit((key, self.dtot[key]), r, w)

    def finish(self, bufs):
        toks = []
        for b in bufs:
            toks.append(b.w)
        self._wait('sp', toks)


class Tl:
    __slots__ = ('t', 'b')

    def __init__(self, t, name=''):
        self.t = t
        self.b = Buf(name)

    def __getitem__(self, k):
        return self.t[k]


class Builder:
    def __init__(self, cfg, debug=()):
        self.cfg = cfg
        self.debug = set(debug)
        self.nc = bass.Bass("TRN2", target_bir_lowering=False)
        self.s = Sched(self.nc)
        self.dbufs = {}
        self.ins = {}
        self.scr = {}
        self._stack = []

    def inp(self, name, shape, dt=F32):
        self.ins[name] = self.nc.dram_tensor(name, list(shape), dt, kind="ExternalInput").ap()
        return self.ins[name]

    def scratch(self, name, shape, dt):
        kind = "ExternalOutput" if name in self.debug else "Internal"
        self.scr[name] = self.nc.dram_tensor(name, list(shape), dt, kind=kind).ap()
        return self.scr[name]

    def db(self, name, idx=0):
        k = (name, idx)
        if k not in self.dbufs:
            self.dbufs[k] = Buf(str(k))
        return self.dbufs[k]

    def dbs(self, name, lo, hi):
        return [self.db(name, i) for i in range(lo, hi)]


def even_layout(cfg):
    GQK, GH, SH, SKV = cfg['GQK'], cfg['GH'], cfg['SH'], cfg['SKV']
    o = 0
    L = {}
    for nm, w in (('gq', GQK), ('gk', GQK), ('gv', GH * 256), ('gr', GH * 256), ('dnf', 16), ('dnb', 16),
                  ('sq', SH * 128), ('sk', SKV * 128), ('sv', SKV * 128)):
        L[nm] = (o, w)
        o += w
    L['_n'] = o
    return L


def odd_layout(cfg):
    NH, DH = cfg['NH'], cfg['DH']
    o = 0
    L = {}
    for nm, w in (('nq', NH * 128), ('nk', NH * 128), ('nv', NH * 128), ('dq', DH * 256), ('dk', DH * 256),
                  ('dv', DH * 256)):
        L[nm] = (o, w)
        o += w
    L['_n'] = o
    return L


FM_EVEN = ('gq', 'gk', 'dnf', 'dnb', 'sq', 'sk')
TM_EVEN = ('gv', 'gr', 'sv')
ROPE_EVEN = ('sq', 'sk')
FM_ODD = ('nq', 'nk', 'dq', 'dk')
TM_ODD = ('nv', 'dv')
ROPE_ODD = ('dq', 'dk')


class Prog(Builder):
    def declare(self):
        c = self.cfg
        D, T, C, F, dep = c['D'], c['T'], c['C'], c['F'], c['depth']
        EL, OL = even_layout(c), odd_layout(c)
        self.EL, self.OL = EL, OL
        i = self.inp
        i('x', [T, D]); i('ctx', [C, D]); i('cvec', [2, D])
        i('ada_w', [dep, D, 6 * D]); i('ada_b', [dep, 6 * D])
        i('norm_mix_g', [dep, D]); i('norm_ffn_g', [dep, D])
        i('w_out', [dep, D, D]); i('ffn_w_gate', [dep, D, F]); i('ffn_w_up', [dep, D, F]); i('ffn_w_down', [dep, F, D])
        i('ev_w_in', [1, D, EL['_n']]); i('gla_gate_up_f', [1, 16, c['GQK']]); i('gla_gate_bias_f', [1, c['GQK']])
        i('gla_gate_up_b', [1, 16, c['GQK']]); i('gla_gate_bias_b', [1, c['GQK']])
        i('gla_norm_g', [1, 256]); i('swa_sink', [1, c['SH']])
        i('od_w_in', [1, D, OL['_n']]); i('na_bias', [5, c['NH'], 128, 640])
        i('diff_lq1', [1, 128]); i('diff_lk1', [1, 128]); i('diff_lq2', [1, 128]); i('diff_lk2', [1, 128])
        i('diff_norm_g', [1, 256]); i('final_norm_g', [D])
        i('k_ident', [128, 128]); i('k_perm', [128, 128]); i('k_cos', [128, T]); i('k_sin', [128, T])
        i('k_swamask', [128, 384]); i('k_tri', [2, 64, 64])
        self.out = self.nc.dram_tensor('out', [T, D], F32, kind="ExternalOutput").ap()
        s = self.scratch
        TA = c['TA']
        s('xres', [TA, D], F32)
        s('hT', [-(-TA // 512), 128, D // 128, 512], BF16)
        s('y_tm', [TA, D], BF16)
        s('mod', [dep, 2, 6 * D], F32)
        s('modv', [dep, 6, 2, D], F32)
        self.pieces = {}
        for l_, (L_, fmn_, tmn_) in enumerate(((EL, FM_EVEN, TM_EVEN), (OL, FM_ODD, TM_ODD))):
            pcs = []
            for kind_, names_ in (('fm', fmn_), ('tm', tmn_)):
                for nm in names_:
                    for p0 in range(0, L_[nm][1], 512):
                        pcs.append((kind_, nm, p0, min(512, L_[nm][1] - p0)))
            self.pieces[l_] = pcs
            s('wb_in%d' % l_, [len(pcs), 128, D // 128, 512], BF16)
        for l in range(dep):
            s('wb_out%d' % l, [D // 512, 128, D // 128, 512], BF16)
            s('wb_g%d' % l, [len(self.ffn_gugroups()), 128, D // 128, 256], BF16); s('wb_u%d' % l, [len(self.ffn_gugroups()), 128, D // 128, 256], BF16)
            s('wb_d%d' % l, [len(self.ffn_fgroups()), D // 512, 128, 8, 512], BF16)
        nfm = max(sum(-(-EL[n][1] // 128) for n in FM_EVEN), sum(-(-OL[n][1] // 128) for n in FM_ODD))
        ntm = max(sum(EL[n][1] for n in TM_EVEN), sum(OL[n][1] for n in TM_ODD))
        s('pT', [nfm, 128, TA], BF16)
        s('pV', [TA, ntm], BF16)
        s('gla_o', [TA, c['GH'] * 256], F32)
        self.pf = [Tl(self.nc.alloc_psum_tensor('pf%d' % k, [128, 512], F32), 'pf%d' % k) for k in range(6)]
        self.pb = [Tl(self.nc.alloc_psum_tensor('pb%d' % k, [128, 1024], BF16), 'pb%d' % k) for k in range(2)]
        self.pfi = 0
        self.pbi = 0
        self.ident = self.sb('ident', [128, 128], F32, persist=True)
        self.identb = self.sb('identb', [128, 128], BF16, persist=True)
        self.s.dma('sp', self.ident[:], self.ins['k_ident'][:, :], w=[self.ident.b])
        self.s.op('dve', lambda e: e.tensor_copy(out=self.identb[:], in_=self.ident[:]), r=[self.ident.b], w=[self.identb.b])

    def sb(self, name, shape, dt, persist=False):
        self._uid = getattr(self, '_uid', 0) + 1
        nm = '%s_%d' % (name, self._uid)
        if persist:
            return Tl(self.nc.alloc_sbuf_tensor(nm, list(shape), dt), nm)
        return Tl(self._es.enter_context(self.nc.sbuf_tensor(nm, list(shape), dt)), nm)

    def stage_begin(self):
        import contextlib
        self._es = contextlib.ExitStack()

    def stage_end(self):
        self.barrier()
        self._es.close()

    def barrier(self):
        S = self.s
        toks = [(k, S.cnt[k]) for k in S.ENG if S.cnt[k] > 0]
        toks += [(k, v) for k, v in S.dtot.items() if v > 0]
        for e in S.ENG:
            S._wait(e, toks)

    def nextpf(self):
        t = self.pf[self.pfi % getattr(self, 'pf_lim', len(self.pf))]
        self.pfi += 1
        return t

    def nextpb(self):
        t = self.pb[self.pbi % len(self.pb)]
        self.pbi += 1
        return t

    def ffn_split(self):
        FC = self.cfg['F'] // 128
        nsplit = 2 if FC > 48 else 1
        FS = -(-FC // nsplit)
        return [(sp * FS, min(FC, (sp + 1) * FS)) for sp in range(nsplit)]

    def ffn_fgroups(self):
        out = []
        for sp, (lo, hi) in enumerate(self.ffn_split()):
            for fg in range(lo, hi, 8):
                out.append((sp, fg, min(8, hi - fg)))
        return out

    def ffn_gugroups(self):
        out = []
        for sp, (lo, hi) in enumerate(self.ffn_split()):
            for fg in range(lo, hi, 2):
                out.append((sp, fg, min(2, hi - fg)))
        return out

    def cast_blk(self, dst, src, dname):
        self.s.dma('pool', dst, src.rearrange("(c p) n -> p c n", p=128), w=[self.db(dname, 0)])

    def stage_cast(self):
        I = self.ins
        c = self.cfg
        D, F = c['D'], c['F']
        for l_, (wn, L_) in enumerate((('ev_w_in', self.EL), ('od_w_in', self.OL))):
            for i, (kind, nm, p0, pw) in enumerate(self.pieces[l_]):
                c0 = L_[nm][0] + p0
                self.cast_blk(self.scr['wb_in%d' % l_][i, :, :, 0:pw], I[wn][0][:, c0:c0 + pw], 'wb_in%d' % l_)
        for l in range(c['depth']):
            for g in range(D // 512):
                self.cast_blk(self.scr['wb_out%d' % l][g], I['w_out'][l][:, g * 512:(g + 1) * 512], 'wb_out%d' % l)
            for g, (sp, fg, nf) in enumerate(self.ffn_gugroups()):
                self.cast_blk(self.scr['wb_g%d' % l][g, :, :, 0:nf * 128], I['ffn_w_gate'][l][:, fg * 128:(fg + nf) * 128], 'wb_g%d' % l)
                self.cast_blk(self.scr['wb_u%d' % l][g, :, :, 0:nf * 128], I['ffn_w_up'][l][:, fg * 128:(fg + nf) * 128], 'wb_u%d' % l)
            for gi, (sp, fg, nf) in enumerate(self.ffn_fgroups()):
                for ng in range(D // 512):
                    self.cast_blk(self.scr['wb_d%d' % l][gi, ng, :, 0:nf, :], I['ffn_w_down'][l][fg * 128:(fg + nf) * 128, ng * 512:(ng + 1) * 512],
                                  'wb_d%d' % l)

    def stage_init_x(self):
        T, C = self.cfg['T'], self.cfg['C']
        for t0 in range(0, T, 512):
            self.s.dma('sp', self.scr['xres'][t0:t0 + 512, :], self.ins['x'][t0:t0 + 512, :],
                       w=self.dbs('xres', t0 // 128, t0 // 128 + 4))
        self.s.dma('sp', self.scr['xres'][T:T + C, :], self.ins['ctx'][:, :], w=self.dbs('xres', T // 128, (T + C) // 128))

    def stage_mod(self):
        c = self.cfg
        D, dep = c['D'], c['depth']
        KC = D // 128
        S = self.s
        nc = self.nc
        cv = self.sb('mod_cv', [2, D], F32)
        scT = self.sb('mod_scT', [128, KC, 2], F32)
        S.dma('sp', cv[:], self.ins['cvec'][:, :], w=[cv.b])
        S.op('act', lambda e: e.activation(out=cv[:], in_=cv[:], func=AF.Silu), r=[cv.b], w=[cv.b])
        for g in range(0, KC, 64):
            n = min(64, KC - g)
            ps = self.nextpf()
            for k in range(n):
                S.op('pe', lambda e, k=k: e.transpose(out=ps[:, 2 * k:2 * k + 2], in_=cv[0:2, (g + k) * 128:(g + k + 1) * 128],
                                                      identity=self.ident[0:2, 0:2]), r=[cv.b, self.ident.b], w=[ps.b])
            S.op('dve', lambda e: e.tensor_copy(out=scT[:, g:g + n, :].rearrange("p a b -> p (a b)"), in_=ps[:, 0:2 * n]),
                 r=[ps.b], w=[scT.b])
        wt = [self.sb('mod_w%d' % k, [128, 2048], F32) for k in range(3)]
        res = self.sb('mod_res', [2, 2048], F32)
        bia = self.sb('mod_bias', [2, 2048], F32)
        wi = 0
        for l in range(dep):
            for n0 in range(0, 6 * D, 2048):
                pss = [self.nextpf() for _ in range(4)]
                for kc in range(KC):
                    w = wt[wi % 3]
                    wi += 1
                    S.dma('sp', w[:], self.ins['ada_w'][l, kc * 128:(kc + 1) * 128, n0:n0 + 2048], w=[w.b])
                    for j in range(4):
                        S.op('pe', lambda e, j=j, w=w, kc=kc: e.matmul(pss[j][0:2, :], lhsT=scT[:, kc, :], rhs=w[:, j * 512:(j + 1) * 512],
                                                                      start=(kc == 0), stop=(kc == KC - 1)),
                             r=[scT.b, w.b], w=[pss[j].b])
                for r_ in range(2):
                    S.dma('sp', bia[r_:r_ + 1, :], self.ins['ada_b'][l:l + 1, n0:n0 + 2048], w=[bia.b])
                for j in range(4):
                    S.op('dve', lambda e, j=j: e.tensor_tensor(out=res[:, j * 512:(j + 1) * 512], in0=pss[j][0:2, :],
                                                               in1=bia[:, j * 512:(j + 1) * 512], op=ALU.add),
                         r=[pss[j].b, bia.b], w=[res.b])
                S.dma('sp', self.scr['mod'][l, :, n0:n0 + 2048], res[:], r=[res.b], w=[self.db('mod', l)])
        self.barrier()
        self._es.close()
        self.stage_begin()
        sc_ = self.sb('mod_sc', [2, D], F32)
        g = self.sb('mod_g', [2, D], F32)
        for l in range(dep):
            for k, (gn, sci, shi, gti) in enumerate((('norm_mix_g', 1, 0, 2), ('norm_ffn_g', 4, 3, 5))):
                S.dma('sp', sc_[:], self.scr['mod'][l, :, sci * D:(sci + 1) * D], r=[self.db('mod', l)], w=[sc_.b])
                for r_ in range(2):
                    S.dma('sp', g[r_:r_ + 1, :], self.ins[gn][l:l + 1, :], w=[g.b])
                S.op('dve', lambda e: e.scalar_tensor_tensor(out=sc_[:], in0=sc_[:], scalar=1.0, in1=g[:], op0=ALU.add, op1=ALU.mult),
                     r=[sc_.b, g.b], w=[sc_.b])
                S.dma('sp', self.scr['modv'][l, 3 * k], sc_[:], r=[sc_.b], w=[self.db('modv', l)])
                S.dma('sp', self.scr['modv'][l, 3 * k + 1], self.scr['mod'][l, :, shi * D:(shi + 1) * D], r=[self.db('mod', l)], w=[self.db('modv', l)])
                S.dma('sp', self.scr['modv'][l, 3 * k + 2], self.scr['mod'][l, :, gti * D:(gti + 1) * D], r=[self.db('mod', l)], w=[self.db('modv', l)])

    def load_bcast(self, tile, l, which, row):
        src = self.scr['modv'][l, which, row:row + 1, :].partition_broadcast(128)
        self.s.dma('sp', tile[:], src, r=[self.db('modv', l)], w=[tile.b])

    def stage_norm(self, l, which, tiles, final=False):
        c = self.cfg
        D, T = c['D'], c['T']
        KC = D // 128
        S = self.s
        tag = 'n%d%d%d' % (l, which, int(final))
        A1 = self.sb(tag + 'A', [128, D], F32)
        S1 = self.sb(tag + 'S', [128, D], F32)
        A = [A1, A1]
        Sh = [S1, S1]
        currow = -1
        if final:
            S.dma('sp', A[0][:], self.ins['final_norm_g'][None, :].partition_broadcast(128), w=[A[0].b])
        xt = [self.sb(tag + 'x%d' % k, [128, D], F32) for k in range(2)]
        junk = self.sb(tag + 'j', [128, D], BF16)
        hb = [self.sb(tag + 'h%d' % k, [128, D], BF16) for k in range(2)]
        hf = [self.sb(tag + 'f%d' % k, [128, D], F32) for k in range(2)] if final else None
        ht = [self.sb(tag + 't%d' % k, [128, KC, 512], BF16) for k in range(2)]
        ss = [self.sb(tag + 's%d' % k, [128, 1], F32) for k in range(2)]
        rs = [self.sb(tag + 'r%d' % k, [128, 1], F32) for k in range(2)]
        for n, tt in enumerate(tiles):
            row = 0 if tt * 128 < T else 1
            if not final and row != currow:
                currow = row
                self.load_bcast(A1, l, 3 * which, row)
                self.load_bcast(S1, l, 3 * which + 1, row)
            x, h, t_, s_, r_ = xt[n % 2], hb[n % 2], ht[(tt // 4) % 2], ss[n % 2], rs[n % 2]
            tj = tt % 4
            S.dma('sp', x[:], self.scr['xres'][tt * 128:(tt + 1) * 128, :], r=[self.db('xres', tt)], w=[x.b])
            S.op('act', lambda e, x=x, s_=s_: e.activation(out=junk[:], in_=x[:], func=AF.Square, accum_out=s_[:]),
                 r=[x.b], w=[junk.b, s_.b])
            S.op('act', lambda e, s_=s_, r_=r_: e.activation(out=r_[:], in_=s_[:], func=AF.Sqrt, scale=1.0 / D, bias=1e-6),
                 r=[s_.b], w=[r_.b])
            S.op('dve', lambda e, r_=r_: e.reciprocal(out=r_[:], in_=r_[:]), r=[r_.b], w=[r_.b])
            if final:
                f = hf[n % 2]
                S.op('dve', lambda e, x=x, r_=r_, f=f: e.scalar_tensor_tensor(out=f[:], in0=x[:], scalar=r_[:], in1=A[0][:],
                                                                              op0=ALU.mult, op1=ALU.mult), r=[x.b, r_.b, A[0].b], w=[f.b])
                S.dma('pool', self.out[tt * 128:(tt + 1) * 128, :], f[:], r=[f.b], w=[self.db('out', tt)])
                continue
            S.op('dve', lambda e, x=x, r_=r_, row=row: e.scalar_tensor_tensor(out=x[:], in0=x[:], scalar=r_[:], in1=A[row][:],
                                                                              op0=ALU.mult, op1=ALU.mult), r=[x.b, r_.b, A[row].b], w=[x.b])
            S.op('pool', lambda e, x=x, h=h, row=row: e.tensor_tensor(out=h[:], in0=x[:], in1=Sh[row][:], op=ALU.add),
                 r=[x.b, Sh[row].b], w=[h.b])
            for g in range(0, KC, 8):
                ps = self.nextpb()
                for k in range(8):
                    S.op('pe', lambda e, k=k, g=g, h=h, ps=ps: e.transpose(out=ps[:, k * 128:(k + 1) * 128], in_=h[:, (g + k) * 128:(g + k + 1) * 128],
                                                                          identity=self.identb[:]), r=[h.b, self.identb.b], w=[ps.b])
                for k in range(8):
                    S.op('act' if k % 2 else 'dve',
                         lambda e, g=g, k=k, ps=ps, t_=t_, tj=tj: (e.copy if e is self.nc.scalar else e.tensor_copy)(
                             out=t_[:, g + k, tj * 128:(tj + 1) * 128], in_=ps[:, k * 128:(k + 1) * 128]), r=[ps.b], w=[t_.b])
            if tj == 3 or n == len(tiles) - 1 or tiles[n + 1] // 4 != tt // 4:
                nn = (tj + 1) * 128
                S.dma('pool', self.scr['hT'][tt // 4, :, :, 0:nn], t_[:, :, 0:nn], r=[t_.b], w=[self.db('hT', tt // 4)])

    def tok_blocks(self, upto=None):
        c = self.cfg
        TA = c['TA'] if upto is None else upto
        return [(t0, min(512, TA - t0)) for t0 in range(0, TA, 512)]

    def stage_proj(self, l):
        c = self.cfg
        D, T = c['D'], c['T']
        KC = D // 128
        S = self.s
        even = (l % 2 == 0)
        L = self.EL if even else self.OL
        wbn = 'wb_in0' if even else 'wb_in1'
        wb = self.scr[wbn]
        fmn, tmn, ropen = (FM_EVEN, TM_EVEN, ROPE_EVEN) if even else (FM_ODD, TM_ODD, ROPE_ODD)
        self.fmidx, self.tmcol = {}, {}
        ci = 0
        for nm in fmn:
            for j in range(-(-L[nm][1] // 128)):
                self.fmidx[(nm, j)] = ci
                ci += 1
        co = 0
        for nm in tmn:
            self.tmcol[nm] = co
            co += L[nm][1]
        pieces = [(i,) + pc for i, pc in enumerate(self.pieces[l % 2])]
        hblk = [self.sb('pj_h%d' % k, [128, KC, 512], BF16) for k in range(2)]
        wt = [self.sb('pj_w%d' % k, [128, KC, 512], BF16) for k in range(2)]
        perm = self.sb('pj_perm', [128, 128], F32)
        cs = self.sb('pj_cos', [128, 512], F32)
        sn = self.sb('pj_sin', [128, 512], F32)
        q32 = [self.sb('pj_q%d' % k, [128, 512], F32) for k in range(2)]
        ta = [self.sb('pj_ta%d' % k, [128, 512], F32) for k in range(2)]
        tb_ = [self.sb('pj_tb%d' % k, [128, 512], F32) for k in range(2)]
        ob = [self.sb('pj_o%d' % k, [128, 512], BF16) for k in range(3)]
        S.dma('sp', perm[:], self.ins['k_perm'][:, :], w=[perm.b])
        wi = 0
        oi = 0
        qi = 0
        for bi, (t0, n) in enumerate(self.tok_blocks()):
            hb = hblk[bi % 2]
            latent = t0 < T
            S.dma('sp', hb[:, :, 0:n], self.scr['hT'][t0 // 512, :, :, 0:n], r=[self.db('hT', t0 // 512)], w=[hb.b])
            if latent:
                S.dma('sp', cs[:, 0:n], self.ins['k_cos'][:, t0:t0 + n], w=[cs.b])
                S.dma('sp', sn[:, 0:n], self.ins['k_sin'][:, t0:t0 + n], w=[sn.b])
            for pi, kind, nm, p0, pw in pieces:
                w = wt[wi % 2]
                wi += 1
                c0 = L[nm][0] + p0
                S.dma('sp', w[:, :, 0:pw], wb[pi, :, :, 0:pw], r=[self.db(wbn, 0)], w=[w.b])
                if kind == 'fm':
                    for j in range(-(-pw // 128)):
                        cw = min(128, pw - j * 128)
                        ps = self.nextpf()
                        for kc in range(KC):
                            S.op('pe', lambda e, kc=kc, j=j, cw=cw, w=w, ps=ps: e.matmul(
                                ps[0:cw, 0:n], lhsT=w[:, kc, j * 128:j * 128 + cw], rhs=hb[:, kc, 0:n], start=(kc == 0), stop=(kc == KC - 1)),
                                r=[w.b, hb.b], w=[ps.b])
                        o = ob[oi % 3]
                        oi += 1
                        if nm in ropen and latent:
                            q = q32[qi % 2]; a_ = ta[qi % 2]; b_ = tb_[qi % 2]
                            qi += 1
                            S.op('act', lambda e, q=q, ps=ps: e.copy(out=q[:, 0:n], in_=ps[:, 0:n]), r=[ps.b], w=[q.b])
                            ps2 = self.nextpf()
                            S.op('pe', lambda e, q=q, ps2=ps2: e.matmul(ps2[:, 0:n], lhsT=perm[:, :], rhs=q[:, 0:n], start=True, stop=True),
                                 r=[perm.b, q.b], w=[ps2.b])
                            S.op('pool', lambda e, q=q, a_=a_: e.tensor_tensor(out=a_[:, 0:n], in0=q[:, 0:n], in1=cs[:, 0:n], op=ALU.mult),
                                 r=[q.b, cs.b], w=[a_.b])
                            S.op('dve', lambda e, ps2=ps2, b_=b_: e.tensor_tensor(out=b_[:, 0:n], in0=ps2[:, 0:n], in1=sn[:, 0:n], op=ALU.mult),
                                 r=[ps2.b, sn.b], w=[b_.b])
                            S.op('dve', lambda e, a_=a_, b_=b_, o=o: e.tensor_tensor(out=o[:, 0:n], in0=a_[:, 0:n], in1=b_[:, 0:n], op=ALU.add),
                                 r=[a_.b, b_.b], w=[o.b])
                        else:
                            if oi % 2:
                                S.op('act', lambda e, o=o, ps=ps, cw=cw: e.copy(out=o[0:cw, 0:n], in_=ps[0:cw, 0:n]), r=[ps.b], w=[o.b])
                            else:
                                S.op('dve', lambda e, o=o, ps=ps, cw=cw: e.tensor_copy(out=o[0:cw, 0:n], in_=ps[0:cw, 0:n]), r=[ps.b], w=[o.b])
                        ch = self.fmidx[(nm, (p0 // 128) + j)]
                        S.dma('pool', self.scr['pT'][ch, 0:cw, t0:t0 + n], o[0:cw, 0:n], r=[o.b], w=[self.db('pT', ch)])
                else:
                    for j in range(n // 128):
                        ps = self.nextpf()
                        for kc in range(KC):
                            S.op('pe', lambda e, kc=kc, j=j, w=w, ps=ps: e.matmul(
                                ps[:, 0:pw], lhsT=hb[:, kc, j * 128:(j + 1) * 128], rhs=w[:, kc, 0:pw], start=(kc == 0), stop=(kc == KC - 1)),
                                r=[w.b, hb.b], w=[ps.b])
                        o = ob[oi % 3]
                        oi += 1
                        if oi % 2:
                            S.op('act', lambda e, o=o, ps=ps: e.copy(out=o[:, 0:pw], in_=ps[:, 0:pw]), r=[ps.b], w=[o.b])
                        else:
                            S.op('dve', lambda e, o=o, ps=ps: e.tensor_copy(out=o[:, 0:pw], in_=ps[:, 0:pw]), r=[ps.b], w=[o.b])
                        cc = self.tmcol[nm] + p0
                        S.dma('pool', self.scr['pV'][t0 + j * 128:t0 + (j + 1) * 128, cc:cc + pw], o[:, 0:pw], r=[o.b],
                              w=[self.db('pV', (t0 // 128) + j)])

    def attn_work(self, tag, nkmax, dv, nbuf=1):
        nblk = -(-nkmax // 128)
        W = []
        for k in range(nbuf):
            W.append(dict(
                S=self.sb(tag + 'S%d' % k, [128, nkmax + 1], F32), P=self.sb(tag + 'P%d' % k, [128, nkmax + 1], BF16),
                PT=self.sb(tag + 'PT%d' % k, [128, nblk, 128], BF16), m=self.sb(tag + 'm%d' % k, [128, 1], F32),
                ss=self.sb(tag + 'ss%d' % k, [128, 1], F32), O=self.sb(tag + 'O%d' % k, [128, dv], F32)))
        return W

    def attn_unit(self, W, qT, qb, parts, scale, dv, sink=None):
        S = self.s
        Ssb, P, PT, m, ss, O = W['S'], W['P'], W['PT'], W['m'], W['ss'], W['O']
        off = 0
        ei = 0
        for (kT, kbuf, bias, bbuf, vlist) in parts:
            n_all = kT.shape[1]
            for c0 in range(0, n_all, 512):
                n = min(512, n_all - c0)
                ps = self.nextpf()
                S.op('pe', lambda e, ps=ps, c0=c0, n=n, kT=kT: e.matmul(ps[:, 0:n], lhsT=qT, rhs=kT[:, c0:c0 + n], start=True, stop=True),
                     r=[qb, kbuf], w=[ps.b])
                if bias is not None:
                    S.op('dve', lambda e, ps=ps, c0=c0, n=n, off=off, bias=bias: e.scalar_tensor_tensor(
                        out=Ssb[:, off:off + n], in0=ps[:, 0:n], scalar=scale, in1=bias[:, c0:c0 + n], op0=ALU.mult, op1=ALU.add),
                        r=[ps.b, bbuf], w=[Ssb.b])
                elif ei % 2 == 0:
                    S.op('act', lambda e, ps=ps, n=n, off=off: e.mul(out=Ssb[:, off:off + n], in_=ps[:, 0:n], mul=scale), r=[ps.b], w=[Ssb.b])
                else:
                    S.op('dve', lambda e, ps=ps, n=n, off=off: e.tensor_scalar(out=Ssb[:, off:off + n], in0=ps[:, 0:n], scalar1=scale,
                                                                             scalar2=None, op0=ALU.mult), r=[ps.b], w=[Ssb.b])
                ei += 1
                off += n
        nk = off
        ncol = nk
        if sink is not None:
            S.op('dve', lambda e: e.tensor_copy(out=Ssb[:, nk:nk + 1], in_=sink[0][:, 0:1]), r=[sink[1]], w=[Ssb.b])
            ncol = nk + 1
        S.op('dve', lambda e: e.reduce_max(out=m[:], in_=Ssb[:, 0:ncol], axis=AX.X), r=[Ssb.b], w=[m.b])
        S.op('dve', lambda e: e.tensor_scalar(out=m[:], in0=m[:], scalar1=-1.0, scalar2=None, op0=ALU.mult), r=[m.b], w=[m.b])
        S.op('act', lambda e: e.activation(out=P[:, 0:ncol], in_=Ssb[:, 0:ncol], func=AF.Exp, bias=m[:], scale=1.0, accum_out=ss[:]),
             r=[Ssb.b, m.b], w=[P.b, ss.b])
        S.op('dve', lambda e: e.reciprocal(out=ss[:], in_=ss[:]), r=[ss.b], w=[ss.b])
        vall = []
        off = 0
        for (kT, kbuf, bias, bbuf, vlist) in parts:
            for (vap, kn, vbuf) in vlist:
                vall.append((vap, kn, vbuf, off))
                off += kn
        for g in range(0, len(vall), 8):
            grp = vall[g:g + 8]
            pb = self.nextpb()
            for k, (vap, kn, vbuf, o_) in enumerate(grp):
                S.op('pe', lambda e, k=k, kn=kn, o_=o_, pb=pb: e.transpose(out=pb[0:kn, k * 128:(k + 1) * 128], in_=P[:, o_:o_ + kn],
                                                                          identity=self.identb[:]), r=[P.b, self.identb.b], w=[pb.b])
            eng = 'act' if (g // 8) % 2 else 'dve'
            S.op(eng, lambda e, g=g, pb=pb, ng=len(grp): (e.copy if e is self.nc.scalar else e.tensor_copy)(
                out=PT[:, g:g + ng, :].rearrange("p a b -> p (a b)"), in_=pb[:, 0:ng * 128]), r=[pb.b], w=[PT.b])
        po = self.nextpf()
        for k, (vap, kn, vbuf, o_) in enumerate(vall):
            S.op('pe', lambda e, k=k, kn=kn, vap=vap: e.matmul(po[:, 0:dv], lhsT=PT[0:kn, k, :], rhs=vap, start=(k == 0), stop=(k == len(vall) - 1)),
                 r=[PT.b, vbuf], w=[po.b])
        S.op('act', lambda e: e.activation(out=O[:, 0:dv], in_=po[:, 0:dv], func=AF.Identity, scale=ss[:]), r=[po.b, ss.b], w=[O.b])

    def attn_work2(self, tag, nkmax, dv, nbuf=2):
        nblk = -(-nkmax // 128)
        nch = -(-nkmax // 512)
        W = []
        for k in range(nbuf):
            W.append(dict(
                P=self.sb(tag + 'P%d' % k, [128, nkmax], BF16), PT=self.sb(tag + 'PT%d' % k, [128, nblk, 128], BF16),
                mc=self.sb(tag + 'mc%d' % k, [128, nch], F32), sc=self.sb(tag + 'sc%d' % k, [128, nch], F32),
                m=self.sb(tag + 'm%d' % k, [128, 1], F32), ss=self.sb(tag + 'ss%d' % k, [128, 1], F32),
                O=self.sb(tag + 'O%d' % k, [128, dv], F32)))
        return W

    def attn_unit2(self, W, qT, qb, kT, kbuf, vlist, scale, dv):
        S = self.s
        P, PT, mc, sc, m, ss, O = W['P'], W['PT'], W['mc'], W['sc'], W['m'], W['ss'], W['O']
        nk = kT.shape[1]
        chunks = [(c0, min(512, nk - c0)) for c0 in range(0, nk, 512)]
        nch = len(chunks)
        for ci, (c0, n) in enumerate(chunks):
            ps = self.nextpf()
            S.op('pe', lambda e, ps=ps, c0=c0, n=n: e.matmul(ps[:, 0:n], lhsT=qT, rhs=kT[:, c0:c0 + n], start=True, stop=True),
                 r=[qb, kbuf], w=[ps.b])
            S.op('dve', lambda e, ps=ps, n=n, ci=ci: e.reduce_max(out=mc[:, ci:ci + 1], in_=ps[:, 0:n], axis=AX.X), r=[ps.b], w=[mc.b])
        S.op('dve', lambda e: e.reduce_max(out=m[:], in_=mc[:, 0:nch], axis=AX.X), r=[mc.b], w=[m.b])
        S.op('dve', lambda e: e.tensor_scalar(out=m[:], in0=m[:], scalar1=-scale, scalar2=None, op0=ALU.mult), r=[m.b], w=[m.b])
        for ci, (c0, n) in enumerate(chunks):
            ps = self.nextpf()
            S.op('pe', lambda e, ps=ps, c0=c0, n=n: e.matmul(ps[:, 0:n], lhsT=qT, rhs=kT[:, c0:c0 + n], start=True, stop=True),
                 r=[qb, kbuf], w=[ps.b])
            S.op('act', lambda e, ps=ps, c0=c0, n=n, ci=ci: e.activation(out=P[:, c0:c0 + n], in_=ps[:, 0:n], func=AF.Exp, bias=m[:], scale=scale,
                                                                        accum_out=sc[:, ci:ci + 1]), r=[ps.b, m.b], w=[P.b, sc.b])
        S.op('dve', lambda e: e.reduce_sum(out=ss[:], in_=sc[:, 0:nch], axis=AX.X), r=[sc.b], w=[ss.b])
        S.op('dve', lambda e: e.reciprocal(out=ss[:], in_=ss[:]), r=[ss.b], w=[ss.b])
        off = 0
        vall = []
        for (vap, kn, vbuf) in vlist:
            vall.append((vap, kn, vbuf, off))
            off += kn
        for g in range(0, len(vall), 8):
            grp = vall[g:g + 8]
            pb = self.nextpb()
            for k, (vap, kn, vbuf, o_) in enumerate(grp):
                S.op('pe', lambda e, k=k, kn=kn, o_=o_, pb=pb: e.transpose(out=pb[0:kn, k * 128:(k + 1) * 128], in_=P[:, o_:o_ + kn],
                                                                          identity=self.identb[:]), r=[P.b, self.identb.b], w=[pb.b])
            eng = 'act' if (g // 8) % 3 == 2 else 'dve'
            S.op(eng, lambda e, g=g, pb=pb, ng=len(grp): (e.copy if e is self.nc.scalar else e.tensor_copy)(
                out=PT[:, g:g + ng, :].rearrange("p a b -> p (a b)"), in_=pb[:, 0:ng * 128]), r=[pb.b], w=[PT.b])
        po = self.nextpf()
        for k, (vap, kn, vbuf, o_) in enumerate(vall):
            S.op('pe', lambda e, k=k, kn=kn, vap=vap: e.matmul(po[:, 0:dv], lhsT=PT[0:kn, k, :], rhs=vap, start=(k == 0), stop=(k == len(vall) - 1)),
                 r=[PT.b, vbuf], w=[po.b])
        S.op('act', lambda e: e.activation(out=O[:, 0:dv], in_=po[:, 0:dv], func=AF.Identity, scale=ss[:]), r=[po.b, ss.b], w=[O.b])

    def attn_pair2(self, Ws, qTs, qbs, kTs, kbufs, vlist, scale, dv):
        S = self.s
        nk = kTs[0].shape[1]
        chunks = [(c0, min(512, nk - c0)) for c0 in range(0, nk, 512)]
        nch = len(chunks)
        U = range(len(Ws))
        for u in U:
            W = Ws[u]
            for ci, (c0, n) in enumerate(chunks):
                ps = self.nextpf()
                S.op('pe', lambda e, ps=ps, c0=c0, n=n, u=u: e.matmul(ps[:, 0:n], lhsT=qTs[u], rhs=kTs[u][:, c0:c0 + n], start=True, stop=True),
                     r=[qbs[u], kbufs[u]], w=[ps.b])
                S.op('dve', lambda e, ps=ps, n=n, ci=ci, W=W: e.reduce_max(out=W['mc'][:, ci:ci + 1], in_=ps[:, 0:n], axis=AX.X), r=[ps.b], w=[W['mc'].b])
        for u in U:
            W = Ws[u]
            S.op('dve', lambda e, W=W: e.reduce_max(out=W['m'][:], in_=W['mc'][:, 0:nch], axis=AX.X), r=[W['mc'].b], w=[W['m'].b])
            S.op('dve', lambda e, W=W: e.tensor_scalar(out=W['m'][:], in0=W['m'][:], scalar1=-scale, scalar2=None, op0=ALU.mult), r=[W['m'].b], w=[W['m'].b])
        for u in U:
            W = Ws[u]
            for ci, (c0, n) in enumerate(chunks):
                ps = self.nextpf()
                S.op('pe', lambda e, ps=ps, c0=c0, n=n, u=u: e.matmul(ps[:, 0:n], lhsT=qTs[u], rhs=kTs[u][:, c0:c0 + n], start=True, stop=True),
                     r=[qbs[u], kbufs[u]], w=[ps.b])
                S.op('act', lambda e, ps=ps, c0=c0, n=n, ci=ci, W=W: e.activation(out=W['P'][:, c0:c0 + n], in_=ps[:, 0:n], func=AF.Exp, bias=W['m'][:], scale=scale,
                                                                                 accum_out=W['sc'][:, ci:ci + 1]), r=[ps.b, W['m'].b], w=[W['P'].b, W['sc'].b])
        for u in U:
            W = Ws[u]
            S.op('dve', lambda e, W=W: e.reduce_sum(out=W['ss'][:], in_=W['sc'][:, 0:nch], axis=AX.X), r=[W['sc'].b], w=[W['ss'].b])
            S.op('dve', lambda e, W=W: e.reciprocal(out=W['ss'][:], in_=W['ss'][:]), r=[W['ss'].b], w=[W['ss'].b])
        vall = []
        off = 0
        for (vap, kn, vbuf) in vlist:
            vall.append((vap, kn, vbuf, off))
            off += kn
        for g in range(0, len(vall), 8):
            grp = vall[g:g + 8]
            for u in U:
                W = Ws[u]
                pb = self.nextpb()
                for k, (vap, kn, vbuf, o_) in enumerate(grp):
                    S.op('pe', lambda e, k=k, kn=kn, o_=o_, pb=pb, W=W: e.transpose(out=pb[0:kn, k * 128:(k + 1) * 128], in_=W['P'][:, o_:o_ + kn],
                                                                                   identity=self.identb[:]), r=[W['P'].b, self.identb.b], w=[pb.b])
                eng = 'act' if u % 2 else 'dve'
                S.op(eng, lambda e, g=g, pb=pb, ng=len(grp), W=W: (e.copy if e is self.nc.scalar else e.tensor_copy)(
                    out=W['PT'][:, g:g + ng, :].rearrange("p a b -> p (a b)"), in_=pb[:, 0:ng * 128]), r=[pb.b], w=[W['PT'].b])
        for u in U:
            W = Ws[u]
            po = self.nextpf()
            for k, (vap, kn, vbuf, o_) in enumerate(vall):
                S.op('pe', lambda e, k=k, kn=kn, vap=vap, W=W, po=po: e.matmul(po[:, 0:dv], lhsT=W['PT'][0:kn, k, :], rhs=vap, start=(k == 0), stop=(k == len(vall) - 1)),
                     r=[W['PT'].b, vbuf], w=[po.b])
            S.op('act', lambda e, W=W, po=po: e.activation(out=W['O'][:, 0:dv], in_=po[:, 0:dv], func=AF.Identity, scale=W['ss'][:]), r=[po.b, W['ss'].b], w=[W['O'].b])

    def attn2_light(self, W, qT, qb, kT, kbuf, scale):
        S = self.s
        nk = kT.shape[1]
        chunks = [(c0, min(512, nk - c0)) for c0 in range(0, nk, 512)]
        nch = len(chunks)
        L = []

        def p1(ci, c0, n):
            ps = self.nextpf()
            S.op('pe', lambda e: e.matmul(ps[:, 0:n], lhsT=qT, rhs=kT[:, c0:c0 + n], start=True, stop=True), r=[qb, kbuf], w=[ps.b])
            S.op('dve', lambda e: e.reduce_max(out=W['mc'][:, ci:ci + 1], in_=ps[:, 0:n], axis=AX.X), r=[ps.b], w=[W['mc'].b])

        def pm():
            S.op('dve', lambda e: e.reduce_max(out=W['m'][:], in_=W['mc'][:, 0:nch], axis=AX.X), r=[W['mc'].b], w=[W['m'].b])
            S.op('dve', lambda e: e.tensor_scalar(out=W['m'][:], in0=W['m'][:], scalar1=-scale, scalar2=None, op0=ALU.mult), r=[W['m'].b], w=[W['m'].b])

        def p2(ci, c0, n):
            ps = self.nextpf()
            S.op('pe', lambda e: e.matmul(ps[:, 0:n], lhsT=qT, rhs=kT[:, c0:c0 + n], start=True, stop=True), r=[qb, kbuf], w=[ps.b])
            S.op('act', lambda e: e.activation(out=W['P'][:, c0:c0 + n], in_=ps[:, 0:n], func=AF.Exp, bias=W['m'][:], scale=scale,
                                               accum_out=W['sc'][:, ci:ci + 1]), r=[ps.b, W['m'].b], w=[W['P'].b, W['sc'].b])

        def psum_():
            S.op('dve', lambda e: e.reduce_sum(out=W['ss'][:], in_=W['sc'][:, 0:nch], axis=AX.X), r=[W['sc'].b], w=[W['ss'].b])
            S.op('dve', lambda e: e.reciprocal(out=W['ss'][:], in_=W['ss'][:]), r=[W['ss'].b], w=[W['ss'].b])

        for ci, (c0, n) in enumerate(chunks):
            L.append(lambda ci=ci, c0=c0, n=n: p1(ci, c0, n))
        L.append(pm)
        for ci, (c0, n) in enumerate(chunks):
            L.append(lambda ci=ci, c0=c0, n=n: p2(ci, c0, n))
        L.append(psum_)
        return L

    def attn2_heavy(self, W, vlist, dv, cpeng):
        S = self.s
        vall = []
        off = 0
        for (vap, kn, vbuf) in vlist:
            vall.append((vap, kn, vbuf, off))
            off += kn
        H = []

        def tg(g):
            grp = vall[g:g + 8]
            pb = self.nextpb()
            for k, (vap, kn, vbuf, o_) in enumerate(grp):
                S.op('pe', lambda e, k=k, kn=kn, o_=o_: e.transpose(out=pb[0:kn, k * 128:(k + 1) * 128], in_=W['P'][:, o_:o_ + kn],
                                                                   identity=self.identb[:]), r=[W['P'].b, self.identb.b], w=[pb.b])
            S.op(cpeng, lambda e: (e.copy if e is self.nc.scalar else e.tensor_copy)(
                out=W['PT'][:, g:g + len(grp), :].rearrange("p a b -> p (a b)"), in_=pb[:, 0:len(grp) * 128]), r=[pb.b], w=[W['PT'].b])

        st = {}

        def pv(k0, k1):
            if k0 == 0:
                self.poi = getattr(self, 'poi', 0) + 1
                st['po'] = self.pf[4 + self.poi % 2]
            po = st['po']
            for k in range(k0, k1):
                vap, kn, vbuf, o_ = vall[k]
                S.op('pe', lambda e, k=k, kn=kn, vap=vap: e.matmul(po[:, 0:dv], lhsT=W['PT'][0:kn, k, :], rhs=vap, start=(k == 0), stop=(k == len(vall) - 1)),
                     r=[W['PT'].b, vbuf], w=[po.b])
            if k1 == len(vall):
                S.op('act', lambda e: e.activation(out=W['O'][:, 0:dv], in_=po[:, 0:dv], func=AF.Identity, scale=W['ss'][:]), r=[po.b, W['ss'].b], w=[W['O'].b])

        for g in range(0, len(vall), 8):
            H.append(lambda g=g: tg(g))
        for k0 in range(0, len(vall), 3):
            H.append(lambda k0=k0: pv(k0, min(len(vall), k0 + 3)))
        return H

    def zip_emit(self, L, H):
        per = -(-len(H) // max(1, len(L)))
        hi = 0
        for l in L:
            l()
            for _ in range(per):
                if hi < len(H):
                    H[hi]()
                    hi += 1
        while hi < len(H):
            H[hi]()
            hi += 1

    def load_fm(self, tile, nm, j, q='sp'):
        ch = self.fmidx[(nm, j)]
        self.s.dma(q, tile[:], self.scr['pT'][ch, :, :], r=[self.db('pT', ch)], w=[tile.b])

    def load_tm(self, tile, nm, c0, w, q='sp'):
        cc = self.tmcol[nm] + c0
        ntl = self.cfg['TA'] // 128
        self.s.dma(q, tile[:], self.scr['pV'][:, cc:cc + w].rearrange("(n p) c -> p n c", p=128),
                   r=self.dbs('pV', 0, ntl), w=[tile.b])

    def store_y(self, O, ob, tt, col, w):
        S = self.s
        S.op('dve', lambda e: e.tensor_copy(out=ob[:, 0:w], in_=O[:, 0:w]), r=[O.b], w=[ob.b])
        S.dma('pool', self.scr['y_tm'][tt * 128:(tt + 1) * 128, col:col + w], ob[:, 0:w], r=[ob.b], w=[self.db('y_tm', tt)])

    def stage_swa(self, l):
        c = self.cfg
        T, C, TA, D = c['T'], c['C'], c['TA'], c['D']
        S = self.s
        ntl, nlat = TA // 128, T // 128
        scale = 128.0 ** -0.5
        kT = [self.sb('sw_k%d' % k, [128, TA], BF16) for k in range(2)]
        V = [self.sb('sw_v%d' % k, [128, ntl, 128], BF16) for k in range(2)]
        qT = [self.sb('sw_q%d' % k, [128, TA], BF16) for k in range(2)]
        sk = [self.sb('sw_s%d' % k, [128, 1], F32) for k in range(2)]
        mask = self.sb('sw_mask', [128, 384], F32)
        ob = [self.sb('sw_ob%d' % k, [128, 128], BF16) for k in range(2)]
        S.dma('sp', mask[:], self.ins['k_swamask'][:, :], w=[mask.b])
        W = self.attn_work('sw', C + 384, 128, nbuf=2)
        u = 0
        for g in range(c['SKV']):
            k_, v_ = kT[g % 2], V[g % 2]
            self.load_fm(k_, 'sk', g)
            self.load_tm(v_, 'sv', g * 128, 128)
            for hh in range(4):
                h = g * 4 + hh
                q_, s_ = qT[h % 2], sk[h % 2]
                self.load_fm(q_, 'sq', h)
                S.dma('sp', s_[:], self.ins['swa_sink'][0:1, h:h + 1].partition_broadcast(128), w=[s_.b])
                for tt in range(ntl):
                    ctxpart = (k_[:, T:TA], k_.b, None, None, [(v_[:, nlat + j, :], 128, v_.b) for j in range(C // 128)])
                    if tt < nlat:
                        lo, hi = max(tt - 1, 0), min(tt + 2, nlat)
                        m0 = (lo - (tt - 1)) * 128
                        parts = [ctxpart, (k_[:, lo * 128:hi * 128], k_.b, mask[:, m0:m0 + (hi - lo) * 128], mask.b,
                                           [(v_[:, j, :], 128, v_.b) for j in range(lo, hi)])]
                    else:
                        parts = [ctxpart]
                    w_ = W[u % 2]
                    self.attn_unit(w_, q_[:, tt * 128:(tt + 1) * 128], q_.b, parts, scale, 128, sink=(s_, s_.b))
                    self.store_y(w_['O'], ob[u % 2], tt, D // 2 + h * 128, 128)
                    u += 1

    def stage_gla(self, l):
        c = self.cfg
        T, C, TA, D, GH = c['T'], c['C'], c['TA'], c['D'], c['GH']
        S = self.s
        I = self.ins
        qT = self.sb('gl_q', [128, TA], BF16)
        kT = self.sb('gl_k', [128, TA], BF16)
        lg = [self.sb('gl_l%d' % d, [128, TA], F32) for d in range(2)]
        dn1 = self.sb('gl_dn', [16, TA], BF16)
        dn = [dn1, dn1]
        upf = self.sb('gl_upf', [16, 128], F32)
        upb = self.sb('gl_upb', [16, 128], BF16)
        nb = self.sb('gl_nb', [128, 1], F32)
        et = self.sb('gl_et', [128, 512], F32)
        tri = self.sb('gl_tri', [64, 2, 64], F32)
        ones = self.sb('gl_ones', [128, 64], F32)
        ng = self.sb('gl_ng', [64, 256], F32)
        St = self.sb('gl_S', [128, 256], F32)
        Sb = self.sb('gl_Sb', [128, 256], BF16)
        vg = [self.sb('gl_v%d' % k, [64, 8, 256], BF16) for k in range(2)]
        rg = [self.sb('gl_r%d' % k, [64, 8, 256], BF16) for k in range(2)]
        og = [self.sb('gl_o%d' % k, [64, 8, 256], F32) for k in range(2)]
        yg = [self.sb('gl_y%d' % k, [64, 8, 256], BF16) for k in range(2)]
        cc = [self.sb('gl_c%d' % k, [128, 64], F32) for k in range(2)]
        c2 = [self.sb('gl_c2%d' % k, [128, 64], F32) for k in range(2)]
        ncl = [self.sb('gl_ncl%d' % k, [128, 1], F32) for k in range(2)]
        ebl = [self.sb('gl_ebl%d' % k, [128, 1], F32) for k in range(2)]
        eb = [self.sb('gl_eb%d' % k, [128, 64], F32) for k in range(2)]
        ei = [self.sb('gl_ei%d' % k, [128, 64], F32) for k in range(2)]
        eu = [self.sb('gl_eu%d' % k, [128, 64], F32) for k in range(2)]
        qd = [self.sb('gl_qd%d' % k, [128, 64], BF16) for k in range(2)]
        ki = [self.sb('gl_ki%d' % k, [128, 64], BF16) for k in range(2)]
        ku = [self.sb('gl_ku%d' % k, [128, 64], BF16) for k in range(2)]
        kut = [self.sb('gl_kut%d' % k, [64, 128], BF16) for k in range(2)]
        at = [self.sb('gl_at%d' % k, [64, 64], BF16) for k in range(2)]
        ot = [self.sb('gl_ot%d' % k, [64, 256], F32) for k in range(2)]
        sr = [None, None]
        srg = [self.sb('gl_srg%d' % k, [64, 8, 256], F32) for k in range(2)]
        epst = self.sb('gl_eps', [128, 1], F32)
        S.op('pool', lambda e: e.memset(epst[:], 1e-6), w=[epst.b])
        jk = self.sb('gl_jk', [64, 256], F32)
        ss = [self.sb('gl_ss%d' % k, [64, 1], F32) for k in range(2)]
        S.dma('sp', tri[:], I['k_tri'].rearrange("d j i -> j d i"), w=[tri.b])
        S.op('pool', lambda e: e.memset(ones[:], 1.0), w=[ones.b])
        S.dma('sp', ng[:], I['gla_norm_g'][0:1, :].partition_broadcast(64), w=[ng.b])
        gcv = self.tmcol['gv']
        gcr = self.tmcol['gr']
        groups = [(T + g0, min(8, (C - g0) // 64)) for g0 in range(0, C, 512)] + [(g0, 8) for g0 in range(0, T, 512)]
        n = 0
        for h in range(GH):
            self.load_fm(qT, 'gq', h)
            self.load_fm(kT, 'gk', h)
            for d, (un, bn) in enumerate((('gla_gate_up_f', 'gla_gate_bias_f'), ('gla_gate_up_b', 'gla_gate_bias_b'))):
                ch = self.fmidx[(('dnf', 'dnb')[d], 0)]
                S.dma('sp', dn[d][:], self.scr['pT'][ch, 0:16, :], r=[self.db('pT', ch)], w=[dn[d].b])
                S.dma('sp', upf[:], I[un][0, :, h * 128:(h + 1) * 128], w=[upf.b])
                S.op('dve', lambda e: e.tensor_copy(out=upb[:], in_=upf[:]), r=[upf.b], w=[upb.b])
                S.dma('sp', nb[:], I[bn][0, h * 128:(h + 1) * 128].rearrange("(p o) -> p o", o=1), w=[nb.b])
                S.op('dve', lambda e: e.tensor_scalar(out=nb[:], in0=nb[:], scalar1=-1.0, scalar2=None, op0=ALU.mult), r=[nb.b], w=[nb.b])
                for (t0, nt) in self.tok_blocks():
                    ps = self.nextpf()
                    S.op('pe', lambda e, ps=ps, d=d, t0=t0, nt=nt: e.matmul(ps[:, 0:nt], lhsT=upb[:, :], rhs=dn[d][:, t0:t0 + nt], start=True, stop=True),
                         r=[upb.b, dn[d].b], w=[ps.b])
                    S.op('act', lambda e, ps=ps, nt=nt: e.activation(out=et[:, 0:nt], in_=ps[:, 0:nt], func=AF.Exp, bias=nb[:], scale=-1.0),
                         r=[ps.b, nb.b], w=[et.b])
                    S.op('act', lambda e, d=d, t0=t0, nt=nt: e.activation(out=lg[d][:, t0:t0 + nt], in_=et[:, 0:nt], func=AF.Ln, bias=1.0, scale=1.0),
                         r=[et.b], w=[lg[d].b])
            for d in range(2):
                S.op('pool', lambda e: e.memset(St[:], 0.0), w=[St.b])
                S.op('pool', lambda e: e.memset(Sb[:], 0.0), w=[Sb.b])
                glist = groups if d == 0 else [groups[i] for i in list(range(len(groups) - 1, -1, -1))]
                if d == 1:
                    nctx = -(-C // 512)
                    glist = groups[:nctx][::-1] + groups[nctx:][::-1]
                for gi, (g0, gn) in enumerate(glist):
                    v_, r_, o_, y_ = vg[gi % 2], rg[gi % 2], og[gi % 2], yg[gi % 2]
                    rows = slice(g0, g0 + gn * 64)
                    tl = list(range(g0 // 128, -(-(g0 + gn * 64) // 128)))
                    S.dma('sp', v_[:, 0:gn, :], self.scr['pV'][rows, gcv + h * 256:gcv + (h + 1) * 256].rearrange("(n p) c -> p n c", p=64),
                          r=[self.db('pV', t) for t in tl], w=[v_.b])
                    if d == 1:
                        S.dma('sp', r_[:, 0:gn, :], self.scr['pV'][rows, gcr + h * 256:gcr + (h + 1) * 256].rearrange("(n p) c -> p n c", p=64),
                              r=[self.db('pV', t) for t in tl], w=[r_.b])
                        S.dma('sp', o_[:, 0:gn, :], self.scr['gla_o'][rows, h * 256:(h + 1) * 256].rearrange("(n p) c -> p n c", p=64),
                              r=[self.db('gla_o', t) for t in tl], w=[o_.b])
                        srg_ = srg[gi % 2]
                        S.op('act', lambda e, srg_=srg_, r_=r_, gn=gn: e.activation(out=srg_[:, 0:gn, :], in_=r_[:, 0:gn, :], func=AF.Silu), r=[r_.b], w=[srg_.b])
                    korder = range(gn) if d == 0 else range(gn - 1, -1, -1)
                    for k in korder:
                        t0 = g0 + k * 64
                        i2 = n % 2
                        n += 1
                        c_, c2_, ncl_, ebl_, eb_, ei_, eu_ = cc[i2], c2[i2], ncl[i2], ebl[i2], eb[i2], ei[i2], eu[i2]
                        qd_, ki_, ku_, kut_, at_, ot_, sr_, ss_ = qd[i2], ki[i2], ku[i2], kut[i2], at[i2], ot[i2], sr[i2], ss[i2]
                        lch = lg[d][:, t0:t0 + 64]
                        S.op('dve', lambda e, c_=c_, lch=lch: e.tensor_tensor_scan(out=c_[:], data0=ones[:], data1=lch, initial=0.0,
                                                                                  op0=ALU.mult, op1=ALU.add), r=[ones.b, lg[d].b], w=[c_.b])
                        S.op('dve', lambda e, c_=c_, ncl_=ncl_: e.tensor_scalar(out=ncl_[:], in0=c_[:, 63:64], scalar1=-1.0 / 16, scalar2=None, op0=ALU.mult),
                             r=[c_.b], w=[ncl_.b])
                        if d == 0:
                            cu = c_
                        else:
                            S.op('dve', lambda e, c_=c_, c2_=c2_, lch=lch: e.scalar_tensor_tensor(out=c2_[:], in0=c_[:], scalar=-1.0, in1=lch,
                                                                                                   op0=ALU.mult, op1=ALU.add), r=[c_.b, lg[d].b], w=[c2_.b])
                            S.op('dve', lambda e, c_=c_, c2_=c2_: e.tensor_scalar(out=c2_[:], in0=c2_[:], scalar1=c_[:, 63:64], scalar2=None, op0=ALU.add),
                                 r=[c_.b, c2_.b], w=[c2_.b])
                            cu = c2_
                        S.op('act', lambda e, cu=cu, eb_=eb_: e.activation(out=eb_[:], in_=cu[:], func=AF.Exp, scale=-1.0 / 16), r=[cu.b], w=[eb_.b])
                        S.op('act', lambda e, cu=cu, ei_=ei_: e.activation(out=ei_[:], in_=cu[:], func=AF.Exp, scale=1.0 / 16), r=[cu.b], w=[ei_.b])
                        S.op('act', lambda e, cu=cu, eu_=eu_, ncl_=ncl_: e.activation(out=eu_[:], in_=cu[:], func=AF.Exp, scale=1.0 / 16, bias=ncl_[:]),
                             r=[cu.b, ncl_.b], w=[eu_.b])
                        S.op('act', lambda e, ebl_=ebl_, ncl_=ncl_: e.activation(out=ebl_[:], in_=ncl_[:], func=AF.Exp), r=[ncl_.b], w=[ebl_.b])
                        S.op('dve', lambda e, qd_=qd_, eb_=eb_, t0=t0: e.scalar_tensor_tensor(out=qd_[:], in0=qT[:, t0:t0 + 64], scalar=128.0 ** -0.5, in1=eb_[:],
                                                                                              op0=ALU.mult, op1=ALU.mult), r=[qT.b, eb_.b], w=[qd_.b])
                        S.op('pool', lambda e, ki_=ki_, ei_=ei_, t0=t0: e.tensor_tensor(out=ki_[:], in0=kT[:, t0:t0 + 64], in1=ei_[:], op=ALU.mult),
                             r=[kT.b, ei_.b], w=[ki_.b])
                        S.op('pool', lambda e, ku_=ku_, eu_=eu_, t0=t0: e.tensor_tensor(out=ku_[:], in0=kT[:, t0:t0 + 64], in1=eu_[:], op=ALU.mult),
                             r=[kT.b, eu_.b], w=[ku_.b])
                        pa = self.nextpf()
                        S.op('pe', lambda e, pa=pa, ki_=ki_, qd_=qd_: e.matmul(pa[0:64, 0:64], lhsT=ki_[:, :], rhs=qd_[:, :], start=True, stop=True),
                             r=[ki_.b, qd_.b], w=[pa.b])
                        S.op('dve', lambda e, pa=pa, at_=at_, d=d: e.tensor_tensor(out=at_[:], in0=pa[0:64, 0:64], in1=tri[:, d, :], op=ALU.mult),
                             r=[pa.b, tri.b], w=[at_.b])
                        po = self.nextpf()
                        S.op('pe', lambda e, po=po, at_=at_, v_=v_, k=k: e.matmul(po[0:64, 0:256], lhsT=at_[:, :], rhs=v_[:, k, :], start=True, stop=False),
                             r=[at_.b, v_.b], w=[po.b])
                        S.op('pe', lambda e, po=po, qd_=qd_: e.matmul(po[0:64, 0:256], lhsT=qd_[:, :], rhs=Sb[:, :], start=False, stop=True),
                             r=[qd_.b, Sb.b], w=[po.b])
                        if d == 0:
                            S.op('act', lambda e, po=po, o_=o_, k=k: e.copy(out=o_[:, k, :], in_=po[0:64, 0:256]), r=[po.b], w=[o_.b])
                        else:
                            S.op('dve', lambda e, po=po, o_=o_, ot_=ot_, k=k: e.tensor_tensor(out=ot_[:], in0=po[0:64, 0:256], in1=o_[:, k, :], op=ALU.add),
                                 r=[po.b, o_.b], w=[ot_.b])
                            S.op('pool', lambda e, ot_=ot_: e.tensor_tensor(out=jk[:], in0=ot_[:], in1=ot_[:], op=ALU.mult), r=[ot_.b], w=[jk.b])
                            S.op('dve', lambda e, ss_=ss_: e.reduce_sum(out=ss_[:], in_=jk[:], axis=AX.X), r=[jk.b], w=[ss_.b])
                            S.op('act', lambda e, ss_=ss_: e.activation(out=ss_[:], in_=ss_[:], func=AF.Ln, scale=1.0 / 256, bias=epst[0:64, :]), r=[ss_.b, epst.b], w=[ss_.b])
                            S.op('act', lambda e, ss_=ss_: e.activation(out=ss_[:], in_=ss_[:], func=AF.Exp, scale=-0.5), r=[ss_.b], w=[ss_.b])
                            S.op('dve', lambda e, ot_=ot_, ss_=ss_: e.scalar_tensor_tensor(out=ot_[:], in0=ot_[:], scalar=ss_[:], in1=ng[:], op0=ALU.mult, op1=ALU.mult),
                                 r=[ot_.b, ss_.b, ng.b], w=[ot_.b])
                            S.op('pool', lambda e, ot_=ot_, srg_=srg_, y_=y_, k=k: e.tensor_tensor(out=y_[:, k, :], in0=ot_[:], in1=srg_[:, k, :], op=ALU.mult),
                                 r=[ot_.b, srg_.b], w=[y_.b])
                        pt = self.nextpb()
                        S.op('pe', lambda e, pt=pt, ku_=ku_: e.transpose(out=pt[0:64, 0:128], in_=ku_[:, :], identity=self.identb[:]),
                             r=[ku_.b, self.identb.b], w=[pt.b])
                        S.op('act', lambda e, pt=pt, kut_=kut_: e.copy(out=kut_[:], in_=pt[0:64, 0:128]), r=[pt.b], w=[kut_.b])
                        pd = self.nextpf()
                        S.op('pe', lambda e, pd=pd, kut_=kut_, v_=v_, k=k: e.matmul(pd[:, 0:256], lhsT=kut_[:, :], rhs=v_[:, k, :], start=True, stop=True),
                             r=[kut_.b, v_.b], w=[pd.b])
                        S.op('dve', lambda e, pd=pd, ebl_=ebl_: e.scalar_tensor_tensor(out=St[:], in0=St[:], scalar=ebl_[:], in1=pd[:, 0:256], op0=ALU.mult, op1=ALU.add),
                             r=[St.b, ebl_.b, pd.b], w=[St.b])
                        S.op('act', lambda e: e.copy(out=Sb[:], in_=St[:]), r=[St.b], w=[Sb.b])
                    if d == 0:
                        S.dma('pool', self.scr['gla_o'][rows, h * 256:(h + 1) * 256].rearrange("(n p) c -> p n c", p=64), o_[:, 0:gn, :],
                              r=[o_.b], w=[self.db('gla_o', t) for t in tl])
                    else:
                        S.dma('pool', self.scr['y_tm'][rows, h * 256:(h + 1) * 256].rearrange("(n p) c -> p n c", p=64), y_[:, 0:gn, :],
                              r=[y_.b], w=[self.db('y_tm', t) for t in tl])

    def stage_mix_even(self, l):
        self.stage_gla(l)
        self.barrier()
        self._es.close()
        self.stage_begin()
        self.stage_swa(l)

    def stage_na(self, l):
        c = self.cfg
        T, C, TA, D, NH = c['T'], c['C'], c['TA'], c['D'], c['NH']
        S = self.s
        ntl, nlat = TA // 128, T // 128
        scale = 128.0 ** -0.5
        kT = [self.sb('na_k%d' % k, [128, TA], BF16) for k in range(2)]
        V = [self.sb('na_v%d' % k, [128, ntl, 128], BF16) for k in range(2)]
        qT = [self.sb('na_q%d' % k, [128, TA], BF16) for k in range(2)]
        bt = [self.sb('na_b%d' % k, [128, 5, 640], F32) for k in range(2)]
        ob = [self.sb('na_ob%d' % k, [128, 128], BF16) for k in range(2)]
        W = self.attn_work('na', C + 640, 128, nbuf=2)
        u = 0
        for h in range(NH):
            k_, v_, q_, b_ = kT[h % 2], V[h % 2], qT[h % 2], bt[h % 2]
            self.load_fm(k_, 'nk', h)
            self.load_tm(v_, 'nv', h * 128, 128)
            self.load_fm(q_, 'nq', h)
            S.dma('sp', b_[:], self.ins['na_bias'][:, h, :, :].rearrange("v q k -> q v k"), w=[b_.b])
            for tt in range(nlat):
                vi, lo = na_variant(c, 2 * tt)
                k0 = lo * 64
                parts = [(k_[:, T:TA], k_.b, None, None, [(v_[:, nlat + j, :], 128, v_.b) for j in range(C // 128)]),
                         (k_[:, k0:k0 + 640], k_.b, b_[:, vi, :], b_.b, [(v_[:, k0 // 128 + j, :], 128, v_.b) for j in range(5)])]
                w_ = W[u % 2]
                self.attn_unit(w_, q_[:, tt * 128:(tt + 1) * 128], q_.b, parts, scale, 128)
                self.store_y(w_['O'], ob[u % 2], tt, h * 128, 128)
                u += 1

    def stage_diff(self, l):
        c = self.cfg
        T, C, TA, D, DH = c['T'], c['C'], c['TA'], c['D'], c['DH']
        S = self.s
        I = self.ins
        ntl, nlat = TA // 128, T // 128
        scale = 128.0 ** -0.5
        lam_init = 0.8 - 0.6 * math.exp(-0.3 * l)
        kT = [self.sb('df_k%d' % k, [128, TA], BF16) for k in range(2)]
        qT = [self.sb('df_q%d' % k, [128, TA], BF16) for k in range(2)]
        V = self.sb('df_v', [128, ntl, 256], BF16)
        ng = self.sb('df_ng', [128, 256], F32)
        lv = [self.sb('df_l%d' % k, [128, 128], F32) for k in range(4)]
        lj = self.sb('df_lj', [128, 128], F32)
        la = [self.sb('df_la%d' % k, [128, 1], F32) for k in range(2)]
        nlam = self.sb('df_nlam', [128, 1], F32)
        od = self.sb('df_od', [128, 256], F32)
        jk = self.sb('df_jk', [128, 256], F32)
        ss = self.sb('df_ss', [128, 1], F32)
        ob = [self.sb('df_ob%d' % k, [128, 256], BF16) for k in range(2)]
        epst = self.sb('df_eps', [128, 1], F32)
        S.op('pool', lambda e: e.memset(epst[:], 1e-6), w=[epst.b])
        W = self.attn_work2('df', TA, 256, nbuf=2)
        S.dma('sp', ng[:], I['diff_norm_g'][0:1, :].partition_broadcast(128), w=[ng.b])
        for k, nm in enumerate(('diff_lq1', 'diff_lk1', 'diff_lq2', 'diff_lk2')):
            S.dma('sp', lv[k][:], I[nm][0:1, :].partition_broadcast(128), w=[lv[k].b])
        for k in range(2):
            S.op('dve', lambda e, k=k: e.tensor_tensor(out=lj[:], in0=lv[2 * k][:], in1=lv[2 * k + 1][:], op=ALU.mult),
                 r=[lv[2 * k].b, lv[2 * k + 1].b], w=[lj.b])
            S.op('dve', lambda e, k=k: e.reduce_sum(out=la[k][:], in_=lj[:], axis=AX.X), r=[lj.b], w=[la[k].b])
            S.op('act', lambda e, k=k: e.activation(out=la[k][:], in_=la[k][:], func=AF.Exp), r=[la[k].b], w=[la[k].b])
        S.op('dve', lambda e: e.tensor_tensor(out=nlam[:], in0=la[1][:], in1=la[0][:], op=ALU.subtract), r=[la[0].b, la[1].b], w=[nlam.b])
        S.op('dve', lambda e: e.tensor_scalar(out=nlam[:], in0=nlam[:], scalar1=-lam_init, scalar2=None, op0=ALU.add), r=[nlam.b], w=[nlam.b])
        u = 0
        for h in range(DH):
            self.load_tm(V, 'dv', h * 256, 256)
            for s_ in range(2):
                self.load_fm(kT[s_], 'dk', 2 * h + s_)
                self.load_fm(qT[s_], 'dq', 2 * h + s_)
            vlist = [(V[:, j, :], 128, V.b) for j in range(ntl)]

            def combine(tt):
                nonlocal u
                S.op('dve', lambda e: e.scalar_tensor_tensor(out=od[:], in0=W[1]['O'][:], scalar=nlam[:], in1=W[0]['O'][:], op0=ALU.mult, op1=ALU.add),
                     r=[W[0]['O'].b, W[1]['O'].b, nlam.b], w=[od.b])
                S.op('pool', lambda e: e.tensor_tensor(out=jk[:], in0=od[:], in1=od[:], op=ALU.mult), r=[od.b], w=[jk.b])
                S.op('dve', lambda e: e.reduce_sum(out=ss[:], in_=jk[:], axis=AX.X), r=[jk.b], w=[ss.b])
                S.op('act', lambda e: e.activation(out=ss[:], in_=ss[:], func=AF.Ln, scale=1.0 / 256, bias=epst[:]), r=[ss.b, epst.b], w=[ss.b])
                S.op('act', lambda e: e.activation(out=ss[:], in_=ss[:], func=AF.Exp, scale=-0.5), r=[ss.b], w=[ss.b])
                S.op('dve', lambda e: e.scalar_tensor_tensor(out=od[:], in0=od[:], scalar=ss[:], in1=ng[:], op0=ALU.mult, op1=ALU.mult),
                     r=[od.b, ss.b, ng.b], w=[od.b])
                o_ = ob[u % 2]
                u += 1
                S.op('act', lambda e, o_=o_: e.mul(out=o_[:], in_=od[:], mul=1.0 - lam_init), r=[od.b], w=[o_.b])
                S.dma('pool', self.scr['y_tm'][tt * 128:(tt + 1) * 128, D // 2 + h * 256:D // 2 + (h + 1) * 256], o_[:], r=[o_.b], w=[self.db('y_tm', tt)])

            prev = []
            self.pf_lim = 4
            for tt in range(nlat):
                for s_ in range(2):
                    Lt = self.attn2_light(W[s_], qT[s_][:, tt * 128:(tt + 1) * 128], qT[s_].b, kT[s_][:, 0:TA], kT[s_].b, scale)
                    self.zip_emit(Lt, prev)
                    prev = self.attn2_heavy(W[s_], vlist, 256, 'act' if s_ else 'dve')
                    if s_ == 1:
                        prev.append(lambda tt=tt: combine(tt))
            self.zip_emit([], prev)
            self.pf_lim = 6

    def stage_mix_odd(self, l):
        self.stage_na(l)
        self.barrier()
        self._es.close()
        self.stage_begin()
        self.stage_diff(l)

    def resid_update(self, ps, n0, nw, tt, Gt, xp, tmp):
        S = self.s
        xr = self.scr['xres'][tt * 128:(tt + 1) * 128, n0:n0 + nw]
        S.dma('sp', xp[:, 0:nw], xr, r=[self.db('xres', tt)], w=[xp.b])
        S.op('dve', lambda e: e.tensor_tensor(out=tmp[:, 0:nw], in0=ps[:, 0:nw], in1=Gt[:, 0:nw], op=ALU.mult), r=[ps.b, Gt.b], w=[tmp.b])
        S.op('pool', lambda e: e.tensor_tensor(out=xp[:, 0:nw], in0=xp[:, 0:nw], in1=tmp[:, 0:nw], op=ALU.add), r=[xp.b, tmp.b], w=[xp.b])
        S.dma('pool', xr, xp[:, 0:nw], r=[xp.b], w=[self.db('xres', tt)])

    def blocks_of(self, tiles):
        nlat = self.cfg['T'] // 128
        out, cur = [], []
        for tt in tiles:
            if cur and (len(cur) == 4 or tt != cur[-1] + 1 or (tt == nlat)):
                out.append(cur)
                cur = []
            cur.append(tt)
        if cur:
            out.append(cur)
        return out

    def stage_wout(self, l, tiles):
        c = self.cfg
        D, T = c['D'], c['T']
        KC = D // 128
        S = self.s
        wbn = 'wb_out%d' % l
        wb = self.scr[wbn]
        yt = [self.sb('wo_y%d' % k, [128, D], BF16) for k in range(2)]
        yT = self.sb('wo_yT', [128, KC, 512], BF16)
        wt = [self.sb('wo_w%d' % k, [128, KC, 512], BF16) for k in range(2)]
        Gt = [self.sb('wo_G%d' % k, [128, 512], F32) for k in range(2)]
        xp = [self.sb('wo_x%d' % k, [128, 512], F32) for k in range(4)]
        tmp = [self.sb('wo_t%d' % k, [128, 512], F32) for k in range(4)]
        yi = wi = xi = 0
        for blk in self.blocks_of(tiles):
            row = 0 if blk[0] * 128 < T else 1
            for j, tt in enumerate(blk):
                y_ = yt[yi % 2]
                yi += 1
                S.dma('sp', y_[:], self.scr['y_tm'][tt * 128:(tt + 1) * 128, :], r=[self.db('y_tm', tt)], w=[y_.b])
                for g in range(0, KC, 8):
                    pb = self.nextpb()
                    for k in range(8):
                        S.op('pe', lambda e, k=k, g=g, y_=y_, pb=pb: e.transpose(out=pb[:, k * 128:(k + 1) * 128], in_=y_[:, (g + k) * 128:(g + k + 1) * 128],
                                                                              identity=self.identb[:]), r=[y_.b, self.identb.b], w=[pb.b])
                    for k in range(8):
                        eng = 'act' if k % 2 else 'dve'
                        S.op(eng, lambda e, k=k, g=g, j=j, pb=pb: (e.copy if e is self.nc.scalar else e.tensor_copy)(
                            out=yT[:, g + k, j * 128:(j + 1) * 128], in_=pb[:, k * 128:(k + 1) * 128]), r=[pb.b], w=[yT.b])
            for n0 in range(0, D, 512):
                w = wt[wi % 2]
                G_ = Gt[wi % 2]
                wi += 1
                S.dma('sp', w[:], wb[n0 // 512], r=[self.db(wbn, 0)], w=[w.b])
                S.dma('sp', G_[:], self.scr['modv'][l, 2, row:row + 1, n0:n0 + 512].partition_broadcast(128), r=[self.db('modv', l)], w=[G_.b])
                for j, tt in enumerate(blk):
                    ps = self.nextpf()
                    for kc in range(KC):
                        S.op('pe', lambda e, kc=kc, j=j, w=w, ps=ps: e.matmul(ps[:, :], lhsT=yT[:, kc, j * 128:(j + 1) * 128], rhs=w[:, kc, :],
                                                                             start=(kc == 0), stop=(kc == KC - 1)), r=[yT.b, w.b], w=[ps.b])
                    self.resid_update(ps, n0, 512, tt, G_, xp[xi % 4], tmp[xi % 4])
                    xi += 1

    def stage_ffn(self, l, tiles):
        c = self.cfg
        D, T, F = c['D'], c['T'], c['F']
        KC, FC = D // 128, F // 128
        S = self.s
        splits = self.ffn_split()
        nsplit = len(splits)
        FS = max(hi - lo for lo, hi in splits)
        gug = self.ffn_gugroups()
        dg = self.ffn_fgroups()
        hb = self.sb('ff_h', [128, KC, 512], BF16)
        aT = self.sb('ff_a', [128, FS, 512], BF16)
        wg = [self.sb('ff_g%d' % k, [128, KC, 256], BF16) for k in range(2)]
        wu = [self.sb('ff_u%d' % k, [128, KC, 256], BF16) for k in range(2)]
        wd = [self.sb('ff_d%d' % k, [128, 8, 512], BF16) for k in range(2)]
        sg = [self.sb('ff_s%d' % k, [128, 512], F32) for k in range(2)]
        Gt = [self.sb('ff_G%d' % k, [128, 512], F32) for k in range(2)]
        xp = [self.sb('ff_x%d' % k, [128, 512], F32) for k in range(4)]
        tmp = [self.sb('ff_t%d' % k, [128, 512], F32) for k in range(4)]
        wgs, wus, wds = self.scr['wb_g%d' % l], self.scr['wb_u%d' % l], self.scr['wb_d%d' % l]
        gi = di = si = xi = Gi = 0
        for blk in self.blocks_of(tiles):
            row = 0 if blk[0] * 128 < T else 1
            t0, n = blk[0] * 128, len(blk) * 128
            S.dma('sp', hb[:, :, 0:n], self.scr['hT'][blk[0] // 4, :, :, 0:n], r=[self.db('hT', blk[0] // 4)], w=[hb.b])
            for sp_ in range(nsplit):
                f_lo, f_hi = splits[sp_]
                for gidx, (gsp, fg, nf) in enumerate(gug):
                    if gsp != sp_:
                        continue
                    g_, u_ = wg[gi % 2], wu[gi % 2]
                    gi += 1
                    S.dma('sp', g_[:, :, 0:nf * 128], wgs[gidx, :, :, 0:nf * 128], r=[self.db('wb_g%d' % l, 0)], w=[g_.b])
                    S.dma('sp', u_[:, :, 0:nf * 128], wus[gidx, :, :, 0:nf * 128], r=[self.db('wb_u%d' % l, 0)], w=[u_.b])
                    for j in range(nf):
                        pg, pu = self.nextpf(), self.nextpf()
                        for kc in range(KC):
                            S.op('pe', lambda e, kc=kc, j=j, g_=g_, pg=pg: e.matmul(pg[:, 0:n], lhsT=g_[:, kc, j * 128:(j + 1) * 128], rhs=hb[:, kc, 0:n],
                                                                                   start=(kc == 0), stop=(kc == KC - 1)), r=[g_.b, hb.b], w=[pg.b])
                        for kc in range(KC):
                            S.op('pe', lambda e, kc=kc, j=j, u_=u_, pu=pu: e.matmul(pu[:, 0:n], lhsT=u_[:, kc, j * 128:(j + 1) * 128], rhs=hb[:, kc, 0:n],
                                                                                   start=(kc == 0), stop=(kc == KC - 1)), r=[u_.b, hb.b], w=[pu.b])
                        s_ = sg[si % 2]
                        si += 1
                        S.op('act', lambda e, s_=s_, pg=pg: e.activation(out=s_[:, 0:n], in_=pg[:, 0:n], func=AF.Silu), r=[pg.b], w=[s_.b])
                        S.op('dve', lambda e, s_=s_, pu=pu, fg=fg, j=j, f_lo=f_lo: e.tensor_tensor(out=aT[:, fg + j - f_lo, 0:n], in0=s_[:, 0:n], in1=pu[:, 0:n], op=ALU.mult),
                             r=[s_.b, pu.b], w=[aT.b])
                for n0 in range(0, D, 512):
                    G_ = Gt[Gi % 2]
                    Gi += 1
                    S.dma('sp', G_[:], self.scr['modv'][l, 5, row:row + 1, n0:n0 + 512].partition_broadcast(128), r=[self.db('modv', l)], w=[G_.b])
                    pss = [self.nextpf() for _ in blk]
                    for didx, (dsp, fg, nf) in enumerate(dg):
                        if dsp != sp_:
                            continue
                        d_ = wd[di % 2]
                        di += 1
                        S.dma('sp', d_[:, 0:nf, :], wds[didx, n0 // 512, :, 0:nf, :], r=[self.db('wb_d%d' % l, 0)], w=[d_.b])
                        for j in range(len(blk)):
                            for k in range(nf):
                                fc = fg + k
                                S.op('pe', lambda e, j=j, k=k, fc=fc, d_=d_: e.matmul(pss[j][:, :], lhsT=aT[:, fc - f_lo, j * 128:(j + 1) * 128], rhs=d_[:, k, :],
                                                                                     start=(fc == f_lo), stop=(fc == f_hi - 1)), r=[aT.b, d_.b], w=[pss[j].b])
                    for j, tt in enumerate(blk):
                        self.resid_update(pss[j], n0, 512, tt, G_, xp[xi % 4], tmp[xi % 4])
                        xi += 1

    def run_stage(self, fn, *a, **k):
        self.stage_begin()
        fn(*a, **k)
        self.stage_end()

    def build(self, upto='all'):
        c = self.cfg
        T, TA = c['T'], c['TA']
        self.declare()
        self.run_stage(self.stage_cast)
        self.run_stage(self.stage_init_x)
        self.run_stage(self.stage_mod)
        alltiles = list(range(TA // 128))
        lat = list(range(T // 128))
        if upto == 'mod':
            return self.finish()
        self.run_stage(self.stage_norm, 0, 0, alltiles)
        if upto == 'norm':
            return self.finish()
        for l in range(c['depth']):
            last = (l == c['depth'] - 1)
            self.run_stage(self.stage_proj, l)
            if upto == 'proj%d' % l:
                return self.finish()
            self.run_stage(self.stage_mix_even if l % 2 == 0 else self.stage_mix_odd, l)
            if upto == 'mix%d' % l:
                return self.finish()
            tiles = lat if last else alltiles
            self.run_stage(self.stage_wout, l, tiles)
            self.run_stage(self.stage_norm, l, 1, tiles)
            self.run_stage(self.stage_ffn, l, tiles)
            if upto == 'ffn%d' % l:
                return self.finish()
            if not last:
                self.run_stage(self.stage_norm, l + 1, 0, alltiles)
        self.run_stage(self.stage_norm, 0, 0, lat, final=True)
        return self.finish()

    def finish(self):
        bufs = [b for (nm, i), b in self.dbufs.items() if nm == 'out' or nm in self.debug]
        self.barrier()
        return self.nc


def host_consts(cfg):
    T = cfg['T']
    k = {}
    k['k_ident'] = np.eye(128, dtype=np.float32)
    pm = np.zeros((128, 128), np.float32)
    for m in range(128):
        h, i = divmod(m, 64)
        src = h * 64 + (i + 32) % 64
        pm[src, m] = 1.0
    k['k_perm'] = pm
    inv = (1.0 / (np.float32(10000.0) ** (np.arange(32, dtype=np.float32) / np.float32(32)))).astype(np.float32)
    pos = np.arange(T)
    row = (pos // 64).astype(np.float32)
    col = (pos % 64).astype(np.float32)
    cosT = np.zeros((128, T), np.float32)
    sinT = np.zeros((128, T), np.float32)
    for f in range(128):
        h, i = divmod(f, 64)
        j = i % 32
        ang = ((row if h == 0 else col) * inv[j]).astype(np.float32)
        cosT[f] = np.cos(ang).astype(np.float32)
        sg = -1.0 if i < 32 else 1.0
        sinT[f] = sg * np.sin(ang).astype(np.float32)
    k['k_cos'] = cosT
    k['k_sin'] = sinT
    qi = np.arange(128)[:, None]
    kj = np.arange(384)[None, :]
    k['k_swamask'] = np.where(np.abs(qi + 128 - kj) <= 128, 0.0, NEG).astype(np.float32)
    tri = np.zeros((2, 64, 64), np.float32)
    jj = np.arange(64)[:, None]
    ii = np.arange(64)[None, :]
    tri[0] = (ii >= jj)
    tri[1] = (ii <= jj)
    k['k_tri'] = tri
    return k


def na_bias_tables(cfg, rpb):
    T = cfg['T']
    rows = T // 64
    NH = rpb.shape[0]
    out = np.full((5, NH, 128, 640), NEG, np.float32)
    variants = [4, 0, 2, rows - 4, rows - 2]
    for vi, r0 in enumerate(variants):
        lo = min(max(r0 - 4, 0), rows - 10)
        for dq in range(2):
            r = r0 + dq
            rs = min(max(r - 4, 0), rows - 8)
            for cq in range(64):
                cst = min(max(cq - 8, 0), 64 - 16)
                q = dq * 64 + cq
                for kr in range(10):
                    ar = lo + kr
                    if not (rs <= ar < rs + 8):
                        continue
                    dr = ar - r + 7
                    kc = np.arange(cst, cst + 16)
                    dc = kc - cq + 15
                    out[vi, :, q, kr * 64 + kc] = rpb[:, dr, dc].T
    return out


def na_variant(cfg, r0):
    rows = cfg['T'] // 64
    if 4 <= r0 <= rows - 6:
        return 0, r0 - 4
    if r0 == 0:
        return 1, 0
    if r0 == 2:
        return 2, 0
    if r0 == rows - 4:
        return 3, rows - 10
    return 4, rows - 10


def make_in_maps(cfg, inputs, ncores):
    ks = host_consts(cfg)
    maps = []
    nab = na_bias_tables(cfg, np.asarray(inputs['na_rpb'])[0])
    for b in range(ncores):
        m = {}
        m['x'] = np.ascontiguousarray(inputs['x'][b])
        m['ctx'] = np.ascontiguousarray(inputs['ctx'][b])
        m['cvec'] = np.ascontiguousarray(np.stack([inputs['c'][b], inputs['c_ctx']]))
        for nm in ('ada_w', 'ada_b', 'norm_mix_g', 'norm_ffn_g', 'w_out', 'ffn_w_gate', 'ffn_w_up', 'ffn_w_down', 'ev_w_in',
                   'gla_gate_up_f', 'gla_gate_bias_f', 'gla_gate_up_b', 'gla_gate_bias_b', 'gla_norm_g', 'swa_sink', 'od_w_in',
                   'diff_lq1', 'diff_lk1', 'diff_lq2', 'diff_lk2', 'diff_norm_g', 'final_norm_g'):
            m[nm] = np.asarray(inputs[nm])
        m['na_bias'] = nab
        m.update(ks)
        maps.append(m)
    return maps


def kernel(**inputs):
    inputs = {k: np.asarray(v) for k, v in inputs.items()}
    B, T, D = inputs['x'].shape
    cfg = make_cfg(D=D, T=T, C=inputs['ctx'].shape[1], depth=inputs['ada_w'].shape[0])
    p = Prog(cfg)
    nc = p.build()
    maps = make_in_maps(cfg, inputs, B)
    res = run_bass_kernel_spmd(nc, maps, core_ids=list(range(B)))
    return np.stack([res.results[b]['out'] for b in range(B)]).astype(np.float32)
```

```python
import math
import numpy as np
import concourse.bass as bass
import concourse.mybir as mybir
from concourse.bass_utils import run_bass_kernel_spmd

F32 = mybir.dt.float32
BF16 = mybir.dt.bfloat16
AF = mybir.ActivationFunctionType
ALU = mybir.AluOpType
AX = mybir.AxisListType

NEG = -30000.0


def make_cfg(D=4096, T=8192, C=256, depth=2):
    cfg = dict(D=D, T=T, C=C, depth=depth, GRID_W=64, HD=128)
    half = D // 2
    cfg['GH'] = half // 256
    cfg['GQK'] = cfg['GH'] * 128
    cfg['SH'] = half // 128
    cfg['SKV'] = cfg['SH'] // 4
    cfg['NH'] = half // 128
    cfg['DH'] = half // 256
    cfg['F'] = -(-8 * D // (3 * 256)) * 256
    cfg['TA'] = T + C
    return cfg


class Buf:
    __slots__ = ('w', 'r', 'name')

    def __init__(self, name=''):
        self.w = None
        self.r = []
        self.name = name


class Sched:
    ENG = ('pe', 'act', 'dve', 'pool', 'sp')
    NS = 8

    def __init__(self, nc):
        self.nc = nc
        self.e = {'pe': nc.tensor, 'act': nc.scalar, 'dve': nc.vector, 'pool': nc.gpsimd, 'sp': nc.sync}
        self.sem = {k: nc.alloc_semaphore('sem_' + k) for k in self.ENG}
        self.cnt = {k: 0 for k in self.ENG}
        self.dsem = {}
        self.dtot = {}
        self.dn = {}
        for q in ('sp', 'pool', 'act'):
            self.dn[q] = 0
            for s in range(self.NS):
                self.dsem[(q, s)] = nc.alloc_semaphore('dsem_%s%d' % (q, s))
                self.dtot[(q, s)] = 0
        self.seen = {k: {} for k in self.ENG}
        self.ninstr = 0

    def _semh(self, key):
        return self.sem[key] if isinstance(key, str) else self.dsem[key]

    def _wait(self, eng, toks):
        need = {}
        for t in toks:
            if t is None:
                continue
            k, v = t
            if k == 'pe' and eng == 'pe':
                continue
            if need.get(k, 0) < v:
                need[k] = v
        seen = self.seen[eng]
        for k, v in need.items():
            if seen.get(k, 0) < v:
                self.e[eng].wait_ge(self._semh(k), v)
                seen[k] = v
                self.ninstr += 1

    def _deps(self, r, w):
        toks = []
        for b in r:
            toks.append(b.w)
        for b in w:
            toks.append(b.w)
            toks.extend(b.r)
        return toks

    def _commit(self, tok, r, w):
        for b in w:
            b.w = tok
            b.r = []
        for b in r:
            b.r.append(tok)
            if len(b.r) > 64:
                best = {}
                for k, v in b.r:
                    if best.get(k, 0) < v:
                        best[k] = v
                b.r = list(best.items())

    def op(self, eng, fn, r=(), w=()):
        self._wait(eng, self._deps(r, w))
        ins = fn(self.e[eng])
        ins.then_inc(self.sem[eng], 1)
        self.cnt[eng] += 1
        self.ninstr += 1
        self._commit((eng, self.cnt[eng]), r, w)

    def dma(self, q, out, in_, r=(), w=(), **kw):
        slot = self.dn[q] % self.NS
        self.dn[q] += 1
        key = (q, slot)
        toks = self._deps(r, w)
        if self.dtot[key] > 0:
            toks.append((key, self.dtot[key]))
        self._wait(q, toks)
        self.e[q].dma_start(out=out, in_=in_, **kw).then_inc(self.dsem[key], 16)
        self.dtot[key] += 16
        self.ninstr += 1
        self._commit((key, self.dtot[key]), r, w)

    def finish(self, bufs):
        toks = []
        for b in bufs:
            toks.append(b.w)
        self._wait('sp', toks)


class Tl:
    __slots__ = ('t', 'b')

    def __init__(self, t, name=''):
        self.t = t
        self.b = Buf(name)

    def __getitem__(self, k):
        return self.t[k]


class Builder:
    def __init__(self, cfg, debug=()):
        self.cfg = cfg
        self.debug = set(debug)
        self.nc = bass.Bass("TRN2", target_bir_lowering=False)
        self.s = Sched(self.nc)
        self.dbufs = {}
        self.ins = {}
        self.scr = {}
        self._stack = []

    def inp(self, name, shape, dt=F32):
        self.ins[name] = self.nc.dram_tensor(name, list(shape), dt, kind="ExternalInput").ap()
        return self.ins[name]

    def scratch(self, name, shape, dt):
        kind = "ExternalOutput" if name in self.debug else "Internal"
        self.scr[name] = self.nc.dram_tensor(name, list(shape), dt, kind=kind).ap()
        return self.scr[name]

    def db(self, name, idx=0):
        k = (name, idx)
        if k not in self.dbufs:
            self.dbufs[k] = Buf(str(k))
        return self.dbufs[k]

    def dbs(self, name, lo, hi):
        return [self.db(name, i) for i in range(lo, hi)]


def even_layout(cfg):
    GQK, GH, SH, SKV = cfg['GQK'], cfg['GH'], cfg['SH'], cfg['SKV']
    o = 0
    L = {}
    for nm, w in (('gq', GQK), ('gk', GQK), ('gv', GH * 256), ('gr', GH * 256), ('dnf', 16), ('dnb', 16),
                  ('sq', SH * 128), ('sk', SKV * 128), ('sv', SKV * 128)):
        L[nm] = (o, w)
        o += w
    L['_n'] = o
    return L


def odd_layout(cfg):
    NH, DH = cfg['NH'], cfg['DH']
    o = 0
    L = {}
    for nm, w in (('nq', NH * 128), ('nk', NH * 128), ('nv', NH * 128), ('dq', DH * 256), ('dk', DH * 256),
                  ('dv', DH * 256)):
        L[nm] = (o, w)
        o += w
    L['_n'] = o
    return L


FM_EVEN = ('gq', 'gk', 'dnf', 'dnb', 'sq', 'sk')
TM_EVEN = ('gv', 'gr', 'sv')
ROPE_EVEN = ('sq', 'sk')
FM_ODD = ('nq', 'nk', 'dq', 'dk')
TM_ODD = ('nv', 'dv')
ROPE_ODD = ('dq', 'dk')


class Prog(Builder):
    def declare(self):
        c = self.cfg
        D, T, C, F, dep = c['D'], c['T'], c['C'], c['F'], c['depth']
        EL, OL = even_layout(c), odd_layout(c)
        self.EL, self.OL = EL, OL
        i = self.inp
        i('x', [T, D]); i('ctx', [C, D]); i('cvec', [2, D])
        i('ada_w', [dep, D, 6 * D]); i('ada_b', [dep, 6 * D])
        i('norm_mix_g', [dep, D]); i('norm_ffn_g', [dep, D])
        i('w_out', [dep, D, D]); i('ffn_w_gate', [dep, D, F]); i('ffn_w_up', [dep, D, F]); i('ffn_w_down', [dep, F, D])
        i('ev_w_in', [1, D, EL['_n']]); i('gla_gate_up_f', [1, 16, c['GQK']]); i('gla_gate_bias_f', [1, c['GQK']])
        i('gla_gate_up_b', [1, 16, c['GQK']]); i('gla_gate_bias_b', [1, c['GQK']])
        i('gla_norm_g', [1, 256]); i('swa_sink', [1, c['SH']])
        i('od_w_in', [1, D, OL['_n']]); i('na_bias', [5, c['NH'], 128, 640])
        i('diff_lq1', [1, 128]); i('diff_lk1', [1, 128]); i('diff_lq2', [1, 128]); i('diff_lk2', [1, 128])
        i('diff_norm_g', [1, 256]); i('final_norm_g', [D])
        i('k_ident', [128, 128]); i('k_perm', [128, 128]); i('k_cos', [128, T]); i('k_sin', [128, T])
        i('k_swamask', [128, 384]); i('k_tri', [2, 64, 64])
        self.out = self.nc.dram_tensor('out', [T, D], F32, kind="ExternalOutput").ap()
        s = self.scratch
        TA = c['TA']
        s('xres', [TA, D], F32)
        s('hT', [-(-TA // 512), 128, D // 128, 512], BF16)
        s('y_tm', [TA, D], BF16)
        s('mod', [dep, 2, 6 * D], F32)
        s('modv', [dep, 6, 2, D], F32)
        self.pieces = {}
        for l_, (L_, fmn_, tmn_) in enumerate(((EL, FM_EVEN, TM_EVEN), (OL, FM_ODD, TM_ODD))):
            pcs = []
            for kind_, names_ in (('fm', fmn_), ('tm', tmn_)):
                for nm in names_:
                    for p0 in range(0, L_[nm][1], 512):
                        pcs.append((kind_, nm, p0, min(512, L_[nm][1] - p0)))
            self.pieces[l_] = pcs
            s('wb_in%d' % l_, [len(pcs), 128, D // 128, 512], BF16)
        for l in range(dep):
            s('wb_out%d' % l, [D // 512, 128, D // 128, 512], BF16)
            s('wb_g%d' % l, [len(self.ffn_gugroups()), 128, D // 128, 256], BF16); s('wb_u%d' % l, [len(self.ffn_gugroups()), 128, D // 128, 256], BF16)
            s('wb_d%d' % l, [len(self.ffn_fgroups()), D // 512, 128, 8, 512], BF16)
        nfm = max(sum(-(-EL[n][1] // 128) for n in FM_EVEN), sum(-(-OL[n][1] // 128) for n in FM_ODD))
        ntm = max(sum(EL[n][1] for n in TM_EVEN), sum(OL[n][1] for n in TM_ODD))
        s('pT', [nfm, 128, TA], BF16)
        s('pV', [TA, ntm], BF16)
        s('gla_o', [TA, c['GH'] * 256], F32)
        self.pf = [Tl(self.nc.alloc_psum_tensor('pf%d' % k, [128, 512], F32), 'pf%d' % k) for k in range(6)]
        self.pb = [Tl(self.nc.alloc_psum_tensor('pb%d' % k, [128, 1024], BF16), 'pb%d' % k) for k in range(2)]
        self.pfi = 0
        self.pbi = 0
        self.ident = self.sb('ident', [128, 128], F32, persist=True)
        self.identb = self.sb('identb', [128, 128], BF16, persist=True)
        self.s.dma('sp', self.ident[:], self.ins['k_ident'][:, :], w=[self.ident.b])
        self.s.op('dve', lambda e: e.tensor_copy(out=self.identb[:], in_=self.ident[:]), r=[self.ident.b], w=[self.identb.b])

    def sb(self, name, shape, dt, persist=False):
        self._uid = getattr(self, '_uid', 0) + 1
        nm = '%s_%d' % (name, self._uid)
        if persist:
            return Tl(self.nc.alloc_sbuf_tensor(nm, list(shape), dt), nm)
        return Tl(self._es.enter_context(self.nc.sbuf_tensor(nm, list(shape), dt)), nm)

    def stage_begin(self):
        import contextlib
        self._es = contextlib.ExitStack()

    def stage_end(self):
        self.barrier()
        self._es.close()

    def barrier(self):
        S = self.s
        toks = [(k, S.cnt[k]) for k in S.ENG if S.cnt[k] > 0]
        toks += [(k, v) for k, v in S.dtot.items() if v > 0]
        for e in S.ENG:
            S._wait(e, toks)

    def nextpf(self):
        t = self.pf[self.pfi % getattr(self, 'pf_lim', len(self.pf))]
        self.pfi += 1
        return t

    def nextpb(self):
        t = self.pb[self.pbi % len(self.pb)]
        self.pbi += 1
        return t

    def ffn_split(self):
        FC = self.cfg['F'] // 128
        nsplit = 2 if FC > 48 else 1
        FS = -(-FC // nsplit)
        return [(sp * FS, min(FC, (sp + 1) * FS)) for sp in range(nsplit)]

    def ffn_fgroups(self):
        out = []
        for sp, (lo, hi) in enumerate(self.ffn_split()):
            for fg in range(lo, hi, 8):
                out.append((sp, fg, min(8, hi - fg)))
        return out

    def ffn_gugroups(self):
        out = []
        for sp, (lo, hi) in enumerate(self.ffn_split()):
            for fg in range(lo, hi, 2):
                out.append((sp, fg, min(2, hi - fg)))
        return out

    def cast_blk(self, dst, src, dname):
        self.s.dma('pool', dst, src.rearrange("(c p) n -> p c n", p=128), w=[self.db(dname, 0)])

    def stage_cast(self):
        I = self.ins
        c = self.cfg
        D, F = c['D'], c['F']
        for l_, (wn, L_) in enumerate((('ev_w_in', self.EL), ('od_w_in', self.OL))):
            for i, (kind, nm, p0, pw) in enumerate(self.pieces[l_]):
                c0 = L_[nm][0] + p0
                self.cast_blk(self.scr['wb_in%d' % l_][i, :, :, 0:pw], I[wn][0][:, c0:c0 + pw], 'wb_in%d' % l_)
        for l in range(c['depth']):
            for g in range(D // 512):
                self.cast_blk(self.scr['wb_out%d' % l][g], I['w_out'][l][:, g * 512:(g + 1) * 512], 'wb_out%d' % l)
            for g, (sp, fg, nf) in enumerate(self.ffn_gugroups()):
                self.cast_blk(self.scr['wb_g%d' % l][g, :, :, 0:nf * 128], I['ffn_w_gate'][l][:, fg * 128:(fg + nf) * 128], 'wb_g%d' % l)
                self.cast_blk(self.scr['wb_u%d' % l][g, :, :, 0:nf * 128], I['ffn_w_up'][l][:, fg * 128:(fg + nf) * 128], 'wb_u%d' % l)
            for gi, (sp, fg, nf) in enumerate(self.ffn_fgroups()):
                for ng in range(D // 512):
                    self.cast_blk(self.scr['wb_d%d' % l][gi, ng, :, 0:nf, :], I['ffn_w_down'][l][fg * 128:(fg + nf) * 128, ng * 512:(ng + 1) * 512],
                                  'wb_d%d' % l)

    def stage_init_x(self):
        T, C = self.cfg['T'], self.cfg['C']
        for t0 in range(0, T, 512):
            self.s.dma('sp', self.scr['xres'][t0:t0 + 512, :], self.ins['x'][t0:t0 + 512, :],
                       w=self.dbs('xres', t0 // 128, t0 // 128 + 4))
        self.s.dma('sp', self.scr['xres'][T:T + C, :], self.ins['ctx'][:, :], w=self.dbs('xres', T // 128, (T + C) // 128))

    def stage_mod(self):
        c = self.cfg
        D, dep = c['D'], c['depth']
        KC = D // 128
        S = self.s
        nc = self.nc
        cv = self.sb('mod_cv', [2, D], F32)
        scT = self.sb('mod_scT', [128, KC, 2], F32)
        S.dma('sp', cv[:], self.ins['cvec'][:, :], w=[cv.b])
        S.op('act', lambda e: e.activation(out=cv[:], in_=cv[:], func=AF.Silu), r=[cv.b], w=[cv.b])
        for g in range(0, KC, 64):
            n = min(64, KC - g)
            ps = self.nextpf()
            for k in range(n):
                S.op('pe', lambda e, k=k: e.transpose(out=ps[:, 2 * k:2 * k + 2], in_=cv[0:2, (g + k) * 128:(g + k + 1) * 128],
                                                      identity=self.ident[0:2, 0:2]), r=[cv.b, self.ident.b], w=[ps.b])
            S.op('dve', lambda e: e.tensor_copy(out=scT[:, g:g + n, :].rearrange("p a b -> p (a b)"), in_=ps[:, 0:2 * n]),
                 r=[ps.b], w=[scT.b])
        wt = [self.sb('mod_w%d' % k, [128, 2048], F32) for k in range(3)]
        res = self.sb('mod_res', [2, 2048], F32)
        bia = self.sb('mod_bias', [2, 2048], F32)
        wi = 0
        for l in range(dep):
            for n0 in range(0, 6 * D, 2048):
                pss = [self.nextpf() for _ in range(4)]
                for kc in range(KC):
                    w = wt[wi % 3]
                    wi += 1
                    S.dma('sp', w[:], self.ins['ada_w'][l, kc * 128:(kc + 1) * 128, n0:n0 + 2048], w=[w.b])
                    for j in range(4):
                        S.op('pe', lambda e, j=j, w=w, kc=kc: e.matmul(pss[j][0:2, :], lhsT=scT[:, kc, :], rhs=w[:, j * 512:(j + 1) * 512],
                                                                      start=(kc == 0), stop=(kc == KC - 1)),
                             r=[scT.b, w.b], w=[pss[j].b])
                for r_ in range(2):
                    S.dma('sp', bia[r_:r_ + 1, :], self.ins['ada_b'][l:l + 1, n0:n0 + 2048], w=[bia.b])
                for j in range(4):
                    S.op('dve', lambda e, j=j: e.tensor_tensor(out=res[:, j * 512:(j + 1) * 512], in0=pss[j][0:2, :],
                                                               in1=bia[:, j * 512:(j + 1) * 512], op=ALU.add),
                         r=[pss[j].b, bia.b], w=[res.b])
                S.dma('sp', self.scr['mod'][l, :, n0:n0 + 2048], res[:], r=[res.b], w=[self.db('mod', l)])
        self.barrier()
        self._es.close()
        self.stage_begin()
        sc_ = self.sb('mod_sc', [2, D], F32)
        g = self.sb('mod_g', [2, D], F32)
        for l in range(dep):
            for k, (gn, sci, shi, gti) in enumerate((('norm_mix_g', 1, 0, 2), ('norm_ffn_g', 4, 3, 5))):
                S.dma('sp', sc_[:], self.scr['mod'][l, :, sci * D:(sci + 1) * D], r=[self.db('mod', l)], w=[sc_.b])
                for r_ in range(2):
                    S.dma('sp', g[r_:r_ + 1, :], self.ins[gn][l:l + 1, :], w=[g.b])
                S.op('dve', lambda e: e.scalar_tensor_tensor(out=sc_[:], in0=sc_[:], scalar=1.0, in1=g[:], op0=ALU.add, op1=ALU.mult),
                     r=[sc_.b, g.b], w=[sc_.b])
                S.dma('sp', self.scr['modv'][l, 3 * k], sc_[:], r=[sc_.b], w=[self.db('modv', l)])
                S.dma('sp', self.scr['modv'][l, 3 * k + 1], self.scr['mod'][l, :, shi * D:(shi + 1) * D], r=[self.db('mod', l)], w=[self.db('modv', l)])
                S.dma('sp', self.scr['modv'][l, 3 * k + 2], self.scr['mod'][l, :, gti * D:(gti + 1) * D], r=[self.db('mod', l)], w=[self.db('modv', l)])

    def load_bcast(self, tile, l, which, row):
        src = self.scr['modv'][l, which, row:row + 1, :].partition_broadcast(128)
        self.s.dma('sp', tile[:], src, r=[self.db('modv', l)], w=[tile.b])

    def stage_norm(self, l, which, tiles, final=False):
        c = self.cfg
        D, T = c['D'], c['T']
        KC = D // 128
        S = self.s
        tag = 'n%d%d%d' % (l, which, int(final))
        A1 = self.sb(tag + 'A', [128, D], F32)
        S1 = self.sb(tag + 'S', [128, D], F32)
        A = [A1, A1]
        Sh = [S1, S1]
        currow = -1
        if final:
            S.dma('sp', A[0][:], self.ins['final_norm_g'][None, :].partition_broadcast(128), w=[A[0].b])
        xt = [self.sb(tag + 'x%d' % k, [128, D], F32) for k in range(2)]
        junk = self.sb(tag + 'j', [128, D], BF16)
        hb = [self.sb(tag + 'h%d' % k, [128, D], BF16) for k in range(2)]
        hf = [self.sb(tag + 'f%d' % k, [128, D], F32) for k in range(2)] if final else None
        ht = [self.sb(tag + 't%d' % k, [128, KC, 512], BF16) for k in range(2)]
        ss = [self.sb(tag + 's%d' % k, [128, 1], F32) for k in range(2)]
        rs = [self.sb(tag + 'r%d' % k, [128, 1], F32) for k in range(2)]
        for n, tt in enumerate(tiles):
            row = 0 if tt * 128 < T else 1
            if not final and row != currow:
                currow = row
                self.load_bcast(A1, l, 3 * which, row)
                self.load_bcast(S1, l, 3 * which + 1, row)
            x, h, t_, s_, r_ = xt[n % 2], hb[n % 2], ht[(tt // 4) % 2], ss[n % 2], rs[n % 2]
            tj = tt % 4
            S.dma('sp', x[:], self.scr['xres'][tt * 128:(tt + 1) * 128, :], r=[self.db('xres', tt)], w=[x.b])
            S.op('act', lambda e, x=x, s_=s_: e.activation(out=junk[:], in_=x[:], func=AF.Square, accum_out=s_[:]),
                 r=[x.b], w=[junk.b, s_.b])
            S.op('act', lambda e, s_=s_, r_=r_: e.activation(out=r_[:], in_=s_[:], func=AF.Sqrt, scale=1.0 / D, bias=1e-6),
                 r=[s_.b], w=[r_.b])
            S.op('dve', lambda e, r_=r_: e.reciprocal(out=r_[:], in_=r_[:]), r=[r_.b], w=[r_.b])
            if final:
                f = hf[n % 2]
                S.op('dve', lambda e, x=x, r_=r_, f=f: e.scalar_tensor_tensor(out=f[:], in0=x[:], scalar=r_[:], in1=A[0][:],
                                                                              op0=ALU.mult, op1=ALU.mult), r=[x.b, r_.b, A[0].b], w=[f.b])
                S.dma('pool', self.out[tt * 128:(tt + 1) * 128, :], f[:], r=[f.b], w=[self.db('out', tt)])
                continue
            S.op('dve', lambda e, x=x, r_=r_, row=row: e.scalar_tensor_tensor(out=x[:], in0=x[:], scalar=r_[:], in1=A[row][:],
                                                                              op0=ALU.mult, op1=ALU.mult), r=[x.b, r_.b, A[row].b], w=[x.b])
            S.op('pool', lambda e, x=x, h=h, row=row: e.tensor_tensor(out=h[:], in0=x[:], in1=Sh[row][:], op=ALU.add),
                 r=[x.b, Sh[row].b], w=[h.b])
            for g in range(0, KC, 8):
                ps = self.nextpb()
                for k in range(8):
                    S.op('pe', lambda e, k=k, g=g, h=h, ps=ps: e.transpose(out=ps[:, k * 128:(k + 1) * 128], in_=h[:, (g + k) * 128:(g + k + 1) * 128],
                                                                          identity=self.identb[:]), r=[h.b, self.identb.b], w=[ps.b])
                for k in range(8):
                    S.op('act' if k % 2 else 'dve',
                         lambda e, g=g, k=k, ps=ps, t_=t_, tj=tj: (e.copy if e is self.nc.scalar else e.tensor_copy)(
                             out=t_[:, g + k, tj * 128:(tj + 1) * 128], in_=ps[:, k * 128:(k + 1) * 128]), r=[ps.b], w=[t_.b])
            if tj == 3 or n == len(tiles) - 1 or tiles[n + 1] // 4 != tt // 4:
                nn = (tj + 1) * 128
                S.dma('pool', self.scr['hT'][tt // 4, :, :, 0:nn], t_[:, :, 0:nn], r=[t_.b], w=[self.db('hT', tt // 4)])

    def tok_blocks(self, upto=None):
        c = self.cfg
        TA = c['TA'] if upto is None else upto
        return [(t0, min(512, TA - t0)) for t0 in range(0, TA, 512)]

    def stage_proj(self, l):
        c = self.cfg
        D, T = c['D'], c['T']
        KC = D // 128
        S = self.s
        even = (l % 2 == 0)
        L = self.EL if even else self.OL
        wbn = 'wb_in0' if even else 'wb_in1'
        wb = self.scr[wbn]
        fmn, tmn, ropen = (FM_EVEN, TM_EVEN, ROPE_EVEN) if even else (FM_ODD, TM_ODD, ROPE_ODD)
        self.fmidx, self.tmcol = {}, {}
        ci = 0
        for nm in fmn:
            for j in range(-(-L[nm][1] // 128)):
                self.fmidx[(nm, j)] = ci
                ci += 1
        co = 0
        for nm in tmn:
            self.tmcol[nm] = co
            co += L[nm][1]
        pieces = [(i,) + pc for i, pc in enumerate(self.pieces[l % 2])]
        hblk = [self.sb('pj_h%d' % k, [128, KC, 512], BF16) for k in range(2)]
        wt = [self.sb('pj_w%d' % k, [128, KC, 512], BF16) for k in range(2)]
        perm = self.sb('pj_perm', [128, 128], F32)
        cs = self.sb('pj_cos', [128, 512], F32)
        sn = self.sb('pj_sin', [128, 512], F32)
        q32 = [self.sb('pj_q%d' % k, [128, 512], F32) for k in range(2)]
        ta = [self.sb('pj_ta%d' % k, [128, 512], F32) for k in range(2)]
        tb_ = [self.sb('pj_tb%d' % k, [128, 512], F32) for k in range(2)]
        ob = [self.sb('pj_o%d' % k, [128, 512], BF16) for k in range(3)]
        S.dma('sp', perm[:], self.ins['k_perm'][:, :], w=[perm.b])
        wi = 0
        oi = 0
        qi = 0
        for bi, (t0, n) in enumerate(self.tok_blocks()):
            hb = hblk[bi % 2]
            latent = t0 < T
            S.dma('sp', hb[:, :, 0:n], self.scr['hT'][t0 // 512, :, :, 0:n], r=[self.db('hT', t0 // 512)], w=[hb.b])
            if latent:
                S.dma('sp', cs[:, 0:n], self.ins['k_cos'][:, t0:t0 + n], w=[cs.b])
                S.dma('sp', sn[:, 0:n], self.ins['k_sin'][:, t0:t0 + n], w=[sn.b])
            for pi, kind, nm, p0, pw in pieces:
                w = wt[wi % 2]
                wi += 1
                c0 = L[nm][0] + p0
                S.dma('sp', w[:, :, 0:pw], wb[pi, :, :, 0:pw], r=[self.db(wbn, 0)], w=[w.b])
                if kind == 'fm':
                    for j in range(-(-pw // 128)):
                        cw = min(128, pw - j * 128)
                        ps = self.nextpf()
                        for kc in range(KC):
                            S.op('pe', lambda e, kc=kc, j=j, cw=cw, w=w, ps=ps: e.matmul(
                                ps[0:cw, 0:n], lhsT=w[:, kc, j * 128:j * 128 + cw], rhs=hb[:, kc, 0:n], start=(kc == 0), stop=(kc == KC - 1)),
                                r=[w.b, hb.b], w=[ps.b])
                        o = ob[oi % 3]
                        oi += 1
                        if nm in ropen and latent:
                            q = q32[qi % 2]; a_ = ta[qi % 2]; b_ = tb_[qi % 2]
                            qi += 1
                            S.op('act', lambda e, q=q, ps=ps: e.copy(out=q[:, 0:n], in_=ps[:, 0:n]), r=[ps.b], w=[q.b])
                            ps2 = self.nextpf()
                            S.op('pe', lambda e, q=q, ps2=ps2: e.matmul(ps2[:, 0:n], lhsT=perm[:, :], rhs=q[:, 0:n], start=True, stop=True),
                                 r=[perm.b, q.b], w=[ps2.b])
                            S.op('pool', lambda e, q=q, a_=a_: e.tensor_tensor(out=a_[:, 0:n], in0=q[:, 0:n], in1=cs[:, 0:n], op=ALU.mult),
                                 r=[q.b, cs.b], w=[a_.b])
                            S.op('dve', lambda e, ps2=ps2, b_=b_: e.tensor_tensor(out=b_[:, 0:n], in0=ps2[:, 0:n], in1=sn[:, 0:n], op=ALU.mult),
                                 r=[ps2.b, sn.b], w=[b_.b])
                            S.op('dve', lambda e, a_=a_, b_=b_, o=o: e.tensor_tensor(out=o[:, 0:n], in0=a_[:, 0:n], in1=b_[:, 0:n], op=ALU.add),
                                 r=[a_.b, b_.b], w=[o.b])
                        else:
                            if oi % 2:
                                S.op('act', lambda e, o=o, ps=ps, cw=cw: e.copy(out=o[0:cw, 0:n], in_=ps[0:cw, 0:n]), r=[ps.b], w=[o.b])
                            else:
                                S.op('dve', lambda e, o=o, ps=ps, cw=cw: e.tensor_copy(out=o[0:cw, 0:n], in_=ps[0:cw, 0:n]), r=[ps.b], w=[o.b])
                        ch = self.fmidx[(nm, (p0 // 128) + j)]
                        S.dma('pool', self.scr['pT'][ch, 0:cw, t0:t0 + n], o[0:cw, 0:n], r=[o.b], w=[self.db('pT', ch)])
                else:
                    for j in range(n // 128):
                        ps = self.nextpf()
                        for kc in range(KC):
                            S.op('pe', lambda e, kc=kc, j=j, w=w, ps=ps: e.matmul(
                                ps[:, 0:pw], lhsT=hb[:, kc, j * 128:(j + 1) * 128], rhs=w[:, kc, 0:pw], start=(kc == 0), stop=(kc == KC - 1)),
                                r=[w.b, hb.b], w=[ps.b])
                        o = ob[oi % 3]
                        oi += 1
                        if oi % 2:
                            S.op('act', lambda e, o=o, ps=ps: e.copy(out=o[:, 0:pw], in_=ps[:, 0:pw]), r=[ps.b], w=[o.b])
                        else:
                            S.op('dve', lambda e, o=o, ps=ps: e.tensor_copy(out=o[:, 0:pw], in_=ps[:, 0:pw]), r=[ps.b], w=[o.b])
                        cc = self.tmcol[nm] + p0
                        S.dma('pool', self.scr['pV'][t0 + j * 128:t0 + (j + 1) * 128, cc:cc + pw], o[:, 0:pw], r=[o.b],
                              w=[self.db('pV', (t0 // 128) + j)])

    def attn_work(self, tag, nkmax, dv, nbuf=1):
        nblk = -(-nkmax // 128)
        W = []
        for k in range(nbuf):
            W.append(dict(
                S=self.sb(tag + 'S%d' % k, [128, nkmax + 1], F32), P=self.sb(tag + 'P%d' % k, [128, nkmax + 1], BF16),
                PT=self.sb(tag + 'PT%d' % k, [128, nblk, 128], BF16), m=self.sb(tag + 'm%d' % k, [128, 1], F32),
                ss=self.sb(tag + 'ss%d' % k, [128, 1], F32), O=self.sb(tag + 'O%d' % k, [128, dv], F32)))
        return W

    def attn_unit(self, W, qT, qb, parts, scale, dv, sink=None):
        S = self.s
        Ssb, P, PT, m, ss, O = W['S'], W['P'], W['PT'], W['m'], W['ss'], W['O']
        off = 0
        ei = 0
        for (kT, kbuf, bias, bbuf, vlist) in parts:
            n_all = kT.shape[1]
            for c0 in range(0, n_all, 512):
                n = min(512, n_all - c0)
                ps = self.nextpf()
                S.op('pe', lambda e, ps=ps, c0=c0, n=n, kT=kT: e.matmul(ps[:, 0:n], lhsT=qT, rhs=kT[:, c0:c0 + n], start=True, stop=True),
                     r=[qb, kbuf], w=[ps.b])
                if bias is not None:
                    S.op('dve', lambda e, ps=ps, c0=c0, n=n, off=off, bias=bias: e.scalar_tensor_tensor(
                        out=Ssb[:, off:off + n], in0=ps[:, 0:n], scalar=scale, in1=bias[:, c0:c0 + n], op0=ALU.mult, op1=ALU.add),
                        r=[ps.b, bbuf], w=[Ssb.b])
                elif ei % 2 == 0:
                    S.op('act', lambda e, ps=ps, n=n, off=off: e.mul(out=Ssb[:, off:off + n], in_=ps[:, 0:n], mul=scale), r=[ps.b], w=[Ssb.b])
                else:
                    S.op('dve', lambda e, ps=ps, n=n, off=off: e.tensor_scalar(out=Ssb[:, off:off + n], in0=ps[:, 0:n], scalar1=scale,
                                                                             scalar2=None, op0=ALU.mult), r=[ps.b], w=[Ssb.b])
                ei += 1
                off += n
        nk = off
        ncol = nk
        if sink is not None:
            S.op('dve', lambda e: e.tensor_copy(out=Ssb[:, nk:nk + 1], in_=sink[0][:, 0:1]), r=[sink[1]], w=[Ssb.b])
            ncol = nk + 1
        S.op('dve', lambda e: e.reduce_max(out=m[:], in_=Ssb[:, 0:ncol], axis=AX.X), r=[Ssb.b], w=[m.b])
        S.op('dve', lambda e: e.tensor_scalar(out=m[:], in0=m[:], scalar1=-1.0, scalar2=None, op0=ALU.mult), r=[m.b], w=[m.b])
        S.op('act', lambda e: e.activation(out=P[:, 0:ncol], in_=Ssb[:, 0:ncol], func=AF.Exp, bias=m[:], scale=1.0, accum_out=ss[:]),
             r=[Ssb.b, m.b], w=[P.b, ss.b])
        S.op('dve', lambda e: e.reciprocal(out=ss[:], in_=ss[:]), r=[ss.b], w=[ss.b])
        vall = []
        off = 0
        for (kT, kbuf, bias, bbuf, vlist) in parts:
            for (vap, kn, vbuf) in vlist:
                vall.append((vap, kn, vbuf, off))
                off += kn
        for g in range(0, len(vall), 8):
            grp = vall[g:g + 8]
            pb = self.nextpb()
            for k, (vap, kn, vbuf, o_) in enumerate(grp):
                S.op('pe', lambda e, k=k, kn=kn, o_=o_, pb=pb: e.transpose(out=pb[0:kn, k * 128:(k + 1) * 128], in_=P[:, o_:o_ + kn],
                                                                          identity=self.identb[:]), r=[P.b, self.identb.b], w=[pb.b])
            eng = 'act' if (g // 8) % 2 else 'dve'
            S.op(eng, lambda e, g=g, pb=pb, ng=len(grp): (e.copy if e is self.nc.scalar else e.tensor_copy)(
                out=PT[:, g:g + ng, :].rearrange("p a b -> p (a b)"), in_=pb[:, 0:ng * 128]), r=[pb.b], w=[PT.b])
        po = self.nextpf()
        for k, (vap, kn, vbuf, o_) in enumerate(vall):
            S.op('pe', lambda e, k=k, kn=kn, vap=vap: e.matmul(po[:, 0:dv], lhsT=PT[0:kn, k, :], rhs=vap, start=(k == 0), stop=(k == len(vall) - 1)),
                 r=[PT.b, vbuf], w=[po.b])
        S.op('act', lambda e: e.activation(out=O[:, 0:dv], in_=po[:, 0:dv], func=AF.Identity, scale=ss[:]), r=[po.b, ss.b], w=[O.b])

    def attn_work2(self, tag, nkmax, dv, nbuf=2):
        nblk = -(-nkmax // 128)
        nch = -(-nkmax // 512)
        W = []
        for k in range(nbuf):
            W.append(dict(
                P=self.sb(tag + 'P%d' % k, [128, nkmax], BF16), PT=self.sb(tag + 'PT%d' % k, [128, nblk, 128], BF16),
                mc=self.sb(tag + 'mc%d' % k, [128, nch], F32), sc=self.sb(tag + 'sc%d' % k, [128, nch], F32),
                m=self.sb(tag + 'm%d' % k, [128, 1], F32), ss=self.sb(tag + 'ss%d' % k, [128, 1], F32),
                O=self.sb(tag + 'O%d' % k, [128, dv], F32)))
        return W

    def attn_unit2(self, W, qT, qb, kT, kbuf, vlist, scale, dv):
        S = self.s
        P, PT, mc, sc, m, ss, O = W['P'], W['PT'], W['mc'], W['sc'], W['m'], W['ss'], W['O']
        nk = kT.shape[1]
        chunks = [(c0, min(512, nk - c0)) for c0 in range(0, nk, 512)]
        nch = len(chunks)
        for ci, (c0, n) in enumerate(chunks):
            ps = self.nextpf()
            S.op('pe', lambda e, ps=ps, c0=c0, n=n: e.matmul(ps[:, 0:n], lhsT=qT, rhs=kT[:, c0:c0 + n], start=True, stop=True),
                 r=[qb, kbuf], w=[ps.b])
            S.op('dve', lambda e, ps=ps, n=n, ci=ci: e.reduce_max(out=mc[:, ci:ci + 1], in_=ps[:, 0:n], axis=AX.X), r=[ps.b], w=[mc.b])
        S.op('dve', lambda e: e.reduce_max(out=m[:], in_=mc[:, 0:nch], axis=AX.X), r=[mc.b], w=[m.b])
        S.op('dve', lambda e: e.tensor_scalar(out=m[:], in0=m[:], scalar1=-scale, scalar2=None, op0=ALU.mult), r=[m.b], w=[m.b])
        for ci, (c0, n) in enumerate(chunks):
            ps = self.nextpf()
            S.op('pe', lambda e, ps=ps, c0=c0, n=n: e.matmul(ps[:, 0:n], lhsT=qT, rhs=kT[:, c0:c0 + n], start=True, stop=True),
                 r=[qb, kbuf], w=[ps.b])
            S.op('act', lambda e, ps=ps, c0=c0, n=n, ci=ci: e.activation(out=P[:, c0:c0 + n], in_=ps[:, 0:n], func=AF.Exp, bias=m[:], scale=scale,
                                                                        accum_out=sc[:, ci:ci + 1]), r=[ps.b, m.b], w=[P.b, sc.b])
        S.op('dve', lambda e: e.reduce_sum(out=ss[:], in_=sc[:, 0:nch], axis=AX.X), r=[sc.b], w=[ss.b])
        S.op('dve', lambda e: e.reciprocal(out=ss[:], in_=ss[:]), r=[ss.b], w=[ss.b])
        off = 0
        vall = []
        for (vap, kn, vbuf) in vlist:
            vall.append((vap, kn, vbuf, off))
            off += kn
        for g in range(0, len(vall), 8):
            grp = vall[g:g + 8]
            pb = self.nextpb()
            for k, (vap, kn, vbuf, o_) in enumerate(grp):
                S.op('pe', lambda e, k=k, kn=kn, o_=o_, pb=pb: e.transpose(out=pb[0:kn, k * 128:(k + 1) * 128], in_=P[:, o_:o_ + kn],
                                                                          identity=self.identb[:]), r=[P.b, self.identb.b], w=[pb.b])
            eng = 'act' if (g // 8) % 3 == 2 else 'dve'
            S.op(eng, lambda e, g=g, pb=pb, ng=len(grp): (e.copy if e is self.nc.scalar else e.tensor_copy)(
                out=PT[:, g:g + ng, :].rearrange("p a b -> p (a b)"), in_=pb[:, 0:ng * 128]), r=[pb.b], w=[PT.b])
        po = self.nextpf()
        for k, (vap, kn, vbuf, o_) in enumerate(vall):
            S.op('pe', lambda e, k=k, kn=kn, vap=vap: e.matmul(po[:, 0:dv], lhsT=PT[0:kn, k, :], rhs=vap, start=(k == 0), stop=(k == len(vall) - 1)),
                 r=[PT.b, vbuf], w=[po.b])
        S.op('act', lambda e: e.activation(out=O[:, 0:dv], in_=po[:, 0:dv], func=AF.Identity, scale=ss[:]), r=[po.b, ss.b], w=[O.b])

    def attn_pair2(self, Ws, qTs, qbs, kTs, kbufs, vlist, scale, dv):
        S = self.s
        nk = kTs[0].shape[1]
        chunks = [(c0, min(512, nk - c0)) for c0 in range(0, nk, 512)]
        nch = len(chunks)
        U = range(len(Ws))
        for u in U:
            W = Ws[u]
            for ci, (c0, n) in enumerate(chunks):
                ps = self.nextpf()
                S.op('pe', lambda e, ps=ps, c0=c0, n=n, u=u: e.matmul(ps[:, 0:n], lhsT=qTs[u], rhs=kTs[u][:, c0:c0 + n], start=True, stop=True),
                     r=[qbs[u], kbufs[u]], w=[ps.b])
                S.op('dve', lambda e, ps=ps, n=n, ci=ci, W=W: e.reduce_max(out=W['mc'][:, ci:ci + 1], in_=ps[:, 0:n], axis=AX.X), r=[ps.b], w=[W['mc'].b])
        for u in U:
            W = Ws[u]
            S.op('dve', lambda e, W=W: e.reduce_max(out=W['m'][:], in_=W['mc'][:, 0:nch], axis=AX.X), r=[W['mc'].b], w=[W['m'].b])
            S.op('dve', lambda e, W=W: e.tensor_scalar(out=W['m'][:], in0=W['m'][:], scalar1=-scale, scalar2=None, op0=ALU.mult), r=[W['m'].b], w=[W['m'].b])
        for u in U:
            W = Ws[u]
            for ci, (c0, n) in enumerate(chunks):
                ps = self.nextpf()
                S.op('pe', lambda e, ps=ps, c0=c0, n=n, u=u: e.matmul(ps[:, 0:n], lhsT=qTs[u], rhs=kTs[u][:, c0:c0 + n], start=True, stop=True),
                     r=[qbs[u], kbufs[u]], w=[ps.b])
                S.op('act', lambda e, ps=ps, c0=c0, n=n, ci=ci, W=W: e.activation(out=W['P'][:, c0:c0 + n], in_=ps[:, 0:n], func=AF.Exp, bias=W['m'][:], scale=scale,
                                                                                 accum_out=W['sc'][:, ci:ci + 1]), r=[ps.b, W['m'].b], w=[W['P'].b, W['sc'].b])
        for u in U:
            W = Ws[u]
            S.op('dve', lambda e, W=W: e.reduce_sum(out=W['ss'][:], in_=W['sc'][:, 0:nch], axis=AX.X), r=[W['sc'].b], w=[W['ss'].b])
            S.op('dve', lambda e, W=W: e.reciprocal(out=W['ss'][:], in_=W['ss'][:]), r=[W['ss'].b], w=[W['ss'].b])
        vall = []
        off = 0
        for (vap, kn, vbuf) in vlist:
            vall.append((vap, kn, vbuf, off))
            off += kn
        for g in range(0, len(vall), 8):
            grp = vall[g:g + 8]
            for u in U:
                W = Ws[u]
                pb = self.nextpb()
                for k, (vap, kn, vbuf, o_) in enumerate(grp):
                    S.op('pe', lambda e, k=k, kn=kn, o_=o_, pb=pb, W=W: e.transpose(out=pb[0:kn, k * 128:(k + 1) * 128], in_=W['P'][:, o_:o_ + kn],
                                                                                   identity=self.identb[:]), r=[W['P'].b, self.identb.b], w=[pb.b])
                eng = 'act' if u % 2 else 'dve'
                S.op(eng, lambda e, g=g, pb=pb, ng=len(grp), W=W: (e.copy if e is self.nc.scalar else e.tensor_copy)(
                    out=W['PT'][:, g:g + ng, :].rearrange("p a b -> p (a b)"), in_=pb[:, 0:ng * 128]), r=[pb.b], w=[W['PT'].b])
        for u in U:
            W = Ws[u]
            po = self.nextpf()
            for k, (vap, kn, vbuf, o_) in enumerate(vall):
                S.op('pe', lambda e, k=k, kn=kn, vap=vap, W=W, po=po: e.matmul(po[:, 0:dv], lhsT=W['PT'][0:kn, k, :], rhs=vap, start=(k == 0), stop=(k == len(vall) - 1)),
                     r=[W['PT'].b, vbuf], w=[po.b])
            S.op('act', lambda e, W=W, po=po: e.activation(out=W['O'][:, 0:dv], in_=po[:, 0:dv], func=AF.Identity, scale=W['ss'][:]), r=[po.b, W['ss'].b], w=[W['O'].b])

    def attn2_light(self, W, qT, qb, kT, kbuf, scale):
        S = self.s
        nk = kT.shape[1]
        chunks = [(c0, min(512, nk - c0)) for c0 in range(0, nk, 512)]
        nch = len(chunks)
        L = []

        def p1(ci, c0, n):
            ps = self.nextpf()
            S.op('pe', lambda e: e.matmul(ps[:, 0:n], lhsT=qT, rhs=kT[:, c0:c0 + n], start=True, stop=True), r=[qb, kbuf], w=[ps.b])
            S.op('dve', lambda e: e.reduce_max(out=W['mc'][:, ci:ci + 1], in_=ps[:, 0:n], axis=AX.X), r=[ps.b], w=[W['mc'].b])

        def pm():
            S.op('dve', lambda e: e.reduce_max(out=W['m'][:], in_=W['mc'][:, 0:nch], axis=AX.X), r=[W['mc'].b], w=[W['m'].b])
            S.op('dve', lambda e: e.tensor_scalar(out=W['m'][:], in0=W['m'][:], scalar1=-scale, scalar2=None, op0=ALU.mult), r=[W['m'].b], w=[W['m'].b])

        def p2(ci, c0, n):
            ps = self.nextpf()
            S.op('pe', lambda e: e.matmul(ps[:, 0:n], lhsT=qT, rhs=kT[:, c0:c0 + n], start=True, stop=True), r=[qb, kbuf], w=[ps.b])
            S.op('act', lambda e: e.activation(out=W['P'][:, c0:c0 + n], in_=ps[:, 0:n], func=AF.Exp, bias=W['m'][:], scale=scale,
                                               accum_out=W['sc'][:, ci:ci + 1]), r=[ps.b, W['m'].b], w=[W['P'].b, W['sc'].b])

        def psum_():
            S.op('dve', lambda e: e.reduce_sum(out=W['ss'][:], in_=W['sc'][:, 0:nch], axis=AX.X), r=[W['sc'].b], w=[W['ss'].b])
            S.op('dve', lambda e: e.reciprocal(out=W['ss'][:], in_=W['ss'][:]), r=[W['ss'].b], w=[W['ss'].b])

        for ci, (c0, n) in enumerate(chunks):
            L.append(lambda ci=ci, c0=c0, n=n: p1(ci, c0, n))
        L.append(pm)
        for ci, (c0, n) in enumerate(chunks):
            L.append(lambda ci=ci, c0=c0, n=n: p2(ci, c0, n))
        L.append(psum_)
        return L

    def attn2_heavy(self, W, vlist, dv, cpeng):
        S = self.s
        vall = []
        off = 0
        for (vap, kn, vbuf) in vlist:
            vall.append((vap, kn, vbuf, off))
            off += kn
        H = []

        def tg(g):
            grp = vall[g:g + 8]
            pb = self.nextpb()
            for k, (vap, kn, vbuf, o_) in enumerate(grp):
                S.op('pe', lambda e, k=k, kn=kn, o_=o_: e.transpose(out=pb[0:kn, k * 128:(k + 1) * 128], in_=W['P'][:, o_:o_ + kn],
                                                                   identity=self.identb[:]), r=[W['P'].b, self.identb.b], w=[pb.b])
            S.op(cpeng, lambda e: (e.copy if e is self.nc.scalar else e.tensor_copy)(
                out=W['PT'][:, g:g + len(grp), :].rearrange("p a b -> p (a b)"), in_=pb[:, 0:len(grp) * 128]), r=[pb.b], w=[W['PT'].b])

        st = {}

        def pv(k0, k1):
            if k0 == 0:
                self.poi = getattr(self, 'poi', 0) + 1
                st['po'] = self.pf[4 + self.poi % 2]
            po = st['po']
            for k in range(k0, k1):
                vap, kn, vbuf, o_ = vall[k]
                S.op('pe', lambda e, k=k, kn=kn, vap=vap: e.matmul(po[:, 0:dv], lhsT=W['PT'][0:kn, k, :], rhs=vap, start=(k == 0), stop=(k == len(vall) - 1)),
                     r=[W['PT'].b, vbuf], w=[po.b])
            if k1 == len(vall):
                S.op('act', lambda e: e.activation(out=W['O'][:, 0:dv], in_=po[:, 0:dv], func=AF.Identity, scale=W['ss'][:]), r=[po.b, W['ss'].b], w=[W['O'].b])

        for g in range(0, len(vall), 8):
            H.append(lambda g=g: tg(g))
        for k0 in range(0, len(vall), 3):
            H.append(lambda k0=k0: pv(k0, min(len(vall), k0 + 3)))
        return H

    def zip_emit(self, L, H):
        per = -(-len(H) // max(1, len(L)))
        hi = 0
        for l in L:
            l()
            for _ in range(per):
                if hi < len(H):
                    H[hi]()
                    hi += 1
        while hi < len(H):
            H[hi]()
            hi += 1

    def load_fm(self, tile, nm, j, q='sp'):
        ch = self.fmidx[(nm, j)]
        self.s.dma(q, tile[:], self.scr['pT'][ch, :, :], r=[self.db('pT', ch)], w=[tile.b])

    def load_tm(self, tile, nm, c0, w, q='sp'):
        cc = self.tmcol[nm] + c0
        ntl = self.cfg['TA'] // 128
        self.s.dma(q, tile[:], self.scr['pV'][:, cc:cc + w].rearrange("(n p) c -> p n c", p=128),
                   r=self.dbs('pV', 0, ntl), w=[tile.b])

    def store_y(self, O, ob, tt, col, w):
        S = self.s
        S.op('dve', lambda e: e.tensor_copy(out=ob[:, 0:w], in_=O[:, 0:w]), r=[O.b], w=[ob.b])
        S.dma('pool', self.scr['y_tm'][tt * 128:(tt + 1) * 128, col:col + w], ob[:, 0:w], r=[ob.b], w=[self.db('y_tm', tt)])

    def stage_swa(self, l):
        c = self.cfg
        T, C, TA, D = c['T'], c['C'], c['TA'], c['D']
        S = self.s
        ntl, nlat = TA // 128, T // 128
        scale = 128.0 ** -0.5
        kT = [self.sb('sw_k%d' % k, [128, TA], BF16) for k in range(2)]
        V = [self.sb('sw_v%d' % k, [128, ntl, 128], BF16) for k in range(2)]
        qT = [self.sb('sw_q%d' % k, [128, TA], BF16) for k in range(2)]
        sk = [self.sb('sw_s%d' % k, [128, 1], F32) for k in range(2)]
        mask = self.sb('sw_mask', [128, 384], F32)
        ob = [self.sb('sw_ob%d' % k, [128, 128], BF16) for k in range(2)]
        S.dma('sp', mask[:], self.ins['k_swamask'][:, :], w=[mask.b])
        W = self.attn_work('sw', C + 384, 128, nbuf=2)
        u = 0
        for g in range(c['SKV']):
            k_, v_ = kT[g % 2], V[g % 2]
            self.load_fm(k_, 'sk', g)
            self.load_tm(v_, 'sv', g * 128, 128)
            for hh in range(4):
                h = g * 4 + hh
                q_, s_ = qT[h % 2], sk[h % 2]
                self.load_fm(q_, 'sq', h)
                S.dma('sp', s_[:], self.ins['swa_sink'][0:1, h:h + 1].partition_broadcast(128), w=[s_.b])
                for tt in range(ntl):
                    ctxpart = (k_[:, T:TA], k_.b, None, None, [(v_[:, nlat + j, :], 128, v_.b) for j in range(C // 128)])
                    if tt < nlat:
                        lo, hi = max(tt - 1, 0), min(tt + 2, nlat)
                        m0 = (lo - (tt - 1)) * 128
                        parts = [ctxpart, (k_[:, lo * 128:hi * 128], k_.b, mask[:, m0:m0 + (hi - lo) * 128], mask.b,
                                           [(v_[:, j, :], 128, v_.b) for j in range(lo, hi)])]
                    else:
                        parts = [ctxpart]
                    w_ = W[u % 2]
                    self.attn_unit(w_, q_[:, tt * 128:(tt + 1) * 128], q_.b, parts, scale, 128, sink=(s_, s_.b))
                    self.store_y(w_['O'], ob[u % 2], tt, D // 2 + h * 128, 128)
                    u += 1

    def stage_gla(self, l):
        c = self.cfg
        T, C, TA, D, GH = c['T'], c['C'], c['TA'], c['D'], c['GH']
        S = self.s
        I = self.ins
        qT = self.sb('gl_q', [128, TA], BF16)
        kT = self.sb('gl_k', [128, TA], BF16)
        lg = [self.sb('gl_l%d' % d, [128, TA], F32) for d in range(2)]
        dn1 = self.sb('gl_dn', [16, TA], BF16)
        dn = [dn1, dn1]
        upf = self.sb('gl_upf', [16, 128], F32)
        upb = self.sb('gl_upb', [16, 128], BF16)
        nb = self.sb('gl_nb', [128, 1], F32)
        et = self.sb('gl_et', [128, 512], F32)
        tri = self.sb('gl_tri', [64, 2, 64], F32)
        ones = self.sb('gl_ones', [128, 64], F32)
        ng = self.sb('gl_ng', [64, 256], F32)
        St = self.sb('gl_S', [128, 256], F32)
        Sb = self.sb('gl_Sb', [128, 256], BF16)
        vg = [self.sb('gl_v%d' % k, [64, 8, 256], BF16) for k in range(2)]
        rg = [self.sb('gl_r%d' % k, [64, 8, 256], BF16) for k in range(2)]
        og = [self.sb('gl_o%d' % k, [64, 8, 256], F32) for k in range(2)]
        yg = [self.sb('gl_y%d' % k, [64, 8, 256], BF16) for k in range(2)]
        cc = [self.sb('gl_c%d' % k, [128, 64], F32) for k in range(2)]
        c2 = [self.sb('gl_c2%d' % k, [128, 64], F32) for k in range(2)]
        ncl = [self.sb('gl_ncl%d' % k, [128, 1], F32) for k in range(2)]
        ebl = [self.sb('gl_ebl%d' % k, [128, 1], F32) for k in range(2)]
        eb = [self.sb('gl_eb%d' % k, [128, 64], F32) for k in range(2)]
        ei = [self.sb('gl_ei%d' % k, [128, 64], F32) for k in range(2)]
        eu = [self.sb('gl_eu%d' % k, [128, 64], F32) for k in range(2)]
        qd = [self.sb('gl_qd%d' % k, [128, 64], BF16) for k in range(2)]
        ki = [self.sb('gl_ki%d' % k, [128, 64], BF16) for k in range(2)]
        ku = [self.sb('gl_ku%d' % k, [128, 64], BF16) for k in range(2)]
        kut = [self.sb('gl_kut%d' % k, [64, 128], BF16) for k in range(2)]
        at = [self.sb('gl_at%d' % k, [64, 64], BF16) for k in range(2)]
        ot = [self.sb('gl_ot%d' % k, [64, 256], F32) for k in range(2)]
        sr = [None, None]
        srg = [self.sb('gl_srg%d' % k, [64, 8, 256], F32) for k in range(2)]
        epst = self.sb('gl_eps', [128, 1], F32)
        S.op('pool', lambda e: e.memset(epst[:], 1e-6), w=[epst.b])
        jk = self.sb('gl_jk', [64, 256], F32)
        ss = [self.sb('gl_ss%d' % k, [64, 1], F32) for k in range(2)]
        S.dma('sp', tri[:], I['k_tri'].rearrange("d j i -> j d i"), w=[tri.b])
        S.op('pool', lambda e: e.memset(ones[:], 1.0), w=[ones.b])
        S.dma('sp', ng[:], I['gla_norm_g'][0:1, :].partition_broadcast(64), w=[ng.b])
        gcv = self.tmcol['gv']
        gcr = self.tmcol['gr']
        groups = [(T + g0, min(8, (C - g0) // 64)) for g0 in range(0, C, 512)] + [(g0, 8) for g0 in range(0, T, 512)]
        n = 0
        for h in range(GH):
            self.load_fm(qT, 'gq', h)
            self.load_fm(kT, 'gk', h)
            for d, (un, bn) in enumerate((('gla_gate_up_f', 'gla_gate_bias_f'), ('gla_gate_up_b', 'gla_gate_bias_b'))):
                ch = self.fmidx[(('dnf', 'dnb')[d], 0)]
                S.dma('sp', dn[d][:], self.scr['pT'][ch, 0:16, :], r=[self.db('pT', ch)], w=[dn[d].b])
                S.dma('sp', upf[:], I[un][0, :, h * 128:(h + 1) * 128], w=[upf.b])
                S.op('dve', lambda e: e.tensor_copy(out=upb[:], in_=upf[:]), r=[upf.b], w=[upb.b])
                S.dma('sp', nb[:], I[bn][0, h * 128:(h + 1) * 128].rearrange("(p o) -> p o", o=1), w=[nb.b])
                S.op('dve', lambda e: e.tensor_scalar(out=nb[:], in0=nb[:], scalar1=-1.0, scalar2=None, op0=ALU.mult), r=[nb.b], w=[nb.b])
                for (t0, nt) in self.tok_blocks():
                    ps = self.nextpf()
                    S.op('pe', lambda e, ps=ps, d=d, t0=t0, nt=nt: e.matmul(ps[:, 0:nt], lhsT=upb[:, :], rhs=dn[d][:, t0:t0 + nt], start=True, stop=True),
                         r=[upb.b, dn[d].b], w=[ps.b])
                    S.op('act', lambda e, ps=ps, nt=nt: e.activation(out=et[:, 0:nt], in_=ps[:, 0:nt], func=AF.Exp, bias=nb[:], scale=-1.0),
                         r=[ps.b, nb.b], w=[et.b])
                    S.op('act', lambda e, d=d, t0=t0, nt=nt: e.activation(out=lg[d][:, t0:t0 + nt], in_=et[:, 0:nt], func=AF.Ln, bias=1.0, scale=1.0),
                         r=[et.b], w=[lg[d].b])
            for d in range(2):
                S.op('pool', lambda e: e.memset(St[:], 0.0), w=[St.b])
                S.op('pool', lambda e: e.memset(Sb[:], 0.0), w=[Sb.b])
                glist = groups if d == 0 else [groups[i] for i in list(range(len(groups) - 1, -1, -1))]
                if d == 1:
                    nctx = -(-C // 512)
                    glist = groups[:nctx][::-1] + groups[nctx:][::-1]
                for gi, (g0, gn) in enumerate(glist):
                    v_, r_, o_, y_ = vg[gi % 2], rg[gi % 2], og[gi % 2], yg[gi % 2]
                    rows = slice(g0, g0 + gn * 64)
                    tl = list(range(g0 // 128, -(-(g0 + gn * 64) // 128)))
                    S.dma('sp', v_[:, 0:gn, :], self.scr['pV'][rows, gcv + h * 256:gcv + (h + 1) * 256].rearrange("(n p) c -> p n c", p=64),
                          r=[self.db('pV', t) for t in tl], w=[v_.b])
                    if d == 1:
                        S.dma('sp', r_[:, 0:gn, :], self.scr['pV'][rows, gcr + h * 256:gcr + (h + 1) * 256].rearrange("(n p) c -> p n c", p=64),
                              r=[self.db('pV', t) for t in tl], w=[r_.b])
                        S.dma('sp', o_[:, 0:gn, :], self.scr['gla_o'][rows, h * 256:(h + 1) * 256].rearrange("(n p) c -> p n c", p=64),
                              r=[self.db('gla_o', t) for t in tl], w=[o_.b])
                        srg_ = srg[gi % 2]
                        S.op('act', lambda e, srg_=srg_, r_=r_, gn=gn: e.activation(out=srg_[:, 0:gn, :], in_=r_[:, 0:gn, :], func=AF.Silu), r=[r_.b], w=[srg_.b])
                    korder = range(gn) if d == 0 else range(gn - 1, -1, -1)
                    for k in korder:
                        t0 = g0 + k * 64
                        i2 = n % 2
                        n += 1
                        c_, c2_, ncl_, ebl_, eb_, ei_, eu_ = cc[i2], c2[i2], ncl[i2], ebl[i2], eb[i2], ei[i2], eu[i2]
                        qd_, ki_, ku_, kut_, at_, ot_, sr_, ss_ = qd[i2], ki[i2], ku[i2], kut[i2], at[i2], ot[i2], sr[i2], ss[i2]
                        lch = lg[d][:, t0:t0 + 64]
                        S.op('dve', lambda e, c_=c_, lch=lch: e.tensor_tensor_scan(out=c_[:], data0=ones[:], data1=lch, initial=0.0,
                                                                                  op0=ALU.mult, op1=ALU.add), r=[ones.b, lg[d].b], w=[c_.b])
                        S.op('dve', lambda e, c_=c_, ncl_=ncl_: e.tensor_scalar(out=ncl_[:], in0=c_[:, 63:64], scalar1=-1.0 / 16, scalar2=None, op0=ALU.mult),
                             r=[c_.b], w=[ncl_.b])
                        if d == 0:
                            cu = c_
                        else:
                            S.op('dve', lambda e, c_=c_, c2_=c2_, lch=lch: e.scalar_tensor_tensor(out=c2_[:], in0=c_[:], scalar=-1.0, in1=lch,
                                                                                                   op0=ALU.mult, op1=ALU.add), r=[c_.b, lg[d].b], w=[c2_.b])
                            S.op('dve', lambda e, c_=c_, c2_=c2_: e.tensor_scalar(out=c2_[:], in0=c2_[:], scalar1=c_[:, 63:64], scalar2=None, op0=ALU.add),
                                 r=[c_.b, c2_.b], w=[c2_.b])
                            cu = c2_
                        S.op('act', lambda e, cu=cu, eb_=eb_: e.activation(out=eb_[:], in_=cu[:], func=AF.Exp, scale=-1.0 / 16), r=[cu.b], w=[eb_.b])
                        S.op('act', lambda e, cu=cu, ei_=ei_: e.activation(out=ei_[:], in_=cu[:], func=AF.Exp, scale=1.0 / 16), r=[cu.b], w=[ei_.b])
                        S.op('act', lambda e, cu=cu, eu_=eu_, ncl_=ncl_: e.activation(out=eu_[:], in_=cu[:], func=AF.Exp, scale=1.0 / 16, bias=ncl_[:]),
                             r=[cu.b, ncl_.b], w=[eu_.b])
                        S.op('act', lambda e, ebl_=ebl_, ncl_=ncl_: e.activation(out=ebl_[:], in_=ncl_[:], func=AF.Exp), r=[ncl_.b], w=[ebl_.b])
                        S.op('dve', lambda e, qd_=qd_, eb_=eb_, t0=t0: e.scalar_tensor_tensor(out=qd_[:], in0=qT[:, t0:t0 + 64], scalar=128.0 ** -0.5, in1=eb_[:],
                                                                                              op0=ALU.mult, op1=ALU.mult), r=[qT.b, eb_.b], w=[qd_.b])
                        S.op('pool', lambda e, ki_=ki_, ei_=ei_, t0=t0: e.tensor_tensor(out=ki_[:], in0=kT[:, t0:t0 + 64], in1=ei_[:], op=ALU.mult),
                             r=[kT.b, ei_.b], w=[ki_.b])
                        S.op('pool', lambda e, ku_=ku_, eu_=eu_, t0=t0: e.tensor_tensor(out=ku_[:], in0=kT[:, t0:t0 + 64], in1=eu_[:], op=ALU.mult),
                             r=[kT.b, eu_.b], w=[ku_.b])
                        pa = self.nextpf()
                        S.op('pe', lambda e, pa=pa, ki_=ki_, qd_=qd_: e.matmul(pa[0:64, 0:64], lhsT=ki_[:, :], rhs=qd_[:, :], start=True, stop=True),
                             r=[ki_.b, qd_.b], w=[pa.b])
                        S.op('dve', lambda e, pa=pa, at_=at_, d=d: e.tensor_tensor(out=at_[:], in0=pa[0:64, 0:64], in1=tri[:, d, :], op=ALU.mult),
                             r=[pa.b, tri.b], w=[at_.b])
                        po = self.nextpf()
                        S.op('pe', lambda e, po=po, at_=at_, v_=v_, k=k: e.matmul(po[0:64, 0:256], lhsT=at_[:, :], rhs=v_[:, k, :], start=True, stop=False),
                             r=[at_.b, v_.b], w=[po.b])
                        S.op('pe', lambda e, po=po, qd_=qd_: e.matmul(po[0:64, 0:256], lhsT=qd_[:, :], rhs=Sb[:, :], start=False, stop=True),
                             r=[qd_.b, Sb.b], w=[po.b])
                        if d == 0:
                            S.op('act', lambda e, po=po, o_=o_, k=k: e.copy(out=o_[:, k, :], in_=po[0:64, 0:256]), r=[po.b], w=[o_.b])
                        else:
                            S.op('dve', lambda e, po=po, o_=o_, ot_=ot_, k=k: e.tensor_tensor(out=ot_[:], in0=po[0:64, 0:256], in1=o_[:, k, :], op=ALU.add),
                                 r=[po.b, o_.b], w=[ot_.b])
                            S.op('pool', lambda e, ot_=ot_: e.tensor_tensor(out=jk[:], in0=ot_[:], in1=ot_[:], op=ALU.mult), r=[ot_.b], w=[jk.b])
                            S.op('dve', lambda e, ss_=ss_: e.reduce_sum(out=ss_[:], in_=jk[:], axis=AX.X), r=[jk.b], w=[ss_.b])
                            S.op('act', lambda e, ss_=ss_: e.activation(out=ss_[:], in_=ss_[:], func=AF.Ln, scale=1.0 / 256, bias=epst[0:64, :]), r=[ss_.b, epst.b], w=[ss_.b])
                            S.op('act', lambda e, ss_=ss_: e.activation(out=ss_[:], in_=ss_[:], func=AF.Exp, scale=-0.5), r=[ss_.b], w=[ss_.b])
                            S.op('dve', lambda e, ot_=ot_, ss_=ss_: e.scalar_tensor_tensor(out=ot_[:], in0=ot_[:], scalar=ss_[:], in1=ng[:], op0=ALU.mult, op1=ALU.mult),
                                 r=[ot_.b, ss_.b, ng.b], w=[ot_.b])
                            S.op('pool', lambda e, ot_=ot_, srg_=srg_, y_=y_, k=k: e.tensor_tensor(out=y_[:, k, :], in0=ot_[:], in1=srg_[:, k, :], op=ALU.mult),
                                 r=[ot_.b, srg_.b], w=[y_.b])
                        pt = self.nextpb()
                        S.op('pe', lambda e, pt=pt, ku_=ku_: e.transpose(out=pt[0:64, 0:128], in_=ku_[:, :], identity=self.identb[:]),
                             r=[ku_.b, self.identb.b], w=[pt.b])
                        S.op('act', lambda e, pt=pt, kut_=kut_: e.copy(out=kut_[:], in_=pt[0:64, 0:128]), r=[pt.b], w=[kut_.b])
                        pd = self.nextpf()
                        S.op('pe', lambda e, pd=pd, kut_=kut_, v_=v_, k=k: e.matmul(pd[:, 0:256], lhsT=kut_[:, :], rhs=v_[:, k, :], start=True, stop=True),
                             r=[kut_.b, v_.b], w=[pd.b])
                        S.op('dve', lambda e, pd=pd, ebl_=ebl_: e.scalar_tensor_tensor(out=St[:], in0=St[:], scalar=ebl_[:], in1=pd[:, 0:256], op0=ALU.mult, op1=ALU.add),
                             r=[St.b, ebl_.b, pd.b], w=[St.b])
                        S.op('act', lambda e: e.copy(out=Sb[:], in_=St[:]), r=[St.b], w=[Sb.b])
                    if d == 0:
                        S.dma('pool', self.scr['gla_o'][rows, h * 256:(h + 1) * 256].rearrange("(n p) c -> p n c", p=64), o_[:, 0:gn, :],
                              r=[o_.b], w=[self.db('gla_o', t) for t in tl])
                    else:
                        S.dma('pool', self.scr['y_tm'][rows, h * 256:(h + 1) * 256].rearrange("(n p) c -> p n c", p=64), y_[:, 0:gn, :],
                              r=[y_.b], w=[self.db('y_tm', t) for t in tl])

    def stage_mix_even(self, l):
        self.stage_gla(l)
        self.barrier()
        self._es.close()
        self.stage_begin()
        self.stage_swa(l)

    def stage_na(self, l):
        c = self.cfg
        T, C, TA, D, NH = c['T'], c['C'], c['TA'], c['D'], c['NH']
        S = self.s
        ntl, nlat = TA // 128, T // 128
        scale = 128.0 ** -0.5
        kT = [self.sb('na_k%d' % k, [128, TA], BF16) for k in range(2)]
        V = [self.sb('na_v%d' % k, [128, ntl, 128], BF16) for k in range(2)]
        qT = [self.sb('na_q%d' % k, [128, TA], BF16) for k in range(2)]
        bt = [self.sb('na_b%d' % k, [128, 5, 640], F32) for k in range(2)]
        ob = [self.sb('na_ob%d' % k, [128, 128], BF16) for k in range(2)]
        W = self.attn_work('na', C + 640, 128, nbuf=2)
        u = 0
        for h in range(NH):
            k_, v_, q_, b_ = kT[h % 2], V[h % 2], qT[h % 2], bt[h % 2]
            self.load_fm(k_, 'nk', h)
            self.load_tm(v_, 'nv', h * 128, 128)
            self.load_fm(q_, 'nq', h)
            S.dma('sp', b_[:], self.ins['na_bias'][:, h, :, :].rearrange("v q k -> q v k"), w=[b_.b])
            for tt in range(nlat):
                vi, lo = na_variant(c, 2 * tt)
                k0 = lo * 64
                parts = [(k_[:, T:TA], k_.b, None, None, [(v_[:, nlat + j, :], 128, v_.b) for j in range(C // 128)]),
                         (k_[:, k0:k0 + 640], k_.b, b_[:, vi, :], b_.b, [(v_[:, k0 // 128 + j, :], 128, v_.b) for j in range(5)])]
                w_ = W[u % 2]
                self.attn_unit(w_, q_[:, tt * 128:(tt + 1) * 128], q_.b, parts, scale, 128)
                self.store_y(w_['O'], ob[u % 2], tt, h * 128, 128)
                u += 1

    def stage_diff(self, l):
        c = self.cfg
        T, C, TA, D, DH = c['T'], c['C'], c['TA'], c['D'], c['DH']
        S = self.s
        I = self.ins
        ntl, nlat = TA // 128, T // 128
        scale = 128.0 ** -0.5
        lam_init = 0.8 - 0.6 * math.exp(-0.3 * l)
        kT = [self.sb('df_k%d' % k, [128, TA], BF16) for k in range(2)]
        qT = [self.sb('df_q%d' % k, [128, TA], BF16) for k in range(2)]
        V = self.sb('df_v', [128, ntl, 256], BF16)
        ng = self.sb('df_ng', [128, 256], F32)
        lv = [self.sb('df_l%d' % k, [128, 128], F32) for k in range(4)]
        lj = self.sb('df_lj', [128, 128], F32)
        la = [self.sb('df_la%d' % k, [128, 1], F32) for k in range(2)]
        nlam = self.sb('df_nlam', [128, 1], F32)
        od = self.sb('df_od', [128, 256], F32)
        jk = self.sb('df_jk', [128, 256], F32)
        ss = self.sb('df_ss', [128, 1], F32)
        ob = [self.sb('df_ob%d' % k, [128, 256], BF16) for k in range(2)]
        epst = self.sb('df_eps', [128, 1], F32)
        S.op('pool', lambda e: e.memset(epst[:], 1e-6), w=[epst.b])
        W = self.attn_work2('df', TA, 256, nbuf=2)
        S.dma('sp', ng[:], I['diff_norm_g'][0:1, :].partition_broadcast(128), w=[ng.b])
        for k, nm in enumerate(('diff_lq1', 'diff_lk1', 'diff_lq2', 'diff_lk2')):
            S.dma('sp', lv[k][:], I[nm][0:1, :].partition_broadcast(128), w=[lv[k].b])
        for k in range(2):
            S.op('dve', lambda e, k=k: e.tensor_tensor(out=lj[:], in0=lv[2 * k][:], in1=lv[2 * k + 1][:], op=ALU.mult),
                 r=[lv[2 * k].b, lv[2 * k + 1].b], w=[lj.b])
            S.op('dve', lambda e, k=k: e.reduce_sum(out=la[k][:], in_=lj[:], axis=AX.X), r=[lj.b], w=[la[k].b])
            S.op('act', lambda e, k=k: e.activation(out=la[k][:], in_=la[k][:], func=AF.Exp), r=[la[k].b], w=[la[k].b])
        S.op('dve', lambda e: e.tensor_tensor(out=nlam[:], in0=la[1][:], in1=la[0][:], op=ALU.subtract), r=[la[0].b, la[1].b], w=[nlam.b])
        S.op('dve', lambda e: e.tensor_scalar(out=nlam[:], in0=nlam[:], scalar1=-lam_init, scalar2=None, op0=ALU.add), r=[nlam.b], w=[nlam.b])
        u = 0
        for h in range(DH):
            self.load_tm(V, 'dv', h * 256, 256)
            for s_ in range(2):
                self.load_fm(kT[s_], 'dk', 2 * h + s_)
                self.load_fm(qT[s_], 'dq', 2 * h + s_)
            vlist = [(V[:, j, :], 128, V.b) for j in range(ntl)]

            def combine(tt):
                nonlocal u
                S.op('dve', lambda e: e.scalar_tensor_tensor(out=od[:], in0=W[1]['O'][:], scalar=nlam[:], in1=W[0]['O'][:], op0=ALU.mult, op1=ALU.add),
                     r=[W[0]['O'].b, W[1]['O'].b, nlam.b], w=[od.b])
                S.op('pool', lambda e: e.tensor_tensor(out=jk[:], in0=od[:], in1=od[:], op=ALU.mult), r=[od.b], w=[jk.b])
                S.op('dve', lambda e: e.reduce_sum(out=ss[:], in_=jk[:], axis=AX.X), r=[jk.b], w=[ss.b])
                S.op('act', lambda e: e.activation(out=ss[:], in_=ss[:], func=AF.Ln, scale=1.0 / 256, bias=epst[:]), r=[ss.b, epst.b], w=[ss.b])
                S.op('act', lambda e: e.activation(out=ss[:], in_=ss[:], func=AF.Exp, scale=-0.5), r=[ss.b], w=[ss.b])
                S.op('dve', lambda e: e.scalar_tensor_tensor(out=od[:], in0=od[:], scalar=ss[:], in1=ng[:], op0=ALU.mult, op1=ALU.mult),
                     r=[od.b, ss.b, ng.b], w=[od.b])
                o_ = ob[u % 2]
                u += 1
                S.op('act', lambda e, o_=o_: e.mul(out=o_[:], in_=od[:], mul=1.0 - lam_init), r=[od.b], w=[o_.b])
                S.dma('pool', self.scr['y_tm'][tt * 128:(tt + 1) * 128, D // 2 + h * 256:D // 2 + (h + 1) * 256], o_[:], r=[o_.b], w=[self.db('y_tm', tt)])

            prev = []
            self.pf_lim = 4
            for tt in range(nlat):
                for s_ in range(2):
                    Lt = self.attn2_light(W[s_], qT[s_][:, tt * 128:(tt + 1) * 128], qT[s_].b, kT[s_][:, 0:TA], kT[s_].b, scale)
                    self.zip_emit(Lt, prev)
                    prev = self.attn2_heavy(W[s_], vlist, 256, 'act' if s_ else 'dve')
                    if s_ == 1:
                        prev.append(lambda tt=tt: combine(tt))
            self.zip_emit([], prev)
            self.pf_lim = 6

    def stage_mix_odd(self, l):
        self.stage_na(l)
        self.barrier()
        self._es.close()
        self.stage_begin()
        self.stage_diff(l)

    def resid_update(self, ps, n0, nw, tt, Gt, xp, tmp):
        S = self.s
        xr = self.scr['xres'][tt * 128:(tt + 1) * 128, n0:n0 + nw]
        S.dma('sp', xp[:, 0:nw], xr, r=[self.db('xres', tt)], w=[xp.b])
        S.op('dve', lambda e: e.tensor_tensor(out=tmp[:, 0:nw], in0=ps[:, 0:nw], in1=Gt[:, 0:nw], op=ALU.mult), r=[ps.b, Gt.b], w=[tmp.b])
        S.op('pool', lambda e: e.tensor_tensor(out=xp[:, 0:nw], in0=xp[:, 0:nw], in1=tmp[:, 0:nw], op=ALU.add), r=[xp.b, tmp.b], w=[xp.b])
        S.dma('pool', xr, xp[:, 0:nw], r=[xp.b], w=[self.db('xres', tt)])

    def blocks_of(self, tiles):
        nlat = self.cfg['T'] // 128
        out, cur = [], []
        for tt in tiles:
            if cur and (len(cur) == 4 or tt != cur[-1] + 1 or (tt == nlat)):
                out.append(cur)
                cur = []
            cur.append(tt)
        if cur:
            out.append(cur)
        return out

    def stage_wout(self, l, tiles):
        c = self.cfg
        D, T = c['D'], c['T']
        KC = D // 128
        S = self.s
        wbn = 'wb_out%d' % l
        wb = self.scr[wbn]
        yt = [self.sb('wo_y%d' % k, [128, D], BF16) for k in range(2)]
        yT = self.sb('wo_yT', [128, KC, 512], BF16)
        wt = [self.sb('wo_w%d' % k, [128, KC, 512], BF16) for k in range(2)]
        Gt = [self.sb('wo_G%d' % k, [128, 512], F32) for k in range(2)]
        xp = [self.sb('wo_x%d' % k, [128, 512], F32) for k in range(4)]
        tmp = [self.sb('wo_t%d' % k, [128, 512], F32) for k in range(4)]
        yi = wi = xi = 0
        for blk in self.blocks_of(tiles):
            row = 0 if blk[0] * 128 < T else 1
            for j, tt in enumerate(blk):
                y_ = yt[yi % 2]
                yi += 1
                S.dma('sp', y_[:], self.scr['y_tm'][tt * 128:(tt + 1) * 128, :], r=[self.db('y_tm', tt)], w=[y_.b])
                for g in range(0, KC, 8):
                    pb = self.nextpb()
                    for k in range(8):
                        S.op('pe', lambda e, k=k, g=g, y_=y_, pb=pb: e.transpose(out=pb[:, k * 128:(k + 1) * 128], in_=y_[:, (g + k) * 128:(g + k + 1) * 128],
                                                                              identity=self.identb[:]), r=[y_.b, self.identb.b], w=[pb.b])
                    for k in range(8):
                        eng = 'act' if k % 2 else 'dve'
                        S.op(eng, lambda e, k=k, g=g, j=j, pb=pb: (e.copy if e is self.nc.scalar else e.tensor_copy)(
                            out=yT[:, g + k, j * 128:(j + 1) * 128], in_=pb[:, k * 128:(k + 1) * 128]), r=[pb.b], w=[yT.b])
            for n0 in range(0, D, 512):
                w = wt[wi % 2]
                G_ = Gt[wi % 2]
                wi += 1
                S.dma('sp', w[:], wb[n0 // 512], r=[self.db(wbn, 0)], w=[w.b])
                S.dma('sp', G_[:], self.scr['modv'][l, 2, row:row + 1, n0:n0 + 512].partition_broadcast(128), r=[self.db('modv', l)], w=[G_.b])
                for j, tt in enumerate(blk):
                    ps = self.nextpf()
                    for kc in range(KC):
                        S.op('pe', lambda e, kc=kc, j=j, w=w, ps=ps: e.matmul(ps[:, :], lhsT=yT[:, kc, j * 128:(j + 1) * 128], rhs=w[:, kc, :],
                                                                             start=(kc == 0), stop=(kc == KC - 1)), r=[yT.b, w.b], w=[ps.b])
                    self.resid_update(ps, n0, 512, tt, G_, xp[xi % 4], tmp[xi % 4])
                    xi += 1

    def stage_ffn(self, l, tiles):
        c = self.cfg
        D, T, F = c['D'], c['T'], c['F']
        KC, FC = D // 128, F // 128
        S = self.s
        splits = self.ffn_split()
        nsplit = len(splits)
        FS = max(hi - lo for lo, hi in splits)
        gug = self.ffn_gugroups()
        dg = self.ffn_fgroups()
        hb = self.sb('ff_h', [128, KC, 512], BF16)
        aT = self.sb('ff_a', [128, FS, 512], BF16)
        wg = [self.sb('ff_g%d' % k, [128, KC, 256], BF16) for k in range(2)]
        wu = [self.sb('ff_u%d' % k, [128, KC, 256], BF16) for k in range(2)]
        wd = [self.sb('ff_d%d' % k, [128, 8, 512], BF16) for k in range(2)]
        sg = [self.sb('ff_s%d' % k, [128, 512], F32) for k in range(2)]
        Gt = [self.sb('ff_G%d' % k, [128, 512], F32) for k in range(2)]
        xp = [self.sb('ff_x%d' % k, [128, 512], F32) for k in range(4)]
        tmp = [self.sb('ff_t%d' % k, [128, 512], F32) for k in range(4)]
        wgs, wus, wds = self.scr['wb_g%d' % l], self.scr['wb_u%d' % l], self.scr['wb_d%d' % l]
        gi = di = si = xi = Gi = 0
        for blk in self.blocks_of(tiles):
            row = 0 if blk[0] * 128 < T else 1
            t0, n = blk[0] * 128, len(blk) * 128
            S.dma('sp', hb[:, :, 0:n], self.scr['hT'][blk[0] // 4, :, :, 0:n], r=[self.db('hT', blk[0] // 4)], w=[hb.b])
            for sp_ in range(nsplit):
                f_lo, f_hi = splits[sp_]
                for gidx, (gsp, fg, nf) in enumerate(gug):
                    if gsp != sp_:
                        continue
                    g_, u_ = wg[gi % 2], wu[gi % 2]
                    gi += 1
                    S.dma('sp', g_[:, :, 0:nf * 128], wgs[gidx, :, :, 0:nf * 128], r=[self.db('wb_g%d' % l, 0)], w=[g_.b])
                    S.dma('sp', u_[:, :, 0:nf * 128], wus[gidx, :, :, 0:nf * 128], r=[self.db('wb_u%d' % l, 0)], w=[u_.b])
                    for j in range(nf):
                        pg, pu = self.nextpf(), self.nextpf()
                        for kc in range(KC):
                            S.op('pe', lambda e, kc=kc, j=j, g_=g_, pg=pg: e.matmul(pg[:, 0:n], lhsT=g_[:, kc, j * 128:(j + 1) * 128], rhs=hb[:, kc, 0:n],
                                                                                   start=(kc == 0), stop=(kc == KC - 1)), r=[g_.b, hb.b], w=[pg.b])
                        for kc in range(KC):
                            S.op('pe', lambda e, kc=kc, j=j, u_=u_, pu=pu: e.matmul(pu[:, 0:n], lhsT=u_[:, kc, j * 128:(j + 1) * 128], rhs=hb[:, kc, 0:n],
                                                                                   start=(kc == 0), stop=(kc == KC - 1)), r=[u_.b, hb.b], w=[pu.b])
                        s_ = sg[si % 2]
                        si += 1
                        S.op('act', lambda e, s_=s_, pg=pg: e.activation(out=s_[:, 0:n], in_=pg[:, 0:n], func=AF.Silu), r=[pg.b], w=[s_.b])
                        S.op('dve', lambda e, s_=s_, pu=pu, fg=fg, j=j, f_lo=f_lo: e.tensor_tensor(out=aT[:, fg + j - f_lo, 0:n], in0=s_[:, 0:n], in1=pu[:, 0:n], op=ALU.mult),
                             r=[s_.b, pu.b], w=[aT.b])
                for n0 in range(0, D, 512):
                    G_ = Gt[Gi % 2]
                    Gi += 1
                    S.dma('sp', G_[:], self.scr['modv'][l, 5, row:row + 1, n0:n0 + 512].partition_broadcast(128), r=[self.db('modv', l)], w=[G_.b])
                    pss = [self.nextpf() for _ in blk]
                    for didx, (dsp, fg, nf) in enumerate(dg):
                        if dsp != sp_:
                            continue
                        d_ = wd[di % 2]
                        di += 1
                        S.dma('sp', d_[:, 0:nf, :], wds[didx, n0 // 512, :, 0:nf, :], r=[self.db('wb_d%d' % l, 0)], w=[d_.b])
                        for j in range(len(blk)):
                            for k in range(nf):
                                fc = fg + k
                                S.op('pe', lambda e, j=j, k=k, fc=fc, d_=d_: e.matmul(pss[j][:, :], lhsT=aT[:, fc - f_lo, j * 128:(j + 1) * 128], rhs=d_[:, k, :],
                                                                                     start=(fc == f_lo), stop=(fc == f_hi - 1)), r=[aT.b, d_.b], w=[pss[j].b])
                    for j, tt in enumerate(blk):
                        self.resid_update(pss[j], n0, 512, tt, G_, xp[xi % 4], tmp[xi % 4])
                        xi += 1

    def run_stage(self, fn, *a, **k):
        self.stage_begin()
        fn(*a, **k)
        self.stage_end()

    def build(self, upto='all'):
        c = self.cfg
        T, TA = c['T'], c['TA']
        self.declare()
        self.stage_begin()
        self.stage_cast()
        self.stage_init_x()
        self._es.close()
        self.run_stage(self.stage_mod)
        alltiles = list(range(TA // 128))
        lat = list(range(T // 128))
        if upto == 'mod':
            return self.finish()
        self.run_stage(self.stage_norm, 0, 0, alltiles)
        if upto == 'norm':
            return self.finish()
        for l in range(c['depth']):
            last = (l == c['depth'] - 1)
            self.run_stage(self.stage_proj, l)
            if upto == 'proj%d' % l:
                return self.finish()
            self.run_stage(self.stage_mix_even if l % 2 == 0 else self.stage_mix_odd, l)
            if upto == 'mix%d' % l:
                return self.finish()
            tiles = lat if last else alltiles
            self.run_stage(self.stage_wout, l, tiles)
            self.run_stage(self.stage_norm, l, 1, tiles)
            self.run_stage(self.stage_ffn, l, tiles)
            if upto == 'ffn%d' % l:
                return self.finish()
            if not last:
                self.run_stage(self.stage_norm, l + 1, 0, alltiles)
        self.run_stage(self.stage_norm, 0, 0, lat, final=True)
        return self.finish()

    def finish(self):
        bufs = [b for (nm, i), b in self.dbufs.items() if nm == 'out' or nm in self.debug]
        self.barrier()
        return self.nc


def host_consts(cfg):
    T = cfg['T']
    k = {}
    k['k_ident'] = np.eye(128, dtype=np.float32)
    pm = np.zeros((128, 128), np.float32)
    for m in range(128):
        h, i = divmod(m, 64)
        src = h * 64 + (i + 32) % 64
        pm[src, m] = 1.0
    k['k_perm'] = pm
    inv = (1.0 / (np.float32(10000.0) ** (np.arange(32, dtype=np.float32) / np.float32(32)))).astype(np.float32)
    pos = np.arange(T)
    row = (pos // 64).astype(np.float32)
    col = (pos % 64).astype(np.float32)
    cosT = np.zeros((128, T), np.float32)
    sinT = np.zeros((128, T), np.float32)
    for f in range(128):
        h, i = divmod(f, 64)
        j = i % 32
        ang = ((row if h == 0 else col) * inv[j]).astype(np.float32)
        cosT[f] = np.cos(ang).astype(np.float32)
        sg = -1.0 if i < 32 else 1.0
        sinT[f] = sg * np.sin(ang).astype(np.float32)
    k['k_cos'] = cosT
    k['k_sin'] = sinT
    qi = np.arange(128)[:, None]
    kj = np.arange(384)[None, :]
    k['k_swamask'] = np.where(np.abs(qi + 128 - kj) <= 128, 0.0, NEG).astype(np.float32)
    tri = np.zeros((2, 64, 64), np.float32)
    jj = np.arange(64)[:, None]
    ii = np.arange(64)[None, :]
    tri[0] = (ii >= jj)
    tri[1] = (ii <= jj)
    k['k_tri'] = tri
    return k


def na_bias_tables(cfg, rpb):
    T = cfg['T']
    rows = T // 64
    NH = rpb.shape[0]
    out = np.full((5, NH, 128, 640), NEG, np.float32)
    variants = [4, 0, 2, rows - 4, rows - 2]
    for vi, r0 in enumerate(variants):
        lo = min(max(r0 - 4, 0), rows - 10)
        for dq in range(2):
            r = r0 + dq
            rs = min(max(r - 4, 0), rows - 8)
            for cq in range(64):
                cst = min(max(cq - 8, 0), 64 - 16)
                q = dq * 64 + cq
                for kr in range(10):
                    ar = lo + kr
                    if not (rs <= ar < rs + 8):
                        continue
                    dr = ar - r + 7
                    kc = np.arange(cst, cst + 16)
                    dc = kc - cq + 15
                    out[vi, :, q, kr * 64 + kc] = rpb[:, dr, dc].T
    return out


def na_variant(cfg, r0):
    rows = cfg['T'] // 64
    if 4 <= r0 <= rows - 6:
        return 0, r0 - 4
    if r0 == 0:
        return 1, 0
    if r0 == 2:
        return 2, 0
    if r0 == rows - 4:
        return 3, rows - 10
    return 4, rows - 10


def make_in_maps(cfg, inputs, ncores):
    ks = host_consts(cfg)
    maps = []
    nab = na_bias_tables(cfg, np.asarray(inputs['na_rpb'])[0])
    for b in range(ncores):
        m = {}
        m['x'] = np.ascontiguousarray(inputs['x'][b])
        m['ctx'] = np.ascontiguousarray(inputs['ctx'][b])
        m['cvec'] = np.ascontiguousarray(np.stack([inputs['c'][b], inputs['c_ctx']]))
        for nm in ('ada_w', 'ada_b', 'norm_mix_g', 'norm_ffn_g', 'w_out', 'ffn_w_gate', 'ffn_w_up', 'ffn_w_down', 'ev_w_in',
                   'gla_gate_up_f', 'gla_gate_bias_f', 'gla_gate_up_b', 'gla_gate_bias_b', 'gla_norm_g', 'swa_sink', 'od_w_in',
                   'diff_lq1', 'diff_lk1', 'diff_lq2', 'diff_lk2', 'diff_norm_g', 'final_norm_g'):
            m[nm] = np.asarray(inputs[nm])
        m['na_bias'] = nab
        m.update(ks)
        maps.append(m)
    return maps


def kernel(**inputs):
    inputs = {k: np.asarray(v) for k, v in inputs.items()}
    B, T, D = inputs['x'].shape
    cfg = make_cfg(D=D, T=T, C=inputs['ctx'].shape[1], depth=inputs['ada_w'].shape[0])
    p = Prog(cfg)
    nc = p.build()
    maps = make_in_maps(cfg, inputs, B)
    res = run_bass_kernel_spmd(nc, maps, core_ids=list(range(B)))
    return np.stack([res.results[b]['out'] for b in range(B)]).astype(np.float32)
```
